# Optimizing a Trainium2 kernel written in Bass

```python
import jax, jax.numpy as jnp
from jax import lax
import numpy as np

D_MODEL = 1024
BATCH = 4
SEQ = 4096
DEPTH = 4
DEC_BATCH = 32
DEC_SEQ = 8
PAST_LEN = 8192
PAGE_SIZE = 128

N_A_LAYERS = DEPTH // 2
N_B_LAYERS = DEPTH - N_A_LAYERS
POOL_WINDOWS = (2, 4, 8, 16)
N_POOL_GROUPS = len(POOL_WINDOWS)
POOL_GROUP = D_MODEL // N_POOL_GROUPS
POOL_HIST = max(POOL_WINDOWS) - 1
WINDOWS = (128, 512, 2048)
DILATIONS = (1, 4, 16)
N_BRANCH = len(WINDOWS)
HEAD_DIM = 64
N_HEADS = D_MODEL // HEAD_DIM
ATTN_WIDTH = N_HEADS * HEAD_DIM
D_FF = -(-8 * D_MODEL // (3 * 256)) * 256
ROPE_THETA = 10000.0
EPS = 1e-6

kernel_name = "yoco_pool_dilated_swa_step"

F32 = jnp.float32


def _rmsnorm(x, g):
    xf = x.astype(F32)
    y = xf * lax.rsqrt(jnp.mean(xf * xf, axis=-1, keepdims=True) + EPS)
    return (y * g.astype(F32)).astype(x.dtype)


def _rope_tables(pos):
    inv = jnp.power(ROPE_THETA, -jnp.arange(0, HEAD_DIM, 2, dtype=F32) / HEAD_DIM)
    ang = pos.astype(F32)[:, None] * inv[None, :]
    ang = jnp.concatenate([ang, ang], axis=-1)
    return jnp.cos(ang), jnp.sin(ang)


def _rope(x, cos, sin):
    xf = x.astype(F32)
    x1, x2 = jnp.split(xf, 2, axis=-1)
    rot = jnp.concatenate([-x2, x1], axis=-1)
    c = cos[None, :, None, None, :]
    s = sin[None, :, None, None, :]
    return (xf * c + rot * s).astype(x.dtype)


def _swiglu(h, w_gate, w_up, w_down):
    return (jax.nn.silu(h @ w_gate) * (h @ w_up)) @ w_down


def _pool_mixer(u_ext, offset, w_pool, scale):
    L = u_ext.shape[1]
    idx = jnp.arange(offset, L)
    c = jnp.pad(jnp.cumsum(u_ext.astype(F32), axis=1), ((0, 0), (1, 0), (0, 0)))
    hi = c[:, offset + 1:]
    u = u_ext[:, offset:].astype(F32)
    outs = []
    for g, win in enumerate(POOL_WINDOWS):
        sl = slice(g * POOL_GROUP, (g + 1) * POOL_GROUP)
        lo = jnp.take(c[:, :, sl], jnp.maximum(idx + 1 - win, 0), axis=1)
        cnt = jnp.minimum(idx + 1, win).astype(F32)[None, :, None]
        d = (hi[:, :, sl] - lo) / cnt - u[:, :, sl]
        outs.append(d.astype(u_ext.dtype) @ w_pool[g])
    return jnp.concatenate(outs, axis=-1) * scale


def _shared_kv(x, kv_norm, w_kv, k_norm, cos, sin):
    B, S, _ = x.shape
    kv = (_rmsnorm(x, kv_norm) @ w_kv).reshape(B, S, 2, N_BRANCH, N_HEADS, HEAD_DIM)
    k = _rope(_rmsnorm(kv[:, :, 0], k_norm[:, None, :]), cos, sin)
    return k, kv[:, :, 1]


def _queries(x, b_norm, w_q, q_norm, cos, sin):
    B, S, _ = x.shape
    q = (_rmsnorm(x, b_norm) @ w_q).reshape(B, S, N_BRANCH, N_HEADS, HEAD_DIM)
    return _rope(_rmsnorm(q, q_norm[:, None, :]), cos, sin)


def _dilated_prompt(q, k, v, window, dil):
    B, S, H, D = q.shape
    blk = window // dil
    L = S // dil
    nb = -(-L // blk)
    Lp = nb * blk

    def to_blocks(a):
        a = a.reshape(B, L, dil, H, D).transpose(0, 2, 1, 3, 4)
        a = jnp.pad(a, ((0, 0), (0, 0), (0, Lp - L), (0, 0), (0, 0)))
        return a.reshape(B, dil, nb, blk, H, D)

    def with_prev(a):
        prev = jnp.pad(a, ((0, 0), (0, 0), (1, 0), (0, 0), (0, 0), (0, 0)))[:, :, :-1]
        return jnp.concatenate([prev, a], axis=3)

    qb = to_blocks(q)
    kc = with_prev(to_blocks(k))
    vc = with_prev(to_blocks(v))
    s = jnp.einsum('brnqhd,brnkhd->brnhqk', qb, kc, preferred_element_type=F32) * (HEAD_DIM ** -0.5)
    qi = jnp.arange(blk)[:, None]
    ki = jnp.arange(2 * blk)[None, :]
    rel = qi + blk - ki
    band = (rel >= 0) & (rel <= blk)
    kidx = jnp.arange(nb)[:, None] * blk - blk + jnp.arange(2 * blk)[None, :]
    mask = band[None, None, :, :] & (kidx >= 0)[:, None, None, :]
    s = jnp.where(mask, s, -jnp.inf)
    lse = jax.nn.logsumexp(s, axis=-1)
    p = jnp.exp(s - lse[..., None])
    o = jnp.einsum('brnhqk,brnkhd->brnqhd', p.astype(v.dtype), vc)
    o = o.reshape(B, dil, Lp, H, D)[:, :, :L].transpose(0, 2, 1, 3, 4).reshape(B, S, H, D)
    lse = lse.transpose(0, 1, 2, 4, 3).reshape(B, dil, Lp, H)[:, :, :L]
    lse = lse.transpose(0, 2, 1, 3).reshape(B, S, H)
    return o, lse


def _dilated_sample(q, kc, vc, hist, window, dil):
    T = q.shape[1]
    steps = window // dil
    idx = hist + jnp.arange(T)[:, None] - dil * jnp.arange(steps + 1)[None, :]
    valid = idx >= 0
    idx = jnp.maximum(idx, 0)
    kg = kc[:, idx]
    vg = vc[:, idx]
    s = jnp.einsum('nthd,ntkhd->nthk', q, kg, preferred_element_type=F32) * (HEAD_DIM ** -0.5)
    s = jnp.where(valid[None, :, None, :], s, -jnp.inf)
    lse = jax.nn.logsumexp(s, axis=-1)
    p = jnp.exp(s - lse[..., None])
    o = jnp.einsum('nthk,ntkhd->nthd', p.astype(vc.dtype), vg)
    return o, lse


def _merge(outs, lses, w_o):
    wts = jax.nn.softmax(jnp.stack(lses, axis=0), axis=0)
    o = jnp.sum(wts[..., None] * jnp.stack(outs, axis=0).astype(F32), axis=0)
    B, S = o.shape[:2]
    return o.reshape(B, S, ATTN_WIDTH).astype(outs[0].dtype) @ w_o


def setup_inputs(seed: int = 0) -> dict:
    key = jax.random.key(seed)
    ks = jax.random.split(key, 24)

    def nrm(k, shape, scale=1.0):
        return jax.random.normal(k, shape, F32) * scale

    def gain(k, shape):
        return 1.0 + 0.02 * jax.random.normal(k, shape, F32)

    hist = [min(w, PAST_LEN) for w in WINDOWS]
    return {
        "x_prompt": nrm(ks[0], (BATCH, SEQ, D_MODEL)),
        "x_sample": nrm(ks[1], (DEC_BATCH, DEC_SEQ, D_MODEL)),
        "state_pool": nrm(ks[2], (DEC_BATCH, N_A_LAYERS, POOL_HIST, D_MODEL)),
        "cache_kv_w128": nrm(ks[3], (DEC_BATCH, hist[0], 2, N_HEADS, HEAD_DIM)),
        "cache_kv_w512": nrm(ks[4], (DEC_BATCH, hist[1], 2, N_HEADS, HEAD_DIM)),
        "cache_kv_w2048": nrm(ks[5], (DEC_BATCH, hist[2], 2, N_HEADS, HEAD_DIM)),
        "a_norm": gain(ks[6], (N_A_LAYERS, D_MODEL)),
        "pool_w": nrm(ks[7], (N_A_LAYERS, N_POOL_GROUPS, POOL_GROUP, POOL_GROUP), POOL_GROUP ** -0.5),
        "pool_scale": gain(ks[8], (N_A_LAYERS, D_MODEL)),
        "kv_norm": gain(ks[9], (D_MODEL,)),
        "w_kv": nrm(ks[10], (D_MODEL, 2 * N_BRANCH * ATTN_WIDTH), D_MODEL ** -0.5),
        "k_norm": gain(ks[11], (N_BRANCH, HEAD_DIM)),
        "b_norm": gain(ks[12], (N_B_LAYERS, D_MODEL)),
        "w_q": nrm(ks[13], (N_B_LAYERS, D_MODEL, N_BRANCH * ATTN_WIDTH), D_MODEL ** -0.5),
        "q_norm": gain(ks[14], (N_B_LAYERS, N_BRANCH, HEAD_DIM)),
        "w_o": nrm(ks[15], (N_B_LAYERS, ATTN_WIDTH, D_MODEL), ATTN_WIDTH ** -0.5),
        "ffn_norm": gain(ks[16], (DEPTH, D_MODEL)),
        "w_gate": nrm(ks[17], (DEPTH, D_MODEL, D_FF), D_MODEL ** -0.5),
        "w_up": nrm(ks[18], (DEPTH, D_MODEL, D_FF), D_MODEL ** -0.5),
        "w_down": nrm(ks[19], (DEPTH, D_FF, D_MODEL), D_FF ** -0.5),
    }


def reference(x_prompt, x_sample, state_pool, cache_kv_w128, cache_kv_w512, cache_kv_w2048,
              a_norm, pool_w, pool_scale, kv_norm, w_kv, k_norm, b_norm, w_q, q_norm, w_o,
              ffn_norm, w_gate, w_up, w_down):
    S_p = x_prompt.shape[1]
    T = x_sample.shape[1]
    cos_p, sin_p = _rope_tables(jnp.arange(S_p))
    cos_s, sin_s = _rope_tables(PAST_LEN + jnp.arange(T))
    caches = (cache_kv_w128, cache_kv_w512, cache_kv_w2048)

    xp, xs = x_prompt, x_sample
    pool_p, pool_s = [], []
    kp = vp = None
    kv_cat = []
    kv_new_p = []
    for layer in range(DEPTH):
        if layer < N_A_LAYERS:
            i = layer
            up = _rmsnorm(xp, a_norm[i])
            us = _rmsnorm(xs, a_norm[i])
            us_ext = jnp.concatenate([state_pool[:, i].astype(us.dtype), us], axis=1)
            xp = xp + _pool_mixer(up, 0, pool_w[i], pool_scale[i])
            xs = xs + _pool_mixer(us_ext, POOL_HIST, pool_w[i], pool_scale[i])
            pool_p.append(up[:, -POOL_HIST:])
            pool_s.append(us_ext[:, -POOL_HIST:])
        else:
            j = layer - N_A_LAYERS
            if j == 0:
                kp, vp = _shared_kv(xp, kv_norm, w_kv, k_norm, cos_p, sin_p)
                ks_, vs_ = _shared_kv(xs, kv_norm, w_kv, k_norm, cos_s, sin_s)
                for g in range(N_BRANCH):
                    new_rows = jnp.stack([ks_[:, :, g], vs_[:, :, g]], axis=2)
                    kv_cat.append(jnp.concatenate([caches[g].astype(new_rows.dtype), new_rows], axis=1))
                    keep = min(WINDOWS[g], S_p)
                    kv_new_p.append(jnp.stack([kp[:, -keep:, g], vp[:, -keep:, g]], axis=2))
            qp = _queries(xp, b_norm[j], w_q[j], q_norm[j], cos_p, sin_p)
            qs = _queries(xs, b_norm[j], w_q[j], q_norm[j], cos_s, sin_s)
            outs_p, lses_p, outs_s, lses_s = [], [], [], []
            for g in range(N_BRANCH):
                o, l = _dilated_prompt(qp[:, :, g], kp[:, :, g], vp[:, :, g], WINDOWS[g], DILATIONS[g])
                outs_p.append(o)
                lses_p.append(l)
                o, l = _dilated_sample(qs[:, :, g], kv_cat[g][:, :, 0], kv_cat[g][:, :, 1],
                                       caches[g].shape[1], WINDOWS[g], DILATIONS[g])
                outs_s.append(o)
                lses_s.append(l)
            xp = xp + _merge(outs_p, lses_p, w_o[j])
            xs = xs + _merge(outs_s, lses_s, w_o[j])
        xp = xp + _swiglu(_rmsnorm(xp, ffn_norm[layer]), w_gate[layer], w_up[layer], w_down[layer])
        xs = xs + _swiglu(_rmsnorm(xs, ffn_norm[layer]), w_gate[layer], w_up[layer], w_down[layer])

    pool_prompt = jnp.stack(pool_p, axis=1)
    pool_sample = jnp.stack(pool_s, axis=1)
    kv128_prompt = kv_new_p[0]
    kv512_prompt = kv_new_p[1]
    kv2048_prompt = kv_new_p[2]
    kv128_sample = kv_cat[0][:, -caches[0].shape[1]:]
    kv512_sample = kv_cat[1][:, -caches[1].shape[1]:]
    kv2048_sample = kv_cat[2][:, -caches[2].shape[1]:]
    return (xp, xs, pool_prompt, pool_sample, kv128_prompt, kv128_sample,
            kv512_prompt, kv512_sample, kv2048_prompt, kv2048_sample)
```

```python
import numpy as np
from contextlib import ExitStack
import concourse.bass as bass
import concourse.mybir as mybir
from concourse.bass_utils import run_bass_kernel_spmd

F32 = mybir.dt.float32
BF16 = mybir.dt.bfloat16
AF = mybir.ActivationFunctionType
ALU = mybir.AluOpType
AX = mybir.AxisListType

ENGS = ("pe", "act", "dve", "pool", "sp")


class _Op:
    __slots__ = ("eng", "fn", "reads", "writes", "dkey", "waits", "tok", "need_inc", "idx")


class Prog:
    def __init__(self, nc):
        self.nc = nc
        self.ops = {e: [] for e in ENGS}
        self.all_ops = []
        self.last_w = {}
        self.readers = {}

    def op(self, eng, fn, reads=(), writes=(), dkey=None):
        o = _Op()
        o.eng = eng
        o.fn = fn
        o.dkey = dkey
        o.need_inc = dkey is not None
        o.tok = None
        deps = []
        for r in reads:
            w = self.last_w.get(r)
            if w is not None:
                deps.append((w, "raw"))
        for r in writes:
            w = self.last_w.get(r)
            if w is not None:
                deps.append((w, "waw"))
            rd = self.readers.get(r)
            if rd is not None:
                for x in rd[0].values():
                    deps.append((x, "war"))
                for x in rd[1]:
                    deps.append((x, "war"))
        o.waits = deps
        for r in reads:
            rd = self.readers.get(r)
            if rd is None:
                rd = self.readers[r] = ({}, [])
            if dkey is None:
                rd[0][eng] = o
            else:
                rd[1].append(o)
        for r in writes:
            self.last_w[r] = o
            self.readers[r] = ({}, [])
        self.ops[eng].append(o)
        self.all_ops.append(o)
        return o

    def resolve(self):
        for o in self.all_ops:
            real = []
            for (d, kind) in o.waits:
                if d is o:
                    continue
                if d.dkey is None and o.dkey is None and d.eng == o.eng:
                    if o.eng == "pe" or kind != "raw":
                        continue
                real.append(d)
            o.waits = real
            for d in real:
                d.need_inc = True
        cnt = {e: 0 for e in ENGS}
        gen = {e: 0 for e in ENGS}
        dcnt = {}
        for o in self.all_ops:
            if not o.need_inc:
                continue
            if o.dkey is not None:
                k = ("d", o.dkey)
                dcnt[k] = dcnt.get(k, 0) + 16
                o.tok = (k, dcnt[k])
            else:
                e = o.eng
                if cnt[e] >= 12000:
                    gen[e] += 1
                    cnt[e] = 0
                cnt[e] += 1
                o.tok = (("e", e, gen[e]), cnt[e])
        self.sem_keys = []
        self.final = {}
        for o in self.all_ops:
            if o.tok is not None:
                if o.tok[0] not in self.final:
                    self.sem_keys.append(o.tok[0])
                self.final[o.tok[0]] = max(self.final.get(o.tok[0], 0), o.tok[1])

    def emit(self, sems):
        nc = self.nc
        engmap = {"pe": "tensor", "act": "scalar", "dve": "vector", "pool": "gpsimd", "sp": "sync"}
        with nc.Block() as block:
            for e in ENGS:
                def body(eng, ops=self.ops[e], e=e):
                    seen = {}
                    for o in ops:
                        need = {}
                        for d in o.waits:
                            k, v = d.tok
                            if seen.get(k, 0) >= v:
                                continue
                            if need.get(k, 0) < v:
                                need[k] = v
                        for k, v in need.items():
                            eng.wait_ge(sems[k], v)
                            seen[k] = v
                        ins = o.fn(eng)
                        if o.tok is not None:
                            ins.then_inc(sems[o.tok[0]], 16 if o.dkey is not None else 1)
                    if e == "sp":
                        for k, v in self.final.items():
                            if k[0] == "d":
                                eng.wait_ge(sems[k], v)
                getattr(block, engmap[e])(body)


D = 1024
DFF = 2816
NJ = 22
EPS = 1e-6
WIN = (2, 4, 8, 16)
NEG = -30000.0


def build(stage=99):
    nc = bass.Bass("TRN2", target_bir_lowering=False)
    P = Prog(nc)
    es = ExitStack()

    BIGW = ("w_kv", "w_q", "w_o", "w_gate", "w_up", "w_down", "xe", "c128", "c512", "c2048")

    def din(name, shape, dt=F32):
        if stage == -1 and name in BIGW:
            shape = [1, 1]
        return nc.dram_tensor(name, list(shape), dt, kind="ExternalInput").ap()

    def dout(name, shape):
        return nc.dram_tensor(name, list(shape), F32, kind="ExternalOutput").ap()

    def dint(name, shape, dt):
        return nc.dram_tensor(name, list(shape), dt, kind="Internal").ap()

    def sb(name, shape, dt):
        return es.enter_context(nc.sbuf_tensor(name, list(shape), dt))

    esA = ExitStack()

    def sbA(name, shape, dt):
        return esA.enter_context(nc.sbuf_tensor(name, list(shape), dt))

    def psum(name, shape, dt):
        return es.enter_context(nc.psum_tensor(name, list(shape), dt))

    xe = din("xe", [4096, D])
    xs = din("xs", [32, D])
    spool = din("spool", [4, 2, 15, D])
    cch = [din("c128", [4, 128, 2, D]), din("c512", [4, 512, 2, D]), din("c2048", [4, 2048, 2, D])]
    a_norm = din("a_norm", [2, D])
    pool_w = din("pool_w", [2, 4, 256, 256])
    pool_scale = din("pool_scale", [2, D])
    kv_norm = din("kv_norm", [D])
    w_kv = din("w_kv", [D, 6144])
    k_norm = din("k_norm", [3, 64])
    b_norm = din("b_norm", [2, D])
    w_q = din("w_q", [2, D, 3072])
    q_norm = din("q_norm", [2, 3, 64])
    w_o = din("w_o", [2, D, D])
    ffn_norm = din("ffn_norm", [4, D])
    w_gate = din("w_gate", [4, D, DFF])
    w_up = din("w_up", [4, D, DFF])
    w_down = din("w_down", [4, DFF, D])
    cs_e = din("cs_e", [4096, 128])
    cs_s = din("cs_s", [32, 128])
    ic = din("ic", [2, 4, 16])
    hb = din("hb", [128, 1])
    mkall = din("mkall", [128, 11, 512])

    y_own = dout("y_own", [2048, D])
    y_s = dout("y_s", [32, D])
    pool_p = dout("pool_p", [2, 15, D])
    pool_s = dout("pool_s", [4, 2, 15, D])
    kvp = [dout("kvp128", [128, 2, D]), dout("kvp512", [512, 2, D]), dout("kvp2048", [2048, 2, D])]
    kvs = [dout("kvs128", [4, 128, 2, D]), dout("kvs512", [4, 512, 2, D]), dout("kvs2048", [4, 2048, 2, D])]

    wg_b = dint("wg_b", [4, D, DFF], BF16)
    wu_b = dint("wu_b", [4, D, DFF], BF16)
    wd_b = dint("wd_b", [4, DFF, D], BF16)
    wkv_b = dint("wkv_b", [D, 6144], BF16)
    wq_b = dint("wq_b", [2, D, 3072], BF16)
    wo_b = dint("wo_b", [2, D, D], BF16)
    kt0_s = dint("kt0_s", [8, 128, 4096], BF16)
    kt1_s = dint("kt1_s", [8, 128, 4, 1024], BF16)
    kt2_s = dint("kt2_s", [8, 128, 16, 256], BF16)
    v_s = dint("v_s", [3, 4096, D], BF16)
    ktn_s = dint("ktn_s", [3, 8, 128, 32], BF16)
    vn_s = dint("vn_s", [3, 32, D], BF16)
    mks_in = din("mks", [128, 80])
    mkn_in = din("mkn", [32, 12, 64])

    xT = sb("xT", [128, 8, 2080], F32)
    identf = sb("identf", [128, 128], F32)
    identb = sb("identb", [128, 128], BF16)
    onesb = sb("onesb", [128, 128], BF16)
    epst = sb("epst", [128, 1], F32)
    gcols = sb("gcols", [128, 9, 8], F32)
    sqb = sb("sqb", [128, 2, 512], BF16)
    rstd = sb("rstd", [128, 512], F32)
    hT = sb("hT", [128, 8, 512], BF16)
    big = sb("big", [128, 24, 512], BF16)
    sg = sb("sg", [128, 2, 512], F32)
    wgt = sb("wgt", [128, 2, 8, 256], BF16)
    wut = sb("wut", [128, 2, 8, 256], BF16)
    wdt = sb("wdt", [128, 2, 22, 128], BF16)
    gAB = sb("gAB", [128, 3, 3, 128], F32)
    gtmp = sb("gtmp", [128, 3, 64], F32)
    cst = sb("cst", [128, 128], F32)
    cs4 = sb("cs4", [128, 4, 128], F32)
    tabt = sb("tabt", [128, 128], F32)
    ss8 = sb("ss8", [128, 8], F32)
    t1 = sb("t1", [128, 1, 512], F32)
    w1 = sb("w1", [128, 512], F32)
    kf = sb("kf", [128, 2, 512], F32)
    kb = sb("kb", [128, 2, 512], BF16)
    ktst = sb("ktst", [128, 4, 512], BF16)
    yst = sb("yst", [128, 2, D], F32)

    uT = sbA("uT", [128, 2, 8, 528], BF16)
    wsum = sbA("wsum", [128, 2, 528], F32)
    icb = sbA("icb", [128, 2, 4, 16], F32)
    wpf = sbA("wpf", [128, 512], F32)
    wp = sbA("wp", [128, 2, 4, 2, 256], BF16)
    usT = sbA("usT", [128, 8, 4, 24], BF16)
    wsS = sbA("wsS", [128, 2, 4, 24], F32)
    uf = sbA("uf", [128, 8, 32], F32)
    psS = [psum("psS%d" % i, [128, 2, 512], F32) for i in range(2)]
    ps = [psS[0][:, 0, :], psS[0][:, 1, :], psS[1][:, 0, :], psS[1][:, 1, :]] + \
         [psum("ps%d" % i, [128, 512], F32) for i in range(4, 8)]
    psT = ps[7][:].bitcast(BF16)

    def op(eng, fn, reads=(), writes=(), dkey=None):
        return P.op(eng, fn, reads, writes, dkey)

    op("pool", lambda e: e.memset(identf[:], 0.0), writes=["identf"])
    op("pool", lambda e: e.affine_select(out=identf[:], in_=identf[:], pattern=[[-1, 128]],
                                         compare_op=ALU.not_equal, fill=1.0, base=0, channel_multiplier=1),
       reads=["identf"], writes=["identf"])
    op("dve", lambda e: e.tensor_copy(out=identb[:], in_=identf[:]), reads=["identf"], writes=["identb"])
    op("dve", lambda e: e.memset(onesb[:], 1.0), writes=["onesb"])
    op("dve", lambda e: e.memset(epst[:], EPS), writes=["epst"])
    op("dve", lambda e: e.memset(uT[:], 0.0), writes=[("uT", 0), ("uT", 1)])
    gsrc = [a_norm[0], a_norm[1], ffn_norm[0], ffn_norm[1], ffn_norm[2], ffn_norm[3], kv_norm, b_norm[0], b_norm[1]]
    for i, g in enumerate(gsrc):
        op("sp", lambda e, i=i, g=g: e.dma_start(out=gcols[:, i, :], in_=g.rearrange("(k p) -> p k", p=128),
                                                 allow_slow_non_contiguous=True),
           writes=["gcols"], dkey="gcols")
    op("sp", lambda e: e.dma_start(out=icb[:].rearrange("p a g t -> p (a g t)"),
                                   in_=ic.rearrange("a g t -> (a g t)").partition_broadcast(128)),
       writes=["icb"], dkey="icb")
    op("sp", lambda e: e.dma_start(out=yst[:].rearrange("p a c -> p (a c)"),
                                   in_=pool_scale.rearrange("a c -> (a c)").partition_broadcast(128)),
       writes=[("yst", 0), ("yst", 1)], dkey="scb")
    for l in range(2):
        for g in range(4):
            op("sp", lambda e, l=l, g=g: e.dma_start(
                out=wpf[:, 0:512].rearrange("p (k n) -> p k n", k=2),
                in_=pool_w[l, g].rearrange("(k p) n -> p k n", p=128)),
               writes=["wpf"], dkey="wpf")
            for k in range(2):
                op("dve", lambda e, l=l, g=g, k=k: e.tensor_tensor(
                    out=wp[:, l, g, k, :], in0=wpf[:, k * 256:(k + 1) * 256],
                    in1=yst[:, l, g * 256:(g + 1) * 256], op=ALU.mult),
                   reads=["wpf", ("yst", 0), ("yst", 1)], writes=["wp"])

    def cast(dst, src, key, nsplit=4):
        n = src.shape[0]
        st = n // nsplit
        for s in range(nsplit):
            op("pool", lambda e, s=s: e.dma_start(out=dst[s * st:(s + 1) * st], in_=src[s * st:(s + 1) * st]),
               writes=[key], dkey=key)

    def cast_ffn(l):
        cast(wg_b[l], w_gate[l], ("wg", l))
        cast(wu_b[l], w_up[l], ("wu", l))
        cast(wd_b[l], w_down[l], ("wd", l))

    if stage >= 0:
        cast_ffn(0)
        cast_ffn(1)
        cast(wkv_b, w_kv, "wkv", 8)
    if stage >= 3:
        cast(wq_b[0], w_q[0], ("wq", 0))
        cast(wo_b[0], w_o[0], ("wo", 0))
        cast_ffn(2)
        cast(wq_b[1], w_q[1], ("wq", 1))
        cast(wo_b[1], w_o[1], ("wo", 1))
        cast_ffn(3)

    cnt = {"sp": 0, "pt": 0, "kf": 0, "x": 0, "sq": 0, "psA": 0, "psB": 0, "psD": 0, "sg": 0, "w": 0, "wd": 0, "ys": 0, "wk": 0}

    def load_group_x(src, r0, ntok, c0):
        for t0 in range(0, ntok, 128):
            nr = min(128, ntok - t0)
            b = cnt["x"] % 2
            cnt["x"] += 1
            op("sp", lambda e, b=b, t0=t0, nr=nr: e.dma_start(out=yst[:nr, b, :], in_=src[r0 + t0:r0 + t0 + nr, :]),
               writes=[("yst", b)], dkey=("yst", b))
            for hf in range(2):
                pb = ps[5 + hf]
                for k4 in range(4):
                    kc = hf * 4 + k4
                    op("pe", lambda e, b=b, kc=kc, k4=k4, nr=nr, pb=pb: e.transpose(
                        out=pb[:, k4 * 128:k4 * 128 + nr], in_=yst[:nr, b, kc * 128:(kc + 1) * 128],
                        identity=identf[:nr, :nr]),
                       reads=[("yst", b), "identf"], writes=[("ps", 5 + hf)])
                op("act", lambda e, hf=hf, nr=nr, t0=t0, pb=pb: e.activation(
                    out=xT[:, hf * 4:hf * 4 + 4, c0 + t0:c0 + t0 + nr],
                    in_=pb[:].rearrange("p (k t) -> p k t", k=4)[:, :, :nr], func=AF.Copy),
                   reads=[("ps", 5 + hf)], writes=[("xT", c0)])

    def store_group_T(srcT, srckey, c0, ntok, dst, r0, dkey):
        for t0 in range(0, ntok, 128):
            nr = min(128, ntok - t0)
            b = cnt["ys"] % 2
            cnt["ys"] += 1
            for hf in range(2):
                pb = ps[5 + hf]
                for k4 in range(4):
                    kc = hf * 4 + k4
                    op("pe", lambda e, kc=kc, k4=k4, nr=nr, t0=t0, pb=pb: e.transpose(
                        out=pb[:nr, k4 * 128:(k4 + 1) * 128], in_=srcT[:, kc, c0 + t0:c0 + t0 + nr],
                        identity=identf[:]),
                       reads=[srckey, "identf"], writes=[("ps", 5 + hf)])
                op("act", lambda e, hf=hf, nr=nr, b=b, pb=pb: e.activation(
                    out=yst[:nr, b, hf * 512:(hf + 1) * 512], in_=pb[:nr, :], func=AF.Copy),
                   reads=[("ps", 5 + hf)], writes=[("yst", b)])
            op("sp", lambda e, b=b, nr=nr, t0=t0: e.dma_start(out=dst[r0 + t0:r0 + t0 + nr, :], in_=yst[:nr, b, :]),
               reads=[("yst", b)], dkey=dkey)

    def norm_feat(c0, ntok, gi, dst, dkeyw, dcol0, samp=False):
        pr = ps[4]
        for kc in range(8):
            b = cnt["sq"] % 2
            cnt["sq"] += 1
            op("act", lambda e, kc=kc, b=b: e.activation(out=sqb[:, b, :ntok], in_=xT[:, kc, c0:c0 + ntok], func=AF.Square),
               reads=[("xT", c0)], writes=[("sqb", b)])
            op("pe", lambda e, kc=kc, b=b: e.matmul(pr[:, :ntok], lhsT=onesb[:], rhs=sqb[:, b, :ntok],
                                                    start=(kc == 0), stop=(kc == 7)),
               reads=[("sqb", b), "onesb"], writes=[("ps", 4)])
        op("act", lambda e: e.activation(out=rstd[:, :ntok], in_=pr[:, :ntok], func=AF.Sqrt, scale=1.0 / D, bias=epst[:]),
           reads=[("ps", 4), "epst"], writes=["rstd"])
        op("dve", lambda e: e.reciprocal(out=rstd[:, :ntok], in_=rstd[:, :ntok]), reads=["rstd"], writes=["rstd"])
        for kc in range(8):
            if samp:
                op("dve", lambda e, kc=kc: e.scalar_tensor_tensor(
                    out=dst[:, kc, :, 16:24], in0=xT[:, kc, c0:c0 + 32].rearrange("p (n t) -> p n t", n=4),
                    scalar=gcols[:, gi, kc:kc + 1], in1=rstd[:, :32].rearrange("p (n t) -> p n t", n=4),
                    op0=ALU.mult, op1=ALU.mult),
                   reads=[("xT", c0), "gcols", "rstd"], writes=[dkeyw])
            else:
                op("dve", lambda e, kc=kc: e.scalar_tensor_tensor(
                    out=dst[:, kc, dcol0:dcol0 + ntok], in0=xT[:, kc, c0:c0 + ntok], scalar=gcols[:, gi, kc:kc + 1],
                    in1=rstd[:, :ntok], op0=ALU.mult, op1=ALU.mult),
                   reads=[("xT", c0), "gcols", "rstd"], writes=[dkeyw])

    def u_rows(gi, c0, rcol0):
        for kc in range(8):
            op("dve", lambda e, kc=kc: e.scalar_tensor_tensor(
                out=uf[:, kc, :], in0=xT[:, kc, c0:c0 + 32], scalar=gcols[:, gi, kc:kc + 1],
                in1=rstd[:, rcol0:rcol0 + 32], op0=ALU.mult, op1=ALU.mult),
               reads=[("xT", c0), "gcols", "rstd"], writes=["uf"])
        b = cnt["ys"] % 2
        cnt["ys"] += 1
        for hf in range(2):
            pb = ps[5 + hf]
            for k4 in range(4):
                kc = hf * 4 + k4
                op("pe", lambda e, kc=kc, k4=k4, pb=pb: e.transpose(
                    out=pb[:32, k4 * 128:(k4 + 1) * 128], in_=uf[:, kc, :], identity=identf[:]),
                   reads=["uf", "identf"], writes=[("ps", 5 + hf)])
            op("act", lambda e, hf=hf, b=b, pb=pb: e.activation(
                out=yst[:32, b, hf * 512:(hf + 1) * 512], in_=pb[:32, :], func=AF.Copy),
               reads=[("ps", 5 + hf)], writes=[("yst", b)])
        return b

    def ffn(l, c0, ntok):
        gi = 2 + l
        norm_feat(c0, ntok, gi, hT[:], "hT", 0)
        for jp in range(11):
            wb = cnt["w"] % 2
            cnt["w"] += 1
            op("sp", lambda e, jp=jp, wb=wb: e.dma_start(
                out=wgt[:, wb], in_=wg_b[l, :, jp * 256:(jp + 1) * 256].rearrange("(k p) n -> p k n", p=128)),
               reads=[("wg", l)], writes=[("wgt", wb)], dkey=("wgt", wb))
            op("sp", lambda e, jp=jp, wb=wb: e.dma_start(
                out=wut[:, wb], in_=wu_b[l, :, jp * 256:(jp + 1) * 256].rearrange("(k p) n -> p k n", p=128)),
               reads=[("wu", l)], writes=[("wut", wb)], dkey=("wut", wb))
            for j2 in range(2):
                j = jp * 2 + j2
                pa = cnt["psA"] % 2
                cnt["psA"] += 1
                pG, pU = ps[pa], ps[2 + pa]
                for kc in range(8):
                    op("pe", lambda e, kc=kc, wb=wb, j2=j2, pG=pG: e.matmul(
                        pG[:, :ntok], lhsT=wgt[:, wb, kc, j2 * 128:(j2 + 1) * 128], rhs=hT[:, kc, :ntok],
                        start=(kc == 0), stop=(kc == 7)),
                       reads=[("wgt", wb), "hT"], writes=[("ps", pa)])
                for kc in range(8):
                    op("pe", lambda e, kc=kc, wb=wb, j2=j2, pU=pU: e.matmul(
                        pU[:, :ntok], lhsT=wut[:, wb, kc, j2 * 128:(j2 + 1) * 128], rhs=hT[:, kc, :ntok],
                        start=(kc == 0), stop=(kc == 7)),
                       reads=[("wut", wb), "hT"], writes=[("ps", 2 + pa)])
                sgb = cnt["sg"] % 2
                cnt["sg"] += 1
                op("act", lambda e, sgb=sgb, pG=pG: e.activation(out=sg[:, sgb, :ntok], in_=pG[:, :ntok], func=AF.Silu),
                   reads=[("ps", pa)], writes=[("sg", sgb)])
                op("dve", lambda e, sgb=sgb, pU=pU, j=j: e.tensor_tensor(
                    out=big[:, j, :ntok], in0=pU[:, :ntok], in1=sg[:, sgb, :ntok], op=ALU.mult),
                   reads=[("ps", 2 + pa), ("sg", sgb)], writes=[("big", j)])
        for c in range(8):
            wb = cnt["wd"] % 2
            cnt["wd"] += 1
            op("sp", lambda e, c=c, wb=wb: e.dma_start(
                out=wdt[:, wb], in_=wd_b[l, :, c * 128:(c + 1) * 128].rearrange("(j p) n -> p j n", p=128)),
               reads=[("wd", l)], writes=[("wdt", wb)], dkey=("wdt", wb))
            pd = cnt["psD"] % 2
            cnt["psD"] += 1
            pD = ps[pd]
            for j in range(NJ):
                op("pe", lambda e, j=j, wb=wb, pD=pD: e.matmul(
                    pD[:, :ntok], lhsT=wdt[:, wb, j, :], rhs=big[:, j, :ntok],
                    start=(j == 0), stop=(j == NJ - 1)),
                   reads=[("wdt", wb), ("big", j)], writes=[("ps", pd)])
            op("dve", lambda e, c=c, pD=pD: e.tensor_tensor(
                out=xT[:, c, c0:c0 + ntok], in0=pD[:, :ntok], in1=xT[:, c, c0:c0 + ntok], op=ALU.add),
               reads=[("ps", pd), ("xT", c0)], writes=[("xT", c0)])

    def pool_layer(i, c0, ntok, start_tab, last=False):
        u = uT[:, i]
        norm_feat(c0, ntok, i, u, ("uT", i), 16)
        if last:
            yb = u_rows(i, c0 + ntok - 32, ntok - 32)
            op("sp", lambda e, yb=yb: e.dma_start(out=pool_p[i], in_=yst[17:32, yb, :]), reads=[("yst", yb)], dkey="po")
        for kc in range(8):
            g = kc // 2
            src = None
            nst = g + 1
            for s in range(nst):
                sh = 1 << s
                lo = 2 * sh
                wbuf = s % 2
                if s == 0:
                    op("dve", lambda e, kc=kc: e.tensor_tensor(
                        out=wsum[:, 0, 2:16 + ntok], in0=u[:, kc, 2:16 + ntok], in1=u[:, kc, 1:15 + ntok], op=ALU.add),
                       reads=[("uT", i)], writes=[("wsum", 0)])
                else:
                    op("dve", lambda e, sh=sh, lo=lo, wbuf=wbuf: e.tensor_tensor(
                        out=wsum[:, wbuf, lo:16 + ntok], in0=wsum[:, 1 - wbuf, lo:16 + ntok],
                        in1=wsum[:, 1 - wbuf, lo - sh:16 + ntok - sh], op=ALU.add),
                       reads=[("wsum", 1 - wbuf)], writes=[("wsum", wbuf)])
            wl = (nst - 1) % 2
            op("dve", lambda e, kc=kc, wl=wl, g=g: e.scalar_tensor_tensor(
                out=hT[:, kc, :ntok], in0=wsum[:, wl, 16:16 + ntok], scalar=1.0 / WIN[g], in1=u[:, kc, 16:16 + ntok],
                op0=ALU.mult, op1=ALU.subtract),
               reads=[("wsum", wl), ("uT", i)], writes=["hT"])
            if start_tab is not None:
                op("dve", lambda e, kc=kc, wl=wl, g=g: e.tensor_tensor(
                    out=wsum[:, wl, 0:16], in0=wsum[:, wl, 16:32], in1=icb[:, start_tab, g, :], op=ALU.mult),
                   reads=[("wsum", wl), "icb"], writes=[("wsum", wl)])
                op("dve", lambda e, kc=kc, wl=wl: e.tensor_tensor(
                    out=hT[:, kc, 0:16], in0=wsum[:, wl, 0:16], in1=u[:, kc, 16:32], op=ALU.subtract),
                   reads=[("wsum", wl), ("uT", i)], writes=["hT"])
        pool_mm(i, c0, ntok)
        op("pool", lambda e: e.tensor_copy(out=u[:, :, 1:16], in_=u[:, :, 1 + ntok:16 + ntok]),
           reads=[("uT", i)], writes=[("uT", i)])

    def pool_mm(i, c0, ntok):
        for c in range(8):
            g = c // 2
            pd = cnt["psD"] % 2
            cnt["psD"] += 1
            pD = ps[pd]
            for k in range(2):
                op("pe", lambda e, k=k, g=g, c=c, pD=pD: e.matmul(
                    pD[:, :ntok], lhsT=wp[:, i, g, k, (c % 2) * 128:(c % 2) * 128 + 128], rhs=hT[:, 2 * g + k, :ntok],
                    start=(k == 0), stop=(k == 1)),
                   reads=["wp", "hT"], writes=[("ps", pd)])
            op("dve", lambda e, c=c, pD=pD: e.tensor_tensor(
                out=xT[:, c, c0:c0 + ntok], in0=pD[:, :ntok], in1=xT[:, c, c0:c0 + ntok], op=ALU.add),
               reads=[("ps", pd), ("xT", c0)], writes=[("xT", c0)])


    def pool_layer_sample(i):
        c0 = 2048
        b = cnt["x"] % 2
        cnt["x"] += 1
        for n in range(4):
            op("sp", lambda e, b=b, n=n: e.dma_start(out=yst[n * 15:(n + 1) * 15, b, :], in_=spool[n, i]),
               writes=[("yst", b)], dkey=("yst", b))
        for hf in range(2):
            pb = ps[5 + hf]
            for k4 in range(4):
                kc = hf * 4 + k4
                op("pe", lambda e, b=b, kc=kc, k4=k4, pb=pb: e.transpose(
                    out=pb[:, k4 * 128:k4 * 128 + 60], in_=yst[:60, b, kc * 128:(kc + 1) * 128],
                    identity=identf[:60, :60]),
                   reads=[("yst", b), "identf"], writes=[("ps", 5 + hf)])
            for k4 in range(4):
                kc = hf * 4 + k4
                op("act", lambda e, kc=kc, k4=k4, pb=pb: e.activation(
                    out=usT[:, kc, :, 1:16], in_=pb[:, k4 * 128:k4 * 128 + 60].rearrange("p (n r) -> p n r", n=4),
                    func=AF.Copy),
                   reads=[("ps", 5 + hf)], writes=["usT"])
        norm_feat(c0, 32, i, usT, "usT", 0, samp=True)
        yb = u_rows(i, c0, 0)
        for n in range(4):
            op("sp", lambda e, n=n, yb=yb: e.dma_start(out=pool_s[n, i, 7:15, :], in_=yst[n * 8:(n + 1) * 8, yb, :]),
               reads=[("yst", yb)], dkey="po")
            op("sp", lambda e, n=n: e.dma_start(out=pool_s[n, i, 0:7, :], in_=spool[n, i, 8:15, :]), dkey="po")
        for kc in range(8):
            g = kc // 2
            nst = g + 1
            for s_ in range(nst):
                sh = 1 << s_
                lo = 2 * sh
                wbuf = s_ % 2
                if s_ == 0:
                    op("dve", lambda e, kc=kc: e.tensor_tensor(
                        out=wsS[:, 0, :, 2:24], in0=usT[:, kc, :, 2:24], in1=usT[:, kc, :, 1:23], op=ALU.add),
                       reads=["usT"], writes=[("wsS", 0)])
                else:
                    op("dve", lambda e, sh=sh, lo=lo, wbuf=wbuf: e.tensor_tensor(
                        out=wsS[:, wbuf, :, lo:24], in0=wsS[:, 1 - wbuf, :, lo:24],
                        in1=wsS[:, 1 - wbuf, :, lo - sh:24 - sh], op=ALU.add),
                       reads=[("wsS", 1 - wbuf)], writes=[("wsS", wbuf)])
            wl = (nst - 1) % 2
            op("dve", lambda e, kc=kc, wl=wl, g=g: e.scalar_tensor_tensor(
                out=hT[:, kc, 0:32].rearrange("p (n t) -> p n t", n=4), in0=wsS[:, wl, :, 16:24],
                scalar=1.0 / WIN[g], in1=usT[:, kc, :, 16:24], op0=ALU.mult, op1=ALU.subtract),
               reads=[("wsS", wl), "usT"], writes=["hT"])
        pool_mm(i, c0, 32)

    def setup_gab(si, gsrc_ap):
        op("sp", lambda e: e.dma_start(out=gtmp[:].rearrange("p b d -> p (b d)"),
                                       in_=gsrc_ap.rearrange("b d -> (b d)").partition_broadcast(128)),
           writes=["gtmp"], dkey="gtmp")
        op("dve", lambda e: e.tensor_copy(out=gAB[:, si, :, 0:64], in_=gtmp[:]), reads=["gtmp"], writes=["gAB"])
        op("dve", lambda e: e.tensor_scalar(out=gAB[:, si, :, 64:96], in0=gtmp[:, :, 32:64], scalar1=-1.0, scalar2=None,
                                            op0=ALU.mult), reads=["gtmp"], writes=["gAB"])
        op("dve", lambda e: e.tensor_copy(out=gAB[:, si, :, 96:128], in_=gtmp[:, :, 0:32]), reads=["gtmp"], writes=["gAB"])

    def make_tabs(si, cs_src, r0, ntok):
        cnt["si"] = si
        for t0 in range(0, ntok, 128):
            nr = min(128, ntok - t0)
            tt = t0 // 128
            op("sp", lambda e, t0=t0, nr=nr, tt=tt: e.dma_start(out=cs4[:nr, tt, :], in_=cs_src[r0 + t0:r0 + t0 + nr, :]),
               writes=["cs4"], dkey="cs4")

    def normrope(pk, nr, tt, b, si_unused=None):
        x3 = pk[:nr, :].rearrange("p (h d) -> p h d", h=8)
        kb_ = cnt["kf"] % 2
        cnt["kf"] += 1
        op("act", lambda e: e.activation(out=w1[:nr, :], in_=pk[:nr, :], func=AF.Square),
           reads=[pkkey(pk)], writes=["w1"])
        op("dve", lambda e: e.tensor_reduce(out=ss8[:nr, :], in_=w1[:nr, :].rearrange("p (h d) -> p h d", h=8),
                                            axis=AX.X, op=ALU.add), reads=["w1"], writes=["ss8"])
        op("act", lambda e: e.activation(out=ss8[:nr, :], in_=ss8[:nr, :], func=AF.Sqrt, scale=1.0 / 64, bias=epst[:nr, :]),
           reads=["ss8", "epst"], writes=["ss8"])
        op("dve", lambda e: e.reciprocal(out=ss8[:nr, :], in_=ss8[:nr, :]), reads=["ss8"], writes=["ss8"])
        si = cnt["si"]
        op("dve", lambda e: e.tensor_tensor(out=tabt[:nr, :], in0=cs4[:nr, tt, :], in1=gAB[:nr, si, b, :], op=ALU.mult),
           reads=["cs4", "gAB"], writes=["tabK"])
        tv = t1[:nr, 0, :].rearrange("p (h d) -> p h d", h=8)
        wv = w1[:nr, :].rearrange("p (h d) -> p h d", h=8)
        op("dve", lambda e: e.tensor_tensor(out=tv, in0=x3, in1=tabt[:nr, 0:64].unsqueeze(1).to_broadcast([nr, 8, 64]),
                                            op=ALU.mult), reads=[pkkey(pk), "tabK"], writes=["t1"])
        op("dve", lambda e: e.tensor_tensor(out=wv[:, :, 0:32], in0=x3[:, :, 32:64],
                                            in1=tabt[:nr, 64:96].unsqueeze(1).to_broadcast([nr, 8, 32]), op=ALU.mult),
           reads=[pkkey(pk), "tabK"], writes=["w1"])
        op("dve", lambda e: e.tensor_tensor(out=wv[:, :, 32:64], in0=x3[:, :, 0:32],
                                            in1=tabt[:nr, 96:128].unsqueeze(1).to_broadcast([nr, 8, 32]), op=ALU.mult),
           reads=[pkkey(pk), "tabK"], writes=["w1"])
        op("pool", lambda e: e.tensor_tensor(out=t1[:nr, 0, :], in0=t1[:nr, 0, :], in1=w1[:nr, :], op=ALU.add),
           reads=["t1", "w1"], writes=["t1"])
        op("pool", lambda e: e.tensor_tensor(out=kf[:nr, kb_, :].rearrange("p (h d) -> p h d", h=8), in0=tv,
                                             in1=ss8[:nr, :].unsqueeze(2).to_broadcast([nr, 8, 64]), op=ALU.mult),
           reads=["t1", "ss8"], writes=[("kf", kb_)])
        return kb_

    pskeys = {}

    def pkkey(pk):
        return pskeys[id(pk)]

    for i_, p_ in enumerate(ps):
        pskeys[id(p_)] = ("ps", i_)

    def load_wchunk(src2d, col0, key):
        wb = cnt["wk"] % 2
        cnt["wk"] += 1
        t = wgt if wb == 0 else wut
        nm = "wgt" if wb == 0 else "wut"
        view = t[:].rearrange("p b k n -> p (b k n)").rearrange("p (k n) -> p k n", k=8)
        op("sp", lambda e: e.dma_start(out=view, in_=src2d[:, col0:col0 + 512].rearrange("(k p) n -> p k n", p=128)),
           reads=[key], writes=[(nm, 0), (nm, 1)], dkey=(nm, 0))
        return view, [(nm, 0), (nm, 1)]

    def kv_proj(kind, gi_, c0, ntok, r0):
        norm_feat(c0, ntok, 6, hT[:], "hT", 0)
        samp = (kind == "S")
        make_tabs(0, cs_s if samp else cs_e, 0 if samp else r0, ntok)
        if kind == "H" and gi_ < 3:
            chunks = [4, 5, 10, 11]
        else:
            chunks = list(range(12))
        for ck in chunks:
            isk = ck < 6
            b = (ck % 6) // 2
            hh = ck % 2
            wv, wkeys = load_wchunk(wkv_b, ck * 512, "wkv")
            for t0 in range(0, ntok, 128):
                nr = min(128, ntok - t0)
                tt = t0 // 128
                pa = cnt["psA"] % 2
                cnt["psA"] += 1
                pk = ps[pa]
                for kc in range(8):
                    op("pe", lambda e, kc=kc, pk=pk, t0=t0, nr=nr, wv=wv: e.matmul(
                        pk[:nr, :], lhsT=hT[:, kc, t0:t0 + nr], rhs=wv[:, kc, :], start=(kc == 0), stop=(kc == 7)),
                       reads=["hT"] + wkeys, writes=[("ps", pa)])
                if isk:
                    fb = normrope(pk, nr, tt, b)
                else:
                    fb = cnt["kf"] % 2
                    cnt["kf"] += 1
                    op("act", lambda e, fb=fb, pk=pk, nr=nr: e.activation(out=kf[:nr, fb, :], in_=pk[:nr, :], func=AF.Copy),
                       reads=[("ps", pa)], writes=[("kf", fb)])
                sel = 0 if isk else 1
                if kind == "O":
                    pos = gi_ * 512 + t0
                    keep = (128, 512, 2048)[b]
                    if pos >= 2048 - keep:
                        rr = pos - (2048 - keep)
                        op("sp", lambda e, fb=fb, rr=rr, nr=nr, b=b, sel=sel, hh=hh: e.dma_start(
                            out=kvp[b][rr:rr + nr, sel, hh * 512:(hh + 1) * 512], in_=kf[:nr, fb, :]),
                           reads=[("kf", fb)], dkey="kvo")
                if samp:
                    Wd = (128, 512, 2048)[b]
                    for n in range(4):
                        op("sp", lambda e, fb=fb, n=n, b=b, sel=sel, hh=hh, Wd=Wd: e.dma_start(
                            out=kvs[b][n, Wd - 8:Wd, sel, hh * 512:(hh + 1) * 512], in_=kf[n * 8:(n + 1) * 8, fb, :]),
                           reads=[("kf", fb)], dkey="kvo")
                if stage >= 3 and samp:
                    op("act", lambda e, fb=fb, nr=nr: e.activation(out=kb[:nr, fb, :], in_=kf[:nr, fb, :], func=AF.Copy),
                       reads=[("kf", fb)], writes=[("kb", fb)])
                    if isk:
                        for q in range(4):
                            op("pe", lambda e, fb=fb, q=q, nr=nr: e.transpose(
                                out=psT[:, q * 128:q * 128 + nr], in_=kb[:nr, fb, q * 128:(q + 1) * 128],
                                identity=identb[:nr, :nr]),
                               reads=[("kb", fb), "identb"], writes=[("ps", 7)])
                        op("dve", lambda e: e.tensor_copy(
                            out=ktst[:, :, 0:32], in_=psT[:, 0:512].rearrange("p (q t) -> p q t", q=4)[:, :, 0:32]),
                           reads=[("ps", 7)], writes=["ktst"])
                        op("sp", lambda e, b=b, hh=hh: e.dma_start(
                            out=ktn_s[b, hh * 4:hh * 4 + 4].rearrange("q p t -> p q t"), in_=ktst[:, :, 0:32]),
                           reads=["ktst"], writes=[("kts", "S", 0)], dkey=("kts", "S", 0))
                    else:
                        op("sp", lambda e, fb=fb, b=b, hh=hh: e.dma_start(
                            out=vn_s[b, :, hh * 512:(hh + 1) * 512], in_=kb[:32, fb, :]),
                           reads=[("kb", fb)], writes=[("vs", "S", 0)], dkey=("vs", "S", 0))
                if stage >= 3 and not samp:
                    op("act", lambda e, fb=fb, nr=nr: e.activation(out=kb[:nr, fb, :], in_=kf[:nr, fb, :], func=AF.Copy),
                       reads=[("kf", fb)], writes=[("kb", fb)])
                    if isk:
                        for q in range(4):
                            op("pe", lambda e, fb=fb, q=q, nr=nr: e.transpose(
                                out=psT[:, q * 128:q * 128 + nr], in_=kb[:nr, fb, q * 128:(q + 1) * 128],
                                identity=identb[:nr, :nr]),
                               reads=[("kb", fb), "identb"], writes=[("ps", 7)])
                        dil = (1, 4, 16)[b]
                        mt = 128 // dil
                        op("dve", lambda e, dil=dil, mt=mt, tt=tt: e.tensor_copy(
                            out=ktst[:].rearrange("p q (r m) -> p q r m", r=dil)[:, :, :, tt * mt:(tt + 1) * mt],
                            in_=psT[:, 0:512].rearrange("p (q m r) -> p q r m", q=4, r=dil)),
                           reads=[("ps", 7)], writes=["ktst"])
                        if t0 + 128 >= ntok:
                            for q in range(4):
                                hpq = hh * 4 + q
                                if b == 0:
                                    dst = kt0_s[hpq, :, r0:r0 + 512]
                                    src_ = ktst[:, q, :]
                                elif b == 1:
                                    dst = kt1_s[hpq, :, :, r0 // 4:r0 // 4 + 128]
                                    src_ = ktst[:, q, :].rearrange("p (r m) -> p r m", r=4)
                                else:
                                    dst = kt2_s[hpq, :, :, r0 // 16:r0 // 16 + 32]
                                    src_ = ktst[:, q, :].rearrange("p (r m) -> p r m", r=16)
                                op("sp", lambda e, dst=dst, src_=src_: e.dma_start(out=dst, in_=src_),
                                   reads=["ktst"], writes=[("kts", kind, gi_)], dkey=("kts", kind, gi_))
                    else:
                        op("sp", lambda e, fb=fb, nr=nr, b=b, hh=hh, t0=t0: e.dma_start(
                            out=v_s[b, r0 + t0:r0 + t0 + nr, hh * 512:(hh + 1) * 512], in_=kb[:nr, fb, :]),
                           reads=[("kb", fb)], writes=[("vs", kind, gi_)], dkey=("vs", kind, gi_))

    if stage >= 0:
        for b, Wd in enumerate((128, 512, 2048)):
            for n in range(4):
                op("act", lambda e, b=b, n=n, Wd=Wd: e.dma_start(out=kvs[b][n, 0:Wd - 8], in_=cch[b][n, 8:Wd]), dkey="cc")
        setup_gab(0, k_norm)
        def ginfo(kind, gi_):
            if kind == "S":
                return 2048, 0, 32
            return gi_ * 512, gi_ * 512 + (2048 if kind == "O" else 0), 512

        pairs = [[("H", 0), ("H", 1), ("H", 2), ("H", 3)], [("O", 0), ("O", 1), ("O", 2), ("O", 3)], [("S", 0)]]
        for pair in pairs:
            for (kind, gi_) in pair:
                c0, r0, ntok = ginfo(kind, gi_)
                load_group_x(xs if kind == "S" else xe, r0, ntok, c0)
            for i in range(2):
                for (kind, gi_) in pair:
                    c0, r0, ntok = ginfo(kind, gi_)
                    st = None
                    if kind == "H" and gi_ == 0:
                        st = 0
                    if kind == "O" and gi_ == 0:
                        st = 1
                    if kind == "S":
                        pool_layer_sample(i)
                    else:
                        pool_layer(i, c0, 512, st, last=(kind == "O" and gi_ == 3))
                for (kind, gi_) in pair:
                    c0, r0, ntok = ginfo(kind, gi_)
                    ffn(i, c0, ntok)
            for (kind, gi_) in pair:
                c0, r0, ntok = ginfo(kind, gi_)
                if stage >= 2:
                    kv_proj(kind, gi_, c0, ntok, r0)
                if stage < 3:
                    if kind == "O":
                        store_group_T(xT, ("xT", c0), c0, 512, y_own, gi_ * 512, "y")
                    if kind == "S":
                        store_group_T(xT, ("xT", c0), c0, 32, y_s, 0, "y")

    import os
    if stage >= 3 or stage == -1:
        OLD = [("uT", 0), ("uT", 1), ("wsum", 0), ("wsum", 1), "icb", "wpf", "wp", "usT", ("wsS", 0), ("wsS", 1), "uf"]
        esA.close()
        KT0 = sb("KT0", [128, 640], BF16)
        KT1 = sb("KT1", [128, 4, 256], BF16)
        KT2 = sb("KT2", [128, 16, 256], BF16)
        V0 = sb("V0", [128, 5, 128], BF16)
        V1 = sb("V1", [128, 2, 4, 128], BF16)
        V2p = sb("V2p", [128, 16, 128], BF16)
        V2c = sb("V2c", [128, 16, 128], BF16)
        mkb = sb("mkb", [128, 11, 512], BF16)
        ptb = sb("ptb", [128, 2, 2, 512], BF16)
        hbt = sb("hbt", [128, 2], F32)
        pending = {"KT0", "KT1", "KT2", "V0", "V1", "V2p", "V2c", "mkb", ("ptb", 0), ("ptb", 1), "hbt", "mks", "mkn"}
        _op0 = op

        def op(eng, fn, reads=(), writes=(), dkey=None):
            writes = list(writes)
            hit = [w for w in writes if w in pending]
            if hit:
                for w in hit:
                    pending.discard(w)
                writes = writes + OLD
            return _op0(eng, fn, reads, writes, dkey)

        op("pool", lambda e: e.dma_start(out=mkb[:], in_=mkall), writes=["mkb"], dkey="mkb")
        op("sp", lambda e: e.dma_start(out=hbt[:, 0:1], in_=hb), writes=["hbt"], dkey="hbt")
        op("dve", lambda e: e.memset(hbt[:, 1:2], 0.0), writes=["hbt"])
        SCR = [("kts", k_, g_) for k_ in ("H", "O") for g_ in range(4)] + [("vs", k_, g_) for k_ in ("H", "O") for g_ in range(4)]

        def q_proj(kind, jb, c0, ntok, r0):
            norm_feat(c0, ntok, 7 + jb, hT[:], "hT", 0)
            samp = (kind == "S")
            make_tabs(1 + jb, cs_s if samp else cs_e, 0 if samp else r0, ntok)
            for ck in range(6):
                b = ck // 2
                hh = ck % 2
                wv, wkeys = load_wchunk(wq_b[jb], ck * 512, ("wq", jb))
                for t0 in range(0, ntok, 128):
                    nr = min(128, ntok - t0)
                    tt = t0 // 128
                    pa = cnt["psA"] % 2
                    cnt["psA"] += 1
                    pk = ps[pa]
                    for kc in range(8):
                        op("pe", lambda e, kc=kc, pk=pk, t0=t0, nr=nr, wv=wv: e.matmul(
                            pk[:nr, :], lhsT=hT[:, kc, t0:t0 + nr], rhs=wv[:, kc, :], start=(kc == 0), stop=(kc == 7)),
                           reads=["hT"] + wkeys, writes=[("ps", pa)])
                    fb = normrope(pk, nr, tt, b)
                    op("act", lambda e, fb=fb, nr=nr: e.activation(out=kb[:nr, fb, :], in_=kf[:nr, fb, :], func=AF.Copy),
                       reads=[("kf", fb)], writes=[("kb", fb)])
                    for q in range(4):
                        op("pe", lambda e, fb=fb, q=q, nr=nr: e.transpose(
                            out=psT[:, q * 128:q * 128 + nr], in_=kb[:nr, fb, q * 128:(q + 1) * 128],
                            identity=identb[:nr, :nr]),
                           reads=[("kb", fb), "identb"], writes=[("ps", 7)])
                    i0 = b * 8 + hh * 4
                    op("dve", lambda e, i0=i0, nr=nr, t0=t0: e.tensor_copy(
                        out=big[:, i0:i0 + 4, t0:t0 + nr], in_=psT[:, 0:512].rearrange("p (q t) -> p q t", q=4)[:, :, :nr]),
                       reads=[("ps", 7)], writes=[("big", i0 + q_) for q_ in range(4)])

        def attention(i):
            E0 = 2048 + 512 * i
            KA = os.environ.get("KA", "012nqmep")
            for hp in range(int(os.environ.get("KB_NHP", "8"))):
                hc = slice(hp * 128, (hp + 1) * 128)
                ld = lambda out, in_, key: op("sp", lambda e: e.dma_start(out=out, in_=in_), reads=SCR, writes=[key], dkey=key)
                ld(KT0[:], kt0_s[hp, :, E0 - 128:E0 + 512], "KT0")
                ld(V0[:], v_s[0, E0 - 128:E0 + 512, hc].rearrange("(k p) c -> p k c", p=128), "V0")
                ld(KT1[:], kt1_s[hp, :, :, (E0 - 512) // 4:(E0 + 512) // 4], "KT1")
                for a_ in range(2):
                    ld(V1[:, a_], v_s[1, E0 - 512 + 512 * a_:E0 + 512 * a_, hc].rearrange("(m r) c -> m r c", r=4), "V1")
                ld2 = lambda out, in_, key: op("sp", lambda e: e.dma_start(out=out, in_=in_), reads=SCR,
                                               writes=[key, (key, 0), (key, 1)], dkey=key)
                ld2(KT2[:], kt2_s[hp], "KT2")
                ld2(V2p[:], v_s[2, 0:2048, hc].rearrange("(m r) c -> m r c", r=16), "V2p")
                ld2(V2c[:], v_s[2, 2048:4096, hc].rearrange("(m r) c -> m r c", r=16), "V2c")
                Ub, Zb = 4 + 2 * (hp % 2), 5 + 2 * (hp % 2)
                U, Z = ps[Ub], ps[Zb]
                firstUZ = [True]
                batches = []
                hbias0 = 0 if i == 0 else 1
                batches.append((hbias0, 0, 128, 256, [(lambda h: KT0[h * 64:(h + 1) * 64, 0:128], "KT0",
                                                        lambda h: V0[:, 0, h * 64:(h + 1) * 64], "V0", 0, slice(0, 128), 128, 128)]))
                for kbj in range(3):
                    batches.append((1, 0, 0, 256, [(lambda h, kbj=kbj: KT0[h * 64:(h + 1) * 64, 128 * (kbj + 1):128 * (kbj + 2)], "KT0",
                                                    lambda h, kbj=kbj: V0[:, kbj + 1, h * 64:(h + 1) * 64], "V0", 0,
                                                    slice(128 * kbj, 128 * kbj + 256), 0, 256)]))
                batches.append((1, 0, 0, 128, [(lambda h: KT0[h * 64:(h + 1) * 64, 512:640], "KT0",
                                                lambda h: V0[:, 4, h * 64:(h + 1) * 64], "V0", 0, slice(384, 512), 0, 128)]))
                for a in range(2):
                    items = []
                    for r in range(4):
                        items.append((lambda h, a=a, r=r: KT1[h * 64:(h + 1) * 64, r, a * 128:(a + 1) * 128], "KT1",
                                      lambda h, a=a, r=r: V1[:, a, r, h * 64:(h + 1) * 64], "V1", 1,
                                      slice(r, 512, 4), r * 128, 128))
                    batches.append(((hbias0 if a == 0 else 1), 1 + a, 0, 512, items))
                for a in range(2):
                    items = []
                    for r in range(16):
                        vt = V2p if a == 0 else V2c
                        items.append((lambda h, r=r, a=a: KT2[h * 64:(h + 1) * 64, r, a * 128:(a + 1) * 128], "KT2",
                                      lambda h, r=r, vt=vt: vt[:, r, h * 64:(h + 1) * 64], ("V2p" if a == 0 else "V2c"), 2,
                                      slice(r, 512, 16), r * 32, 32))
                    batches.append(((0 if a == 0 else 1), (3 + i if a == 0 else 7 + i), 0, 512, items))
                batches = [bt for bt in batches if str(bt[4][0][4]) in KA]
                for (bsel, midx, c_lo, c_hi, items) in batches:
                    sp_ = cnt["sp"] % 2
                    cnt["sp"] += 1
                    Sh = [ps[2 * sp_], ps[2 * sp_ + 1]]
                    skeys = [("ps", 2 * sp_), ("ps", 2 * sp_ + 1)]
                    fs = [True, True]
                    for (ktf, ktk, vf, vk, b, qs, scol, N) in items:
                        for h in range(2):
                            op("pe", lambda e, h=h, ktf=ktf, b=b, qs=qs, scol=scol, N=N, st=fs[h], Sh=Sh, hp=hp: e.matmul(
                                Sh[h][:, scol:scol + N], lhsT=ktf(h), rhs=big[h * 64:(h + 1) * 64, b * 8 + hp, qs],
                                start=st, stop=False, skip_group_check=True),
                               reads=[ktk, ("big", b * 8 + hp)], writes=[skeys[h]])
                            fs[h] = False
                    for h in range(2):
                        op("pe", lambda e, h=h, Sh=Sh, midx=midx, c_lo=c_lo, c_hi=c_hi: e.matmul(
                            Sh[h][:, c_lo:c_hi], lhsT=identb[:, :], rhs=mkb[:, midx, c_lo:c_hi], start=False, stop=True,
                            skip_group_check=True),
                           reads=["identb", "mkb"], writes=[skeys[h]])
                    pb_ = cnt["pt"] % 2
                    cnt["pt"] += 1
                    op("act", lambda e, sp_=sp_, pb_=pb_, bsel=bsel, c_lo=c_lo, c_hi=c_hi: e.activation(
                        out=ptb[:, pb_, :, c_lo:c_hi], in_=psS[sp_][:, :, c_lo:c_hi], func=AF.Exp, scale=0.125,
                        bias=hbt[:, bsel:bsel + 1]),
                       reads=skeys + ["hbt"], writes=[("ptb", pb_)])
                    for (ktf, ktk, vf, vk, b, qs, scol, N) in items:
                        for h in range(2):
                            f1 = firstUZ[0]
                            op("pe", lambda e, h=h, vf=vf, qs=qs, scol=scol, N=N, f1=f1, pb_=pb_, U=U: e.matmul(
                                U[h * 64:(h + 1) * 64, qs], lhsT=vf(h), rhs=ptb[:, pb_, h, scol:scol + N],
                                start=f1, stop=False, skip_group_check=True, tile_position=(0, h * 64)),
                               reads=[vk, ("ptb", pb_)], writes=[("ps", Ub)])
                            op("pe", lambda e, h=h, qs=qs, scol=scol, N=N, f1=f1, pb_=pb_, Z=Z: e.matmul(
                                Z[h * 64:(h + 1) * 64, qs], lhsT=onesb[:, 0:64], rhs=ptb[:, pb_, h, scol:scol + N],
                                start=f1, stop=False, skip_group_check=True, tile_position=(0, h * 64)),
                               reads=["onesb", ("ptb", pb_)], writes=[("ps", Zb)])
                        firstUZ[0] = False
                if "n" not in KA:
                    continue
                op("dve", lambda e, Z=Z: e.reciprocal(out=rstd[:, :], in_=Z[:, :]), reads=[("ps", Zb)], writes=["rstd"])
                op("dve", lambda e, U=U, hp=hp: e.tensor_tensor(out=hT[:, hp, :], in0=U[:, :], in1=rstd[:, :], op=ALU.mult),
                   reads=[("ps", Ub), "rstd"], writes=["hT"])
                if stage == -1:
                    op("dve", lambda e, U=U: e.tensor_copy(out=kf[:, 0, :], in_=U[:, :]), reads=[("ps", Ub)], writes=[("kf", 0)])
                    op("dve", lambda e, Z=Z: e.tensor_copy(out=kf[:, 1, :], in_=Z[:, :]), reads=[("ps", Zb)], writes=[("kf", 1)])
                    op("sp", lambda e, hp=hp: e.dma_start(out=t_u[:, hp, :], in_=kf[:, 0, :]), reads=[("kf", 0)], dkey="tu")
                    op("sp", lambda e, hp=hp: e.dma_start(out=t_z[:, hp, :], in_=kf[:, 1, :]), reads=[("kf", 1)], dkey="tu")


        mks = sb("mks_t", [128, 80], BF16)
        mkn = sb("mkn_t", [32, 12, 64], BF16)
        op("pool", lambda e: e.dma_start(out=mks[:], in_=mks_in), writes=["mks"], dkey="mks")
        op("pool", lambda e: e.dma_start(out=mkn[:], in_=mkn_in), writes=["mkn"], dkey="mks")

        def sample_attention():
            ksrs = [V2p[:].rearrange("p r c -> p (r c)")[:, k_ * 1024:(k_ + 1) * 1024] for k_ in range(2)]
            vsrs = [V2c[:].rearrange("p r c -> p (r c)")[:, k_ * 1024:(k_ + 1) * 1024] for k_ in range(2)]
            KTss = [KT2[:].rearrange("p r m -> p (r m)")[:, k_ * 1024:(k_ + 1) * 1024].rearrange("p (q k) -> p q k", q=8)
                    for k_ in range(2)]
            bcnt = [0]
            KTn = V0[:].rearrange("p k c -> p (k c)")[:, 0:256].rearrange("p (q t) -> p q t", q=8)
            Vn = V1[:].rearrange("p a r c -> p (a r c)")
            Us, Zs = ps[4], ps[5]
            firstUZ = [True]
            SK = [("kts", "S", 0), ("vs", "S", 0)]

            def block(M, qk_items, mask_ap, ncols, vtile, vkey):
                sp_ = cnt["sp"] % 2
                cnt["sp"] += 1
                Sh = [ps[2 * sp_], ps[2 * sp_ + 1]]
                skeys = [("ps", 2 * sp_), ("ps", 2 * sp_ + 1)]
                fs = [True, True]
                for (lf, lk, rf, scol, N, oc) in qk_items:
                    for h in range(2):
                        op("pe", lambda e, h=h, lf=lf, rf=rf, scol=scol, N=N, st=fs[h], Sh=Sh: e.matmul(
                            Sh[h][:M, scol:scol + N], lhsT=lf(h), rhs=rf(h), start=st, stop=False, skip_group_check=True),
                           reads=[lk] + [("big", j_) for j_ in range(24)], writes=[skeys[h]])
                        fs[h] = False
                if mask_ap is not None:
                    for h in range(2):
                        op("pe", lambda e, h=h, Sh=Sh: e.matmul(
                            Sh[h][:M, 0:ncols], lhsT=identb[:M, :M], rhs=mask_ap, start=False, stop=True, skip_group_check=True),
                           reads=["identb", "mks", "mkn"], writes=[skeys[h]])
                pb_ = cnt["pt"] % 2
                cnt["pt"] += 1
                op("act", lambda e, sp_=sp_, pb_=pb_: e.activation(
                    out=ptb[:M, pb_, :, 0:ncols], in_=psS[sp_][:M, :, 0:ncols], func=AF.Exp, scale=0.125),
                   reads=skeys, writes=[("ptb", pb_)])
                for (lf, lk, rf, scol, N, oc) in qk_items:
                    hpq = oc[0]
                    for h in range(2):
                        f1 = firstUZ[0]
                        op("pe", lambda e, h=h, scol=scol, N=N, oc=oc, f1=f1, pb_=pb_, hpq=hpq: e.matmul(
                            Us[h * 64:(h + 1) * 64, oc[1]], lhsT=vtile[:M, hpq * 128 + h * 64:hpq * 128 + h * 64 + 64],
                            rhs=ptb[:M, pb_, h, scol:scol + N], start=f1, stop=False, skip_group_check=True,
                            tile_position=(0, h * 64)),
                           reads=[vkey, ("ptb", pb_)], writes=[("ps", 4)])
                        op("pe", lambda e, h=h, scol=scol, N=N, oc=oc, f1=f1, pb_=pb_: e.matmul(
                            Zs[h * 64:(h + 1) * 64, oc[1]], lhsT=onesb[:M, 0:64],
                            rhs=ptb[:M, pb_, h, scol:scol + N], start=f1, stop=False, skip_group_check=True,
                            tile_position=(0, h * 64)),
                           reads=["onesb", ("ptb", pb_)], writes=[("ps", 5)])
                    firstUZ[0] = False

            for b, (Wd, dil) in enumerate(((128, 1), (512, 4), (2048, 16))):
                op("sp", lambda e, b=b: e.dma_start(out=KTn, in_=ktn_s[b].rearrange("q p t -> p q t")),
                   reads=SK, writes=["V0"], dkey="V0")
                op("sp", lambda e, b=b: e.dma_start(out=Vn[:32, :], in_=vn_s[b]), reads=SK, writes=["V1"], dkey="V1")
                for n in range(4):
                    items = []
                    for hp in range(8):
                        items.append((lambda h, hp=hp: KTn[h * 64:(h + 1) * 64, hp, :], "V0",
                                      lambda h, hp=hp, b=b, n=n: big[h * 64:(h + 1) * 64, b * 8 + hp, n * 8:n * 8 + 8],
                                      hp * 8, 8, (hp, slice(hp * 32 + n * 8, hp * 32 + n * 8 + 8))))
                    block(32, items, mkn[:32, n * 3 + b, :], 64, Vn, "V1")
                    ncls = min(dil, 8)
                    nq = 8 // ncls if dil <= 8 else 1
                    for r in range(ncls):
                        kk_ = bcnt[0] % 2
                        bcnt[0] += 1
                        ksr, vsr, KTs = ksrs[kk_], vsrs[kk_], KTss[kk_]
                        kK, kV, kT = ("V2p", kk_), ("V2c", kk_), ("KT2", kk_)
                        op("pool", lambda e, b=b, n=n, r=r, dil=dil, Wd=Wd, ksr=ksr: e.dma_start(
                            out=ksr, in_=cch[b][n, r:Wd:dil, 0, :]), writes=["V2p", kK], dkey=kK)
                        op("pool", lambda e, b=b, n=n, r=r, dil=dil, Wd=Wd, vsr=vsr: e.dma_start(
                            out=vsr, in_=cch[b][n, r:Wd:dil, 1, :]), writes=["V2c", kV], dkey=kV)
                        for hp in range(8):
                            op("pe", lambda e, hp=hp, ksr=ksr: e.transpose(
                                out=psT[:, hp * 128:(hp + 1) * 128], in_=ksr[:, hp * 128:(hp + 1) * 128], identity=identb[:]),
                               reads=[kK, "identb"], writes=[("ps", 7)])
                        op("dve", lambda e, KTs=KTs: e.tensor_copy(out=KTs, in_=psT[:].rearrange("p (q k) -> p q k", q=8)),
                           reads=[("ps", 7)], writes=["KT2", kT])
                        items = []
                        for hp in range(8):
                            if dil == 1:
                                qsl = slice(n * 8, n * 8 + 8)
                            elif dil == 4:
                                qsl = slice(n * 8 + r, n * 8 + 8, 4)
                            else:
                                qsl = slice(n * 8 + r, n * 8 + r + 1)
                            osl = slice(hp * 32 + qsl.start, hp * 32 + qsl.stop, qsl.step)
                            items.append((lambda h, hp=hp, KTs=KTs: KTs[h * 64:(h + 1) * 64, hp, :], kT,
                                          lambda h, hp=hp, b=b, qsl=qsl: big[h * 64:(h + 1) * 64, b * 8 + hp, qsl],
                                          hp * nq, nq, (hp, osl)))
                        mask_ap = None
                        if dil == 1:
                            mask_ap = mks[:, 0:64]
                        elif dil == 4:
                            mask_ap = mks[:, 64:80]
                        block(128, items, mask_ap, 8 * nq, vsr, kV)
            op("dve", lambda e: e.reciprocal(out=rstd[:, 0:256], in_=Zs[:, 0:256]), reads=[("ps", 5)], writes=["rstd"])
            op("dve", lambda e: e.tensor_tensor(
                out=hT[:, :, 0:32], in0=Us[:, 0:256].rearrange("p (q t) -> p q t", q=8),
                in1=rstd[:, 0:256].rearrange("p (q t) -> p q t", q=8), op=ALU.mult),
               reads=[("ps", 4), "rstd"], writes=["hT"])

        def o_proj(jb, c0, ntok):
            for c in range(8):
                wb = cnt["wd"] % 2
                cnt["wd"] += 1
                op("sp", lambda e, c=c, wb=wb: e.dma_start(
                    out=wdt[:, wb, 0:8, :], in_=wo_b[jb][:, c * 128:(c + 1) * 128].rearrange("(k p) n -> p k n", p=128)),
                   reads=[("wo", jb)], writes=[("wdt", wb)], dkey=("wdt", wb))
                pd = cnt["psD"] % 2
                cnt["psD"] += 1
                pD = ps[pd]
                for k in range(8):
                    op("pe", lambda e, k=k, wb=wb, pD=pD: e.matmul(
                        pD[:, :ntok], lhsT=wdt[:, wb, k, :], rhs=hT[:, k, :ntok], start=(k == 0), stop=(k == 7)),
                       reads=[("wdt", wb), "hT"], writes=[("ps", pd)])
                op("dve", lambda e, c=c, pD=pD: e.tensor_tensor(
                    out=xT[:, c, c0:c0 + ntok], in0=pD[:, :ntok], in1=xT[:, c, c0:c0 + ntok], op=ALU.add),
                   reads=[("ps", pd), ("xT", c0)], writes=[("xT", c0)])

        if stage == -1:
            t_q = din("t_q", [128, 24, 512])
            t_kt0 = din("t_kt0", [8, 128, 4096])
            t_kt1 = din("t_kt1", [8, 128, 4, 1024])
            t_kt2 = din("t_kt2", [8, 128, 16, 256])
            t_v = din("t_v", [3, 4096, D])
            t_out = dout("t_out", [128, 8, 512])
            t_u = dout("t_u", [128, 8, 512])
            t_z = dout("t_z", [128, 8, 512])
            for hp_ in range(8):
                op("pool", lambda e, hp_=hp_: e.dma_start(out=kt0_s[hp_], in_=t_kt0[hp_]), writes=[("kts", "H", 0)], dkey=("kts", "H", 0))
                op("pool", lambda e, hp_=hp_: e.dma_start(out=kt1_s[hp_], in_=t_kt1[hp_]), writes=[("kts", "H", 1)], dkey=("kts", "H", 1))
                op("pool", lambda e, hp_=hp_: e.dma_start(out=kt2_s[hp_], in_=t_kt2[hp_]), writes=[("kts", "H", 2)], dkey=("kts", "H", 2))
            for b_ in range(3):
                op("pool", lambda e, b_=b_: e.dma_start(out=v_s[b_], in_=t_v[b_]), writes=[("vs", "H", b_)], dkey=("vs", "H", b_))
            op("pool", lambda e: e.dma_start(out=big[:], in_=t_q), writes=[("big", j_) for j_ in range(24)], dkey="tq")
            attention(int(os.environ.get("KT_I", "0")))
            for hp_ in range(8):
                op("act", lambda e, hp_=hp_: e.activation(out=yst[:, 0, 0:512], in_=hT[:, hp_, :], func=AF.Copy),
                   reads=["hT"], writes=[("yst", 0)])
                op("sp", lambda e, hp_=hp_: e.dma_start(out=t_out[:, hp_, :], in_=yst[:, 0, 0:512]), reads=[("yst", 0)], dkey="y")
        KB = os.environ.get("KB", "qaofs") if stage >= 3 else ""
        NL0 = 2 if stage >= 3 else 0
        NG = int(os.environ.get("KB_NG", "4"))
        NL = min(NL0, int(os.environ.get("KB_NL", "2")))
        for jb in range(NL):
            setup_gab(1 + jb, q_norm[jb])
            for i in range(NG):
                c0 = 512 * i
                if "q" in KB:
                    q_proj("O", jb, c0, 512, 2048 + 512 * i)
                if "a" in KB:
                    attention(i)
                if "o" in KB:
                    o_proj(jb, c0, 512)
                if "f" in KB:
                    ffn(2 + jb, c0, 512)
                if jb == 1:
                    store_group_T(xT, ("xT", c0), c0, 512, y_own, i * 512, "y")
            if "s" in os.environ.get("KB", "qaofs"):
                q_proj("S", jb, 2048, 32, 0)
                sample_attention()
                o_proj(jb, 2048, 32)
            ffn(2 + jb, 2048, 32)
        if stage >= 3:
            store_group_T(xT, ("xT", 2048), 2048, 32, y_s, 0, "y")

    P.resolve()
    global _LASTP
    _LASTP = P
    sems = {}
    for n_, k in enumerate(P.sem_keys):
        sems[k] = es.enter_context(nc.semaphore("s%d" % n_))
    print("n_sems", len(sems), "n_ops", {e: len(P.ops[e]) for e in ENGS})
    P.emit(sems)
    es.close()
    return nc


_CACHE = {}


def _rope_tab(pos):
    inv = np.power(np.float32(10000.0), -np.arange(0, 64, 2, dtype=np.float32) / np.float32(64)).astype(np.float32)
    ang = pos.astype(np.float32)[:, None] * inv[None, :]
    c = np.cos(ang).astype(np.float32)
    s = np.sin(ang).astype(np.float32)
    return np.concatenate([c, c, s, s], axis=1).astype(np.float32)


def kernel(**inp):
    import os
    stage = int(os.environ.get("KSTAGE", "3"))
    if stage not in _CACHE:
        _CACHE[stage] = build(stage)
    nc = _CACHE[stage]
    f = lambda a: np.ascontiguousarray(np.asarray(a, dtype=np.float32))
    x_prompt = f(inp["x_prompt"])
    x_sample = f(inp["x_sample"])
    state_pool = f(inp["state_pool"])
    caches = [f(inp["cache_kv_w128"]), f(inp["cache_kv_w512"]), f(inp["cache_kv_w2048"])]
    wnames = ["a_norm", "pool_w", "pool_scale", "kv_norm", "w_kv", "k_norm", "b_norm", "w_q", "q_norm", "w_o",
              "ffn_norm", "w_gate", "w_up", "w_down"]
    W = {n: f(inp[n]) for n in wnames}
    kk = np.arange(128)
    cur = np.where(kk[:, None] <= kk[None, :], 0.0, NEG).astype(np.float32)
    prev = np.where(kk[:, None] >= kk[None, :], 0.0, NEG).astype(np.float32)
    mkall = np.zeros((128, 11, 512), np.float32)
    mkall[:, 0] = np.concatenate([cur, prev, cur, prev], axis=1)
    mkall[:, 1] = np.tile(prev, (1, 4))
    mkall[:, 2] = np.tile(cur, (1, 4))
    for i_ in range(4):
        mkall[:, 3 + i_] = np.tile(prev[:, 32 * i_:32 * i_ + 32], (1, 16))
        mkall[:, 7 + i_] = np.tile(cur[:, 32 * i_:32 * i_ + 32], (1, 16))
    mks_h = np.concatenate([np.tile(prev[:, 0:8], (1, 8)), np.tile(prev[:, 0:2], (1, 8))], axis=1).astype(np.float32)
    mkn_h = np.full((32, 12, 64), NEG, np.float32)
    for n_ in range(4):
        for b_, dil_ in enumerate((1, 4, 16)):
            for t_ in range(8):
                for tp_ in range(8):
                    if tp_ <= t_ and (t_ - tp_) % dil_ == 0:
                        mkn_h[n_ * 8 + tp_, n_ * 3 + b_, t_::8] = 0.0
    in_maps = []
    for c in range(8):
        b, half = c // 2, c % 2
        m = dict(W)
        if half == 1:
            m["xe"] = x_prompt[b]
            pos = np.arange(4096)
        else:
            m["xe"] = np.concatenate([np.zeros((2048, D), np.float32), x_prompt[b, :2048]], axis=0)
            pos = np.arange(4096) - 2048
        m["cs_e"] = _rope_tab(np.maximum(pos, 0))
        m["cs_s"] = np.tile(_rope_tab(8192 + np.arange(8)), (4, 1))
        m["xs"] = x_sample[4 * c:4 * c + 4].reshape(32, D)
        m["spool"] = state_pool[4 * c:4 * c + 4]
        m["c128"] = caches[0][4 * c:4 * c + 4].reshape(4, 128, 2, D)
        m["c512"] = caches[1][4 * c:4 * c + 4].reshape(4, 512, 2, D)
        m["c2048"] = caches[2][4 * c:4 * c + 4].reshape(4, 2048, 2, D)
        icv = np.zeros((2, 4, 16), np.float32)
        for g, w in enumerate(WIN):
            real = 1.0 / np.minimum(np.arange(16) + 1, w).astype(np.float32)
            plain = np.full(16, 1.0 / w, np.float32)
            icv[0, g] = real if half == 1 else plain
            icv[1, g] = real if half == 0 else plain
        m["ic"] = icv
        m["hb"] = np.full((128, 1), 0.0 if half == 1 else NEG, np.float32)
        m["mkall"] = mkall
        m["mks"] = mks_h
        m["mkn"] = mkn_h
        in_maps.append(m)
    res = run_bass_kernel_spmd(nc, in_maps, core_ids=list(range(8)))
    R = res.results
    y_prompt = np.zeros((4, 4096, D), np.float32)
    for c in range(8):
        y_prompt[c // 2, (c % 2) * 2048:(c % 2) * 2048 + 2048] = R[c]["y_own"]
    y_sample = np.concatenate([R[c]["y_s"].reshape(4, 8, D) for c in range(8)], axis=0)
    pool_prompt = np.stack([R[2 * b + 1]["pool_p"] for b in range(4)], axis=0)
    pool_sample = np.concatenate([R[c]["pool_s"] for c in range(8)], axis=0)
    outs = [y_prompt, y_sample, pool_prompt, pool_sample]
    for g, w in enumerate((128, 512, 2048)):
        outs.append(np.stack([R[2 * b + 1]["kvp%d" % w].reshape(w, 2, 16, 64) for b in range(4)], axis=0))
        outs.append(np.concatenate([R[c]["kvs%d" % w].reshape(4, w, 2, 16, 64) for c in range(8)], axis=0))
    return tuple(outs)
```

```python
import numpy as np
from contextlib import ExitStack
import concourse.bass as bass
import concourse.mybir as mybir
from concourse.bass_utils import run_bass_kernel_spmd

F32 = mybir.dt.float32
BF16 = mybir.dt.bfloat16
AF = mybir.ActivationFunctionType
ALU = mybir.AluOpType
AX = mybir.AxisListType

ENGS = ("pe", "act", "dve", "pool", "sp")


class _Op:
    __slots__ = ("eng", "fn", "reads", "writes", "dkey", "waits", "tok", "need_inc", "idx")


class Prog:
    def __init__(self, nc):
        self.nc = nc
        self.ops = {e: [] for e in ENGS}
        self.all_ops = []
        self.last_w = {}
        self.readers = {}

    def op(self, eng, fn, reads=(), writes=(), dkey=None):
        o = _Op()
        o.eng = eng
        o.fn = fn
        o.dkey = dkey
        o.need_inc = dkey is not None
        o.tok = None
        deps = []
        for r in reads:
            w = self.last_w.get(r)
            if w is not None:
                deps.append((w, "raw"))
        for r in writes:
            w = self.last_w.get(r)
            if w is not None:
                deps.append((w, "waw"))
            rd = self.readers.get(r)
            if rd is not None:
                for x in rd[0].values():
                    deps.append((x, "war"))
                for x in rd[1]:
                    deps.append((x, "war"))
        o.waits = deps
        for r in reads:
            rd = self.readers.get(r)
            if rd is None:
                rd = self.readers[r] = ({}, [])
            if dkey is None:
                rd[0][eng] = o
            else:
                rd[1].append(o)
        for r in writes:
            self.last_w[r] = o
            self.readers[r] = ({}, [])
        self.ops[eng].append(o)
        self.all_ops.append(o)
        return o

    def resolve(self):
        for o in self.all_ops:
            real = []
            for (d, kind) in o.waits:
                if d is o:
                    continue
                if d.dkey is None and o.dkey is None and d.eng == o.eng:
                    if o.eng == "pe" or kind != "raw":
                        continue
                real.append(d)
            o.waits = real
            for d in real:
                d.need_inc = True
        cnt = {e: 0 for e in ENGS}
        gen = {e: 0 for e in ENGS}
        dcnt = {}
        for o in self.all_ops:
            if not o.need_inc:
                continue
            if o.dkey is not None:
                k = ("d", o.dkey)
                dcnt[k] = dcnt.get(k, 0) + 16
                o.tok = (k, dcnt[k])
            else:
                e = o.eng
                if cnt[e] >= 12000:
                    gen[e] += 1
                    cnt[e] = 0
                cnt[e] += 1
                o.tok = (("e", e, gen[e]), cnt[e])
        self.sem_keys = []
        self.final = {}
        for o in self.all_ops:
            if o.tok is not None:
                if o.tok[0] not in self.final:
                    self.sem_keys.append(o.tok[0])
                self.final[o.tok[0]] = max(self.final.get(o.tok[0], 0), o.tok[1])

    def emit(self, sems):
        nc = self.nc
        engmap = {"pe": "tensor", "act": "scalar", "dve": "vector", "pool": "gpsimd", "sp": "sync"}
        with nc.Block() as block:
            for e in ENGS:
                def body(eng, ops=self.ops[e], e=e):
                    seen = {}
                    for o in ops:
                        need = {}
                        for d in o.waits:
                            k, v = d.tok
                            if seen.get(k, 0) >= v:
                                continue
                            if need.get(k, 0) < v:
                                need[k] = v
                        for k, v in need.items():
                            eng.wait_ge(sems[k], v)
                            seen[k] = v
                        ins = o.fn(eng)
                        if o.tok is not None:
                            ins.then_inc(sems[o.tok[0]], 16 if o.dkey is not None else 1)
                    if e == "sp":
                        for k, v in self.final.items():
                            if k[0] == "d":
                                eng.wait_ge(sems[k], v)
                getattr(block, engmap[e])(body)


D = 1024
DFF = 2816
NJ = 22
EPS = 1e-6
WIN = (2, 4, 8, 16)
NEG = -30000.0


def build(stage=99):
    nc = bass.Bass("TRN2", target_bir_lowering=False)
    P = Prog(nc)
    es = ExitStack()

    BIGW = ("w_kv", "w_q", "w_o", "w_gate", "w_up", "w_down", "xe", "c128", "c512", "c2048")

    def din(name, shape, dt=F32):
        if stage == -1 and name in BIGW:
            shape = [1, 1]
        return nc.dram_tensor(name, list(shape), dt, kind="ExternalInput").ap()

    def dout(name, shape):
        return nc.dram_tensor(name, list(shape), F32, kind="ExternalOutput").ap()

    def dint(name, shape, dt):
        return nc.dram_tensor(name, list(shape), dt, kind="Internal").ap()

    def sb(name, shape, dt):
        return es.enter_context(nc.sbuf_tensor(name, list(shape), dt))

    esA = ExitStack()

    def sbA(name, shape, dt):
        return esA.enter_context(nc.sbuf_tensor(name, list(shape), dt))

    def psum(name, shape, dt):
        return es.enter_context(nc.psum_tensor(name, list(shape), dt))

    xe = din("xe", [4096, D])
    xs = din("xs", [32, D])
    spool = din("spool", [4, 2, 15, D])
    cch = [din("c128", [4, 128, 2, D]), din("c512", [4, 512, 2, D]), din("c2048", [4, 2048, 2, D])]
    a_norm = din("a_norm", [2, D])
    pool_w = din("pool_w", [2, 4, 256, 256])
    pool_scale = din("pool_scale", [2, D])
    kv_norm = din("kv_norm", [D])
    w_kv = din("w_kv", [D, 6144])
    k_norm = din("k_norm", [3, 64])
    b_norm = din("b_norm", [2, D])
    w_q = din("w_q", [2, D, 3072])
    q_norm = din("q_norm", [2, 3, 64])
    w_o = din("w_o", [2, D, D])
    ffn_norm = din("ffn_norm", [4, D])
    w_gate = din("w_gate", [4, D, DFF])
    w_up = din("w_up", [4, D, DFF])
    w_down = din("w_down", [4, DFF, D])
    cs_e = din("cs_e", [4096, 128])
    cs_s = din("cs_s", [32, 128])
    ic = din("ic", [2, 4, 16])
    hb = din("hb", [128, 1])
    mkall = din("mkall", [128, 11, 512])

    y_own = dout("y_own", [2048, D])
    y_s = dout("y_s", [32, D])
    pool_p = dout("pool_p", [2, 15, D])
    pool_s = dout("pool_s", [4, 2, 15, D])
    kvp = [dout("kvp128", [128, 2, D]), dout("kvp512", [512, 2, D]), dout("kvp2048", [2048, 2, D])]
    kvs = [dout("kvs128", [4, 128, 2, D]), dout("kvs512", [4, 512, 2, D]), dout("kvs2048", [4, 2048, 2, D])]

    wg_b = dint("wg_b", [4, D, DFF], BF16)
    wu_b = dint("wu_b", [4, D, DFF], BF16)
    wd_b = dint("wd_b", [4, DFF, D], BF16)
    wkv_b = dint("wkv_b", [D, 6144], BF16)
    wq_b = dint("wq_b", [2, D, 3072], BF16)
    wo_b = dint("wo_b", [2, D, D], BF16)
    kt0_s = dint("kt0_s", [8, 128, 4096], BF16)
    kt1_s = dint("kt1_s", [8, 128, 4, 1024], BF16)
    kt2_s = dint("kt2_s", [8, 128, 16, 256], BF16)
    v_s = dint("v_s", [3, 4096, D], BF16)
    ktn_s = dint("ktn_s", [3, 8, 128, 32], BF16)
    vn_s = dint("vn_s", [3, 32, D], BF16)
    mks_in = din("mks", [128, 80])
    mkn_in = din("mkn", [32, 12, 64])

    xT = sb("xT", [128, 8, 2080], F32)
    identf = sb("identf", [128, 128], F32)
    identb = sb("identb", [128, 128], BF16)
    onesb = sb("onesb", [128, 128], BF16)
    epst = sb("epst", [128, 1], F32)
    gcols = sb("gcols", [128, 9, 8], F32)
    sqb = sb("sqb", [128, 2, 512], BF16)
    rstd = sb("rstd", [128, 512], F32)
    hT = sb("hT", [128, 8, 512], BF16)
    big = sb("big", [128, 24, 512], BF16)
    sg = sb("sg", [128, 2, 512], F32)
    wgt = sb("wgt", [128, 2, 8, 256], BF16)
    wut = sb("wut", [128, 2, 8, 256], BF16)
    wdt = sb("wdt", [128, 2, 22, 128], BF16)
    gAB = sb("gAB", [128, 3, 3, 128], F32)
    gtmp = sb("gtmp", [128, 3, 64], F32)
    cst = sb("cst", [128, 128], F32)
    cs4 = sb("cs4", [128, 4, 128], F32)
    tabt = sb("tabt", [128, 128], F32)
    ss8 = sb("ss8", [128, 2, 8], F32)
    w1 = sb("w1", [128, 2, 512], F32)
    kf = sb("kf", [128, 2, 512], F32)
    kb = sb("kb", [128, 2, 512], BF16)
    ktst = sb("ktst", [128, 4, 512], BF16)
    yst = sb("yst", [128, 2, D], F32)

    uT = sbA("uT", [128, 2, 8, 528], BF16)
    wsum = sbA("wsum", [128, 2, 528], F32)
    icb = sbA("icb", [128, 2, 4, 16], F32)
    wpf = sbA("wpf", [128, 512], F32)
    wp = sbA("wp", [128, 2, 4, 2, 256], BF16)
    usT = sbA("usT", [128, 8, 4, 24], BF16)
    wsS = sbA("wsS", [128, 2, 4, 24], F32)
    uf = sbA("uf", [128, 8, 32], F32)
    psS = [psum("psS%d" % i, [128, 2, 512], F32) for i in range(2)]
    ps = [psS[0][:, 0, :], psS[0][:, 1, :], psS[1][:, 0, :], psS[1][:, 1, :]] + \
         [psum("ps%d" % i, [128, 512], F32) for i in range(4, 8)]
    psT = ps[7][:].bitcast(BF16)

    def op(eng, fn, reads=(), writes=(), dkey=None):
        return P.op(eng, fn, reads, writes, dkey)

    op("pool", lambda e: e.memset(identf[:], 0.0), writes=["identf"])
    op("pool", lambda e: e.affine_select(out=identf[:], in_=identf[:], pattern=[[-1, 128]],
                                         compare_op=ALU.not_equal, fill=1.0, base=0, channel_multiplier=1),
       reads=["identf"], writes=["identf"])
    op("dve", lambda e: e.tensor_copy(out=identb[:], in_=identf[:]), reads=["identf"], writes=["identb"])
    op("dve", lambda e: e.memset(onesb[:], 1.0), writes=["onesb"])
    op("dve", lambda e: e.memset(epst[:], EPS), writes=["epst"])
    op("dve", lambda e: e.memset(uT[:], 0.0), writes=[("uT", 0), ("uT", 1)])
    gsrc = [a_norm[0], a_norm[1], ffn_norm[0], ffn_norm[1], ffn_norm[2], ffn_norm[3], kv_norm, b_norm[0], b_norm[1]]
    for i, g in enumerate(gsrc):
        op("sp", lambda e, i=i, g=g: e.dma_start(out=gcols[:, i, :], in_=g.rearrange("(k p) -> p k", p=128),
                                                 allow_slow_non_contiguous=True),
           writes=["gcols"], dkey="gcols")
    op("sp", lambda e: e.dma_start(out=icb[:].rearrange("p a g t -> p (a g t)"),
                                   in_=ic.rearrange("a g t -> (a g t)").partition_broadcast(128)),
       writes=["icb"], dkey="icb")
    op("sp", lambda e: e.dma_start(out=yst[:].rearrange("p a c -> p (a c)"),
                                   in_=pool_scale.rearrange("a c -> (a c)").partition_broadcast(128)),
       writes=[("yst", 0), ("yst", 1)], dkey="scb")
    for l in range(2):
        for g in range(4):
            op("sp", lambda e, l=l, g=g: e.dma_start(
                out=wpf[:, 0:512].rearrange("p (k n) -> p k n", k=2),
                in_=pool_w[l, g].rearrange("(k p) n -> p k n", p=128)),
               writes=["wpf"], dkey="wpf")
            for k in range(2):
                op("dve", lambda e, l=l, g=g, k=k: e.tensor_tensor(
                    out=wp[:, l, g, k, :], in0=wpf[:, k * 256:(k + 1) * 256],
                    in1=yst[:, l, g * 256:(g + 1) * 256], op=ALU.mult),
                   reads=["wpf", ("yst", 0), ("yst", 1)], writes=["wp"])

    def cast(dst, src, key, nsplit=4):
        n = src.shape[0]
        st = n // nsplit
        for s in range(nsplit):
            op("pool", lambda e, s=s: e.dma_start(out=dst[s * st:(s + 1) * st], in_=src[s * st:(s + 1) * st]),
               writes=[key], dkey=key)

    def cast_ffn(l):
        cast(wg_b[l], w_gate[l], ("wg", l))
        cast(wu_b[l], w_up[l], ("wu", l))
        cast(wd_b[l], w_down[l], ("wd", l))

    if stage >= 0:
        cast_ffn(0)
        cast_ffn(1)
        cast(wkv_b, w_kv, "wkv", 8)
    if stage >= 3:
        cast(wq_b[0], w_q[0], ("wq", 0))
        cast(wo_b[0], w_o[0], ("wo", 0))
        cast_ffn(2)
        cast(wq_b[1], w_q[1], ("wq", 1))
        cast(wo_b[1], w_o[1], ("wo", 1))
        cast_ffn(3)

    cnt = {"psK": 0, "sp": 0, "pt": 0, "kf": 0, "x": 0, "sq": 0, "psA": 0, "psB": 0, "psD": 0, "sg": 0, "w": 0, "wd": 0, "ys": 0, "wk": 0}

    def load_group_x(src, r0, ntok, c0):
        for t0 in range(0, ntok, 128):
            nr = min(128, ntok - t0)
            b = cnt["x"] % 2
            cnt["x"] += 1
            op("sp", lambda e, b=b, t0=t0, nr=nr: e.dma_start(out=yst[:nr, b, :], in_=src[r0 + t0:r0 + t0 + nr, :]),
               writes=[("yst", b)], dkey=("yst", b))
            for hf in range(2):
                pb = ps[5 + hf]
                for k4 in range(4):
                    kc = hf * 4 + k4
                    op("pe", lambda e, b=b, kc=kc, k4=k4, nr=nr, pb=pb: e.transpose(
                        out=pb[:, k4 * 128:k4 * 128 + nr], in_=yst[:nr, b, kc * 128:(kc + 1) * 128],
                        identity=identf[:nr, :nr]),
                       reads=[("yst", b), "identf"], writes=[("ps", 5 + hf)])
                op("act", lambda e, hf=hf, nr=nr, t0=t0, pb=pb: e.activation(
                    out=xT[:, hf * 4:hf * 4 + 4, c0 + t0:c0 + t0 + nr],
                    in_=pb[:].rearrange("p (k t) -> p k t", k=4)[:, :, :nr], func=AF.Copy),
                   reads=[("ps", 5 + hf)], writes=[("xT", c0)])

    def store_group_T(srcT, srckey, c0, ntok, dst, r0, dkey):
        for t0 in range(0, ntok, 128):
            nr = min(128, ntok - t0)
            b = cnt["ys"] % 2
            cnt["ys"] += 1
            for hf in range(2):
                pb = ps[5 + hf]
                for k4 in range(4):
                    kc = hf * 4 + k4
                    op("pe", lambda e, kc=kc, k4=k4, nr=nr, t0=t0, pb=pb: e.transpose(
                        out=pb[:nr, k4 * 128:(k4 + 1) * 128], in_=srcT[:, kc, c0 + t0:c0 + t0 + nr],
                        identity=identf[:]),
                       reads=[srckey, "identf"], writes=[("ps", 5 + hf)])
                op("act", lambda e, hf=hf, nr=nr, b=b, pb=pb: e.activation(
                    out=yst[:nr, b, hf * 512:(hf + 1) * 512], in_=pb[:nr, :], func=AF.Copy),
                   reads=[("ps", 5 + hf)], writes=[("yst", b)])
            op("sp", lambda e, b=b, nr=nr, t0=t0: e.dma_start(out=dst[r0 + t0:r0 + t0 + nr, :], in_=yst[:nr, b, :]),
               reads=[("yst", b)], dkey=dkey)

    def norm_feat(c0, ntok, gi, dst, dkeyw, dcol0, samp=False):
        pr = ps[4]
        for kc in range(8):
            b = cnt["sq"] % 2
            cnt["sq"] += 1
            op("act", lambda e, kc=kc, b=b: e.activation(out=sqb[:, b, :ntok], in_=xT[:, kc, c0:c0 + ntok], func=AF.Square),
               reads=[("xT", c0)], writes=[("sqb", b)])
            op("pe", lambda e, kc=kc, b=b: e.matmul(pr[:, :ntok], lhsT=onesb[:], rhs=sqb[:, b, :ntok],
                                                    start=(kc == 0), stop=(kc == 7)),
               reads=[("sqb", b), "onesb"], writes=[("ps", 4)])
        op("act", lambda e: e.activation(out=rstd[:, :ntok], in_=pr[:, :ntok], func=AF.Sqrt, scale=1.0 / D, bias=epst[:]),
           reads=[("ps", 4), "epst"], writes=["rstd"])
        op("dve", lambda e: e.reciprocal(out=rstd[:, :ntok], in_=rstd[:, :ntok]), reads=["rstd"], writes=["rstd"])
        for kc in range(8):
            if samp:
                op("dve", lambda e, kc=kc: e.scalar_tensor_tensor(
                    out=dst[:, kc, :, 16:24], in0=xT[:, kc, c0:c0 + 32].rearrange("p (n t) -> p n t", n=4),
                    scalar=gcols[:, gi, kc:kc + 1], in1=rstd[:, :32].rearrange("p (n t) -> p n t", n=4),
                    op0=ALU.mult, op1=ALU.mult),
                   reads=[("xT", c0), "gcols", "rstd"], writes=[dkeyw])
            else:
                op("dve", lambda e, kc=kc: e.scalar_tensor_tensor(
                    out=dst[:, kc, dcol0:dcol0 + ntok], in0=xT[:, kc, c0:c0 + ntok], scalar=gcols[:, gi, kc:kc + 1],
                    in1=rstd[:, :ntok], op0=ALU.mult, op1=ALU.mult),
                   reads=[("xT", c0), "gcols", "rstd"], writes=[dkeyw])

    def u_rows(gi, c0, rcol0):
        for kc in range(8):
            op("dve", lambda e, kc=kc: e.scalar_tensor_tensor(
                out=uf[:, kc, :], in0=xT[:, kc, c0:c0 + 32], scalar=gcols[:, gi, kc:kc + 1],
                in1=rstd[:, rcol0:rcol0 + 32], op0=ALU.mult, op1=ALU.mult),
               reads=[("xT", c0), "gcols", "rstd"], writes=["uf"])
        b = cnt["ys"] % 2
        cnt["ys"] += 1
        for hf in range(2):
            pb = ps[5 + hf]
            for k4 in range(4):
                kc = hf * 4 + k4
                op("pe", lambda e, kc=kc, k4=k4, pb=pb: e.transpose(
                    out=pb[:32, k4 * 128:(k4 + 1) * 128], in_=uf[:, kc, :], identity=identf[:]),
                   reads=["uf", "identf"], writes=[("ps", 5 + hf)])
            op("act", lambda e, hf=hf, b=b, pb=pb: e.activation(
                out=yst[:32, b, hf * 512:(hf + 1) * 512], in_=pb[:32, :], func=AF.Copy),
               reads=[("ps", 5 + hf)], writes=[("yst", b)])
        return b

    def ffn(l, c0, ntok):
        gi = 2 + l
        norm_feat(c0, ntok, gi, hT[:], "hT", 0)
        for jp in range(11):
            wb = cnt["w"] % 2
            cnt["w"] += 1
            op("sp", lambda e, jp=jp, wb=wb: e.dma_start(
                out=wgt[:, wb], in_=wg_b[l, :, jp * 256:(jp + 1) * 256].rearrange("(k p) n -> p k n", p=128)),
               reads=[("wg", l)], writes=[("wgt", wb)], dkey=("wgt", wb))
            op("sp", lambda e, jp=jp, wb=wb: e.dma_start(
                out=wut[:, wb], in_=wu_b[l, :, jp * 256:(jp + 1) * 256].rearrange("(k p) n -> p k n", p=128)),
               reads=[("wu", l)], writes=[("wut", wb)], dkey=("wut", wb))
            for j2 in range(2):
                j = jp * 2 + j2
                pa = cnt["psA"] % 2
                cnt["psA"] += 1
                pG, pU = ps[pa], ps[2 + pa]
                for kc in range(8):
                    op("pe", lambda e, kc=kc, wb=wb, j2=j2, pG=pG: e.matmul(
                        pG[:, :ntok], lhsT=wgt[:, wb, kc, j2 * 128:(j2 + 1) * 128], rhs=hT[:, kc, :ntok],
                        start=(kc == 0), stop=(kc == 7)),
                       reads=[("wgt", wb), "hT"], writes=[("ps", pa)])
                for kc in range(8):
                    op("pe", lambda e, kc=kc, wb=wb, j2=j2, pU=pU: e.matmul(
                        pU[:, :ntok], lhsT=wut[:, wb, kc, j2 * 128:(j2 + 1) * 128], rhs=hT[:, kc, :ntok],
                        start=(kc == 0), stop=(kc == 7)),
                       reads=[("wut", wb), "hT"], writes=[("ps", 2 + pa)])
                sgb = cnt["sg"] % 2
                cnt["sg"] += 1
                op("act", lambda e, sgb=sgb, pG=pG: e.activation(out=sg[:, sgb, :ntok], in_=pG[:, :ntok], func=AF.Silu),
                   reads=[("ps", pa)], writes=[("sg", sgb)])
                op("dve", lambda e, sgb=sgb, pU=pU, j=j: e.tensor_tensor(
                    out=big[:, j, :ntok], in0=pU[:, :ntok], in1=sg[:, sgb, :ntok], op=ALU.mult),
                   reads=[("ps", 2 + pa), ("sg", sgb)], writes=[("big", j)])
        for c in range(8):
            wb = cnt["wd"] % 2
            cnt["wd"] += 1
            op("sp", lambda e, c=c, wb=wb: e.dma_start(
                out=wdt[:, wb], in_=wd_b[l, :, c * 128:(c + 1) * 128].rearrange("(j p) n -> p j n", p=128)),
               reads=[("wd", l)], writes=[("wdt", wb)], dkey=("wdt", wb))
            pd = cnt["psD"] % 2
            cnt["psD"] += 1
            pD = ps[pd]
            for j in range(NJ):
                op("pe", lambda e, j=j, wb=wb, pD=pD: e.matmul(
                    pD[:, :ntok], lhsT=wdt[:, wb, j, :], rhs=big[:, j, :ntok],
                    start=(j == 0), stop=(j == NJ - 1)),
                   reads=[("wdt", wb), ("big", j)], writes=[("ps", pd)])
            op("dve", lambda e, c=c, pD=pD: e.tensor_tensor(
                out=xT[:, c, c0:c0 + ntok], in0=pD[:, :ntok], in1=xT[:, c, c0:c0 + ntok], op=ALU.add),
               reads=[("ps", pd), ("xT", c0)], writes=[("xT", c0)])

    def pool_layer(i, c0, ntok, start_tab, last=False):
        u = uT[:, i]
        norm_feat(c0, ntok, i, u, ("uT", i), 16)
        if last:
            yb = u_rows(i, c0 + ntok - 32, ntok - 32)
            op("sp", lambda e, yb=yb: e.dma_start(out=pool_p[i], in_=yst[17:32, yb, :]), reads=[("yst", yb)], dkey="po")
        for kc in range(8):
            g = kc // 2
            src = None
            nst = g + 1
            for s in range(nst):
                sh = 1 << s
                lo = 2 * sh
                wbuf = s % 2
                if s == 0:
                    op("dve", lambda e, kc=kc: e.tensor_tensor(
                        out=wsum[:, 0, 2:16 + ntok], in0=u[:, kc, 2:16 + ntok], in1=u[:, kc, 1:15 + ntok], op=ALU.add),
                       reads=[("uT", i)], writes=[("wsum", 0)])
                else:
                    op("dve", lambda e, sh=sh, lo=lo, wbuf=wbuf: e.tensor_tensor(
                        out=wsum[:, wbuf, lo:16 + ntok], in0=wsum[:, 1 - wbuf, lo:16 + ntok],
                        in1=wsum[:, 1 - wbuf, lo - sh:16 + ntok - sh], op=ALU.add),
                       reads=[("wsum", 1 - wbuf)], writes=[("wsum", wbuf)])
            wl = (nst - 1) % 2
            op("dve", lambda e, kc=kc, wl=wl, g=g: e.scalar_tensor_tensor(
                out=hT[:, kc, :ntok], in0=wsum[:, wl, 16:16 + ntok], scalar=1.0 / WIN[g], in1=u[:, kc, 16:16 + ntok],
                op0=ALU.mult, op1=ALU.subtract),
               reads=[("wsum", wl), ("uT", i)], writes=["hT"])
            if start_tab is not None:
                op("dve", lambda e, kc=kc, wl=wl, g=g: e.tensor_tensor(
                    out=wsum[:, wl, 0:16], in0=wsum[:, wl, 16:32], in1=icb[:, start_tab, g, :], op=ALU.mult),
                   reads=[("wsum", wl), "icb"], writes=[("wsum", wl)])
                op("dve", lambda e, kc=kc, wl=wl: e.tensor_tensor(
                    out=hT[:, kc, 0:16], in0=wsum[:, wl, 0:16], in1=u[:, kc, 16:32], op=ALU.subtract),
                   reads=[("wsum", wl), ("uT", i)], writes=["hT"])
        pool_mm(i, c0, ntok)
        op("act", lambda e: e.activation(out=u[:, :, 1:16], in_=u[:, :, 1 + ntok:16 + ntok], func=AF.Copy),
           reads=[("uT", i)], writes=[("uT", i)])

    def pool_mm(i, c0, ntok):
        for c in range(8):
            g = c // 2
            pd = cnt["psD"] % 2
            cnt["psD"] += 1
            pD = ps[pd]
            for k in range(2):
                op("pe", lambda e, k=k, g=g, c=c, pD=pD: e.matmul(
                    pD[:, :ntok], lhsT=wp[:, i, g, k, (c % 2) * 128:(c % 2) * 128 + 128], rhs=hT[:, 2 * g + k, :ntok],
                    start=(k == 0), stop=(k == 1)),
                   reads=["wp", "hT"], writes=[("ps", pd)])
            op("dve", lambda e, c=c, pD=pD: e.tensor_tensor(
                out=xT[:, c, c0:c0 + ntok], in0=pD[:, :ntok], in1=xT[:, c, c0:c0 + ntok], op=ALU.add),
               reads=[("ps", pd), ("xT", c0)], writes=[("xT", c0)])


    def pool_layer_sample(i):
        c0 = 2048
        b = cnt["x"] % 2
        cnt["x"] += 1
        for n in range(4):
            op("sp", lambda e, b=b, n=n: e.dma_start(out=yst[n * 15:(n + 1) * 15, b, :], in_=spool[n, i]),
               writes=[("yst", b)], dkey=("yst", b))
        for hf in range(2):
            pb = ps[5 + hf]
            for k4 in range(4):
                kc = hf * 4 + k4
                op("pe", lambda e, b=b, kc=kc, k4=k4, pb=pb: e.transpose(
                    out=pb[:, k4 * 128:k4 * 128 + 60], in_=yst[:60, b, kc * 128:(kc + 1) * 128],
                    identity=identf[:60, :60]),
                   reads=[("yst", b), "identf"], writes=[("ps", 5 + hf)])
            for k4 in range(4):
                kc = hf * 4 + k4
                op("act", lambda e, kc=kc, k4=k4, pb=pb: e.activation(
                    out=usT[:, kc, :, 1:16], in_=pb[:, k4 * 128:k4 * 128 + 60].rearrange("p (n r) -> p n r", n=4),
                    func=AF.Copy),
                   reads=[("ps", 5 + hf)], writes=["usT"])
        norm_feat(c0, 32, i, usT, "usT", 0, samp=True)
        yb = u_rows(i, c0, 0)
        for n in range(4):
            op("sp", lambda e, n=n, yb=yb: e.dma_start(out=pool_s[n, i, 7:15, :], in_=yst[n * 8:(n + 1) * 8, yb, :]),
               reads=[("yst", yb)], dkey="po")
            op("sp", lambda e, n=n: e.dma_start(out=pool_s[n, i, 0:7, :], in_=spool[n, i, 8:15, :]), dkey="po")
        for kc in range(8):
            g = kc // 2
            nst = g + 1
            for s_ in range(nst):
                sh = 1 << s_
                lo = 2 * sh
                wbuf = s_ % 2
                if s_ == 0:
                    op("dve", lambda e, kc=kc: e.tensor_tensor(
                        out=wsS[:, 0, :, 2:24], in0=usT[:, kc, :, 2:24], in1=usT[:, kc, :, 1:23], op=ALU.add),
                       reads=["usT"], writes=[("wsS", 0)])
                else:
                    op("dve", lambda e, sh=sh, lo=lo, wbuf=wbuf: e.tensor_tensor(
                        out=wsS[:, wbuf, :, lo:24], in0=wsS[:, 1 - wbuf, :, lo:24],
                        in1=wsS[:, 1 - wbuf, :, lo - sh:24 - sh], op=ALU.add),
                       reads=[("wsS", 1 - wbuf)], writes=[("wsS", wbuf)])
            wl = (nst - 1) % 2
            op("dve", lambda e, kc=kc, wl=wl, g=g: e.scalar_tensor_tensor(
                out=hT[:, kc, 0:32].rearrange("p (n t) -> p n t", n=4), in0=wsS[:, wl, :, 16:24],
                scalar=1.0 / WIN[g], in1=usT[:, kc, :, 16:24], op0=ALU.mult, op1=ALU.subtract),
               reads=[("wsS", wl), "usT"], writes=["hT"])
        pool_mm(i, c0, 32)

    def setup_gab(si, gsrc_ap):
        op("sp", lambda e: e.dma_start(out=gtmp[:].rearrange("p b d -> p (b d)"),
                                       in_=gsrc_ap.rearrange("b d -> (b d)").partition_broadcast(128)),
           writes=["gtmp"], dkey="gtmp")
        op("dve", lambda e: e.tensor_copy(out=gAB[:, si, :, 0:64], in_=gtmp[:]), reads=["gtmp"], writes=["gAB"])
        op("dve", lambda e: e.tensor_scalar(out=gAB[:, si, :, 64:96], in0=gtmp[:, :, 32:64], scalar1=-1.0, scalar2=None,
                                            op0=ALU.mult), reads=["gtmp"], writes=["gAB"])
        op("dve", lambda e: e.tensor_copy(out=gAB[:, si, :, 96:128], in_=gtmp[:, :, 0:32]), reads=["gtmp"], writes=["gAB"])

    def make_tabs(si, cs_src, r0, ntok):
        cnt["si"] = si
        for t0 in range(0, ntok, 128):
            nr = min(128, ntok - t0)
            tt = t0 // 128
            op("sp", lambda e, t0=t0, nr=nr, tt=tt: e.dma_start(out=cs4[:nr, tt, :], in_=cs_src[r0 + t0:r0 + t0 + nr, :]),
               writes=["cs4"], dkey="cs4")

    def normrope(pk, nr, tt, b, si_unused=None):
        x3 = pk[:nr, :].rearrange("p (h d) -> p h d", h=8)
        kb_ = cnt["kf"] % 2
        cnt["kf"] += 1
        si = cnt["si"]
        op("act", lambda e: e.activation(out=w1[:nr, kb_, :], in_=pk[:nr, :], func=AF.Square),
           reads=[pkkey(pk)], writes=[("w1", kb_)])
        op("dve", lambda e: e.tensor_reduce(out=ss8[:nr, kb_, :], in_=w1[:nr, kb_, :].rearrange("p (h d) -> p h d", h=8),
                                            axis=AX.X, op=ALU.add), reads=[("w1", kb_)], writes=[("ss8", kb_)])
        op("act", lambda e: e.activation(out=ss8[:nr, kb_, :], in_=ss8[:nr, kb_, :], func=AF.Sqrt, scale=1.0 / 64, bias=epst[:nr, :]),
           reads=[("ss8", kb_), "epst"], writes=[("ss8", kb_)])
        op("dve", lambda e: e.reciprocal(out=ss8[:nr, kb_, :], in_=ss8[:nr, kb_, :]), reads=[("ss8", kb_)], writes=[("ss8", kb_)])
        op("dve", lambda e: e.tensor_tensor(out=tabt[:nr, :], in0=cs4[:nr, tt, :], in1=gAB[:nr, si, b, :], op=ALU.mult),
           reads=["cs4", "gAB"], writes=["tabK"])
        tv = kf[:nr, kb_, :].rearrange("p (h d) -> p h d", h=8)
        wv = w1[:nr, kb_, :].rearrange("p (h d) -> p h d", h=8)
        op("dve", lambda e: e.tensor_tensor(out=tv, in0=x3, in1=tabt[:nr, 0:64].unsqueeze(1).to_broadcast([nr, 8, 64]),
                                            op=ALU.mult), reads=[pkkey(pk), "tabK"], writes=[("kf", kb_)])
        op("dve", lambda e: e.tensor_tensor(out=wv[:, :, 0:32], in0=x3[:, :, 32:64],
                                            in1=tabt[:nr, 64:96].unsqueeze(1).to_broadcast([nr, 8, 32]), op=ALU.mult),
           reads=[pkkey(pk), "tabK", ("ss8", kb_)], writes=[("w1", kb_)])
        op("dve", lambda e: e.tensor_tensor(out=wv[:, :, 32:64], in0=x3[:, :, 0:32],
                                            in1=tabt[:nr, 96:128].unsqueeze(1).to_broadcast([nr, 8, 32]), op=ALU.mult),
           reads=[pkkey(pk), "tabK"], writes=[("w1", kb_)])
        op("dve", lambda e: e.tensor_tensor(out=kf[:nr, kb_, :], in0=kf[:nr, kb_, :], in1=w1[:nr, kb_, :], op=ALU.add),
           reads=[("kf", kb_), ("w1", kb_)], writes=[("kf", kb_)])
        op("pool", lambda e: e.tensor_tensor(out=tv, in0=tv,
                                             in1=ss8[:nr, kb_, :].unsqueeze(2).to_broadcast([nr, 8, 64]), op=ALU.mult),
           reads=[("kf", kb_), ("ss8", kb_)], writes=[("kf", kb_)])
        return kb_

    pskeys = {}

    def pkkey(pk):
        return pskeys[id(pk)]

    for i_, p_ in enumerate(ps):
        pskeys[id(p_)] = ("ps", i_)

    def load_wchunk(src2d, col0, key):
        wb = cnt["wk"] % 2
        cnt["wk"] += 1
        t = wgt if wb == 0 else wut
        nm = "wgt" if wb == 0 else "wut"
        view = t[:].rearrange("p b k n -> p (b k n)").rearrange("p (k n) -> p k n", k=8)
        op("sp", lambda e: e.dma_start(out=view, in_=src2d[:, col0:col0 + 512].rearrange("(k p) n -> p k n", p=128)),
           reads=[key], writes=[(nm, 0), (nm, 1)], dkey=(nm, 0))
        return view, [(nm, 0), (nm, 1)]

    def kv_proj(kind, gi_, c0, ntok, r0):
        norm_feat(c0, ntok, 6, hT[:], "hT", 0)
        samp = (kind == "S")
        make_tabs(0, cs_s if samp else cs_e, 0 if samp else r0, ntok)
        if kind == "H" and gi_ < 3:
            chunks = [4, 5, 10, 11]
        else:
            chunks = list(range(12))
        for ck in chunks:
            isk = ck < 6
            b = (ck % 6) // 2
            hh = ck % 2
            wv, wkeys = load_wchunk(wkv_b, ck * 512, "wkv")
            for t0 in range(0, ntok, 128):
                nr = min(128, ntok - t0)
                tt = t0 // 128
                pa = cnt["psK"] % 4
                cnt["psK"] += 1
                pk = ps[pa]
                for kc in range(8):
                    op("pe", lambda e, kc=kc, pk=pk, t0=t0, nr=nr, wv=wv: e.matmul(
                        pk[:nr, :], lhsT=hT[:, kc, t0:t0 + nr], rhs=wv[:, kc, :], start=(kc == 0), stop=(kc == 7)),
                       reads=["hT"] + wkeys, writes=[("ps", pa)])
                if isk:
                    fb = normrope(pk, nr, tt, b)
                else:
                    fb = cnt["kf"] % 2
                    cnt["kf"] += 1
                    op("act", lambda e, fb=fb, pk=pk, nr=nr: e.activation(out=kf[:nr, fb, :], in_=pk[:nr, :], func=AF.Copy),
                       reads=[("ps", pa)], writes=[("kf", fb)])
                sel = 0 if isk else 1
                if kind == "O":
                    pos = gi_ * 512 + t0
                    keep = (128, 512, 2048)[b]
                    if pos >= 2048 - keep:
                        rr = pos - (2048 - keep)
                        op("sp", lambda e, fb=fb, rr=rr, nr=nr, b=b, sel=sel, hh=hh: e.dma_start(
                            out=kvp[b][rr:rr + nr, sel, hh * 512:(hh + 1) * 512], in_=kf[:nr, fb, :]),
                           reads=[("kf", fb)], dkey="kvo")
                if samp:
                    Wd = (128, 512, 2048)[b]
                    for n in range(4):
                        op("sp", lambda e, fb=fb, n=n, b=b, sel=sel, hh=hh, Wd=Wd: e.dma_start(
                            out=kvs[b][n, Wd - 8:Wd, sel, hh * 512:(hh + 1) * 512], in_=kf[n * 8:(n + 1) * 8, fb, :]),
                           reads=[("kf", fb)], dkey="kvo")
                if stage >= 3 and samp:
                    op("act", lambda e, fb=fb, nr=nr: e.activation(out=kb[:nr, fb, :], in_=kf[:nr, fb, :], func=AF.Copy),
                       reads=[("kf", fb)], writes=[("kb", fb)])
                    if isk:
                        for q in range(4):
                            op("pe", lambda e, fb=fb, q=q, nr=nr: e.transpose(
                                out=psT[:, q * 128:q * 128 + nr], in_=kb[:nr, fb, q * 128:(q + 1) * 128],
                                identity=identb[:nr, :nr]),
                               reads=[("kb", fb), "identb"], writes=[("ps", 7)])
                        op("dve", lambda e: e.tensor_copy(
                            out=ktst[:, :, 0:32], in_=psT[:, 0:512].rearrange("p (q t) -> p q t", q=4)[:, :, 0:32]),
                           reads=[("ps", 7)], writes=["ktst"])
                        op("sp", lambda e, b=b, hh=hh: e.dma_start(
                            out=ktn_s[b, hh * 4:hh * 4 + 4].rearrange("q p t -> p q t"), in_=ktst[:, :, 0:32]),
                           reads=["ktst"], writes=[("kts", "S", 0)], dkey=("kts", "S", 0))
                    else:
                        op("sp", lambda e, fb=fb, b=b, hh=hh: e.dma_start(
                            out=vn_s[b, :, hh * 512:(hh + 1) * 512], in_=kb[:32, fb, :]),
                           reads=[("kb", fb)], writes=[("vs", "S", 0)], dkey=("vs", "S", 0))
                if stage >= 3 and not samp:
                    op("act", lambda e, fb=fb, nr=nr: e.activation(out=kb[:nr, fb, :], in_=kf[:nr, fb, :], func=AF.Copy),
                       reads=[("kf", fb)], writes=[("kb", fb)])
                    if isk:
                        for q in range(4):
                            op("pe", lambda e, fb=fb, q=q, nr=nr: e.transpose(
                                out=psT[:, q * 128:q * 128 + nr], in_=kb[:nr, fb, q * 128:(q + 1) * 128],
                                identity=identb[:nr, :nr]),
                               reads=[("kb", fb), "identb"], writes=[("ps", 7)])
                        dil = (1, 4, 16)[b]
                        mt = 128 // dil
                        op("dve", lambda e, dil=dil, mt=mt, tt=tt: e.tensor_copy(
                            out=ktst[:].rearrange("p q (r m) -> p q r m", r=dil)[:, :, :, tt * mt:(tt + 1) * mt],
                            in_=psT[:, 0:512].rearrange("p (q m r) -> p q r m", q=4, r=dil)),
                           reads=[("ps", 7)], writes=["ktst"])
                        if t0 + 128 >= ntok:
                            for q in range(4):
                                hpq = hh * 4 + q
                                if b == 0:
                                    dst = kt0_s[hpq, :, r0:r0 + 512]
                                    src_ = ktst[:, q, :]
                                elif b == 1:
                                    dst = kt1_s[hpq, :, :, r0 // 4:r0 // 4 + 128]
                                    src_ = ktst[:, q, :].rearrange("p (r m) -> p r m", r=4)
                                else:
                                    dst = kt2_s[hpq, :, :, r0 // 16:r0 // 16 + 32]
                                    src_ = ktst[:, q, :].rearrange("p (r m) -> p r m", r=16)
                                op("sp", lambda e, dst=dst, src_=src_: e.dma_start(out=dst, in_=src_),
                                   reads=["ktst"], writes=[("kts", kind, gi_)], dkey=("kts", kind, gi_))
                    else:
                        op("sp", lambda e, fb=fb, nr=nr, b=b, hh=hh, t0=t0: e.dma_start(
                            out=v_s[b, r0 + t0:r0 + t0 + nr, hh * 512:(hh + 1) * 512], in_=kb[:nr, fb, :]),
                           reads=[("kb", fb)], writes=[("vs", kind, gi_)], dkey=("vs", kind, gi_))

    if stage >= 0:
        for b, Wd in enumerate((128, 512, 2048)):
            for n in range(4):
                op("act", lambda e, b=b, n=n, Wd=Wd: e.dma_start(out=kvs[b][n, 0:Wd - 8], in_=cch[b][n, 8:Wd]), dkey="cc")
        setup_gab(0, k_norm)
        def ginfo(kind, gi_):
            if kind == "S":
                return 2048, 0, 32
            return gi_ * 512, gi_ * 512 + (2048 if kind == "O" else 0), 512

        pairs = [[("H", 0), ("H", 1), ("H", 2), ("H", 3)], [("O", 0), ("O", 1), ("O", 2), ("O", 3)], [("S", 0)]]
        for pair in pairs:
            for (kind, gi_) in pair:
                c0, r0, ntok = ginfo(kind, gi_)
                load_group_x(xs if kind == "S" else xe, r0, ntok, c0)
            for i in range(2):
                for (kind, gi_) in pair:
                    c0, r0, ntok = ginfo(kind, gi_)
                    st = None
                    if kind == "H" and gi_ == 0:
                        st = 0
                    if kind == "O" and gi_ == 0:
                        st = 1
                    if kind == "S":
                        pool_layer_sample(i)
                    else:
                        pool_layer(i, c0, 512, st, last=(kind == "O" and gi_ == 3))
                for (kind, gi_) in pair:
                    c0, r0, ntok = ginfo(kind, gi_)
                    ffn(i, c0, ntok)
            for (kind, gi_) in pair:
                c0, r0, ntok = ginfo(kind, gi_)
                if stage >= 2:
                    kv_proj(kind, gi_, c0, ntok, r0)
                if stage < 3:
                    if kind == "O":
                        store_group_T(xT, ("xT", c0), c0, 512, y_own, gi_ * 512, "y")
                    if kind == "S":
                        store_group_T(xT, ("xT", c0), c0, 32, y_s, 0, "y")

    import os
    if stage >= 3 or stage == -1:
        OLD = [("uT", 0), ("uT", 1), ("wsum", 0), ("wsum", 1), "icb", "wpf", "wp", "usT", ("wsS", 0), ("wsS", 1), "uf"]
        esA.close()
        KT0 = sb("KT0", [128, 640], BF16)
        KT1 = sb("KT1", [128, 4, 256], BF16)
        KT2 = sb("KT2", [128, 16, 256], BF16)
        V0 = sb("V0", [128, 5, 128], BF16)
        V1 = sb("V1", [128, 2, 4, 128], BF16)
        V2p = sb("V2p", [128, 16, 128], BF16)
        V2c = sb("V2c", [128, 16, 128], BF16)
        mkb = sb("mkb", [128, 11, 512], BF16)
        ptb = sb("ptb", [128, 2, 2, 512], BF16)
        hbt = sb("hbt", [128, 2], F32)
        pending = {"KT0", "KT1", "KT2", "V0", "V1", "V2p", "V2c", "mkb", ("ptb", 0), ("ptb", 1), "hbt", "mks", "mkn"}
        _op0 = op

        def op(eng, fn, reads=(), writes=(), dkey=None):
            writes = list(writes)
            hit = [w for w in writes if w in pending]
            if hit:
                for w in hit:
                    pending.discard(w)
                writes = writes + OLD
            return _op0(eng, fn, reads, writes, dkey)

        op("pool", lambda e: e.dma_start(out=mkb[:], in_=mkall), writes=["mkb"], dkey="mkb")
        op("sp", lambda e: e.dma_start(out=hbt[:, 0:1], in_=hb), writes=["hbt"], dkey="hbt")
        op("dve", lambda e: e.memset(hbt[:, 1:2], 0.0), writes=["hbt"])
        SCR = [("kts", k_, g_) for k_ in ("H", "O") for g_ in range(4)] + [("vs", k_, g_) for k_ in ("H", "O") for g_ in range(4)]

        def q_proj(kind, jb, c0, ntok, r0):
            norm_feat(c0, ntok, 7 + jb, hT[:], "hT", 0)
            samp = (kind == "S")
            make_tabs(1 + jb, cs_s if samp else cs_e, 0 if samp else r0, ntok)
            for ck in range(6):
                b = ck // 2
                hh = ck % 2
                wv, wkeys = load_wchunk(wq_b[jb], ck * 512, ("wq", jb))
                for t0 in range(0, ntok, 128):
                    nr = min(128, ntok - t0)
                    tt = t0 // 128
                    pa = cnt["psK"] % 4
                    cnt["psK"] += 1
                    pk = ps[pa]
                    for kc in range(8):
                        op("pe", lambda e, kc=kc, pk=pk, t0=t0, nr=nr, wv=wv: e.matmul(
                            pk[:nr, :], lhsT=hT[:, kc, t0:t0 + nr], rhs=wv[:, kc, :], start=(kc == 0), stop=(kc == 7)),
                           reads=["hT"] + wkeys, writes=[("ps", pa)])
                    fb = normrope(pk, nr, tt, b)
                    op("act", lambda e, fb=fb, nr=nr: e.activation(out=kb[:nr, fb, :], in_=kf[:nr, fb, :], func=AF.Copy),
                       reads=[("kf", fb)], writes=[("kb", fb)])
                    for q in range(4):
                        op("pe", lambda e, fb=fb, q=q, nr=nr: e.transpose(
                            out=psT[:, q * 128:q * 128 + nr], in_=kb[:nr, fb, q * 128:(q + 1) * 128],
                            identity=identb[:nr, :nr]),
                           reads=[("kb", fb), "identb"], writes=[("ps", 7)])
                    i0 = b * 8 + hh * 4
                    op("dve", lambda e, i0=i0, nr=nr, t0=t0: e.tensor_copy(
                        out=big[:, i0:i0 + 4, t0:t0 + nr], in_=psT[:, 0:512].rearrange("p (q t) -> p q t", q=4)[:, :, :nr]),
                       reads=[("ps", 7)], writes=[("big", i0 + q_) for q_ in range(4)])

        def attention(i):
            E0 = 2048 + 512 * i
            KA = os.environ.get("KA", "012nqmep")
            for hp in range(int(os.environ.get("KB_NHP", "8"))):
                hc = slice(hp * 128, (hp + 1) * 128)
                ld = lambda out, in_, key: op("sp", lambda e: e.dma_start(out=out, in_=in_), reads=SCR, writes=[key], dkey=key)
                ld(KT0[:], kt0_s[hp, :, E0 - 128:E0 + 512], "KT0")
                ld(V0[:], v_s[0, E0 - 128:E0 + 512, hc].rearrange("(k p) c -> p k c", p=128), "V0")
                ld(KT1[:], kt1_s[hp, :, :, (E0 - 512) // 4:(E0 + 512) // 4], "KT1")
                for a_ in range(2):
                    ld(V1[:, a_], v_s[1, E0 - 512 + 512 * a_:E0 + 512 * a_, hc].rearrange("(m r) c -> m r c", r=4), "V1")
                ld2 = lambda out, in_, key: op("sp", lambda e: e.dma_start(out=out, in_=in_), reads=SCR,
                                               writes=[key, (key, 0), (key, 1)], dkey=key)
                ld2(KT2[:], kt2_s[hp], "KT2")
                ld2(V2p[:], v_s[2, 0:2048, hc].rearrange("(m r) c -> m r c", r=16), "V2p")
                ld2(V2c[:], v_s[2, 2048:4096, hc].rearrange("(m r) c -> m r c", r=16), "V2c")
                Ub, Zb = 4 + 2 * (hp % 2), 5 + 2 * (hp % 2)
                U, Z = ps[Ub], ps[Zb]
                firstUZ = [True]
                batches = []
                hbias0 = 0 if i == 0 else 1
                batches.append((hbias0, 0, 128, 256, [(lambda h: KT0[h * 64:(h + 1) * 64, 0:128], "KT0",
                                                        lambda h: V0[:, 0, h * 64:(h + 1) * 64], "V0", 0, slice(0, 128), 128, 128)]))
                for kbj in range(3):
                    batches.append((1, 0, 0, 256, [(lambda h, kbj=kbj: KT0[h * 64:(h + 1) * 64, 128 * (kbj + 1):128 * (kbj + 2)], "KT0",
                                                    lambda h, kbj=kbj: V0[:, kbj + 1, h * 64:(h + 1) * 64], "V0", 0,
                                                    slice(128 * kbj, 128 * kbj + 256), 0, 256)]))
                batches.append((1, 0, 0, 128, [(lambda h: KT0[h * 64:(h + 1) * 64, 512:640], "KT0",
                                                lambda h: V0[:, 4, h * 64:(h + 1) * 64], "V0", 0, slice(384, 512), 0, 128)]))
                for a in range(2):
                    items = []
                    for r in range(4):
                        items.append((lambda h, a=a, r=r: KT1[h * 64:(h + 1) * 64, r, a * 128:(a + 1) * 128], "KT1",
                                      lambda h, a=a, r=r: V1[:, a, r, h * 64:(h + 1) * 64], "V1", 1,
                                      slice(r, 512, 4), r * 128, 128))
                    batches.append(((hbias0 if a == 0 else 1), 1 + a, 0, 512, items))
                for a in range(2):
                    items = []
                    for r in range(16):
                        vt = V2p if a == 0 else V2c
                        items.append((lambda h, r=r, a=a: KT2[h * 64:(h + 1) * 64, r, a * 128:(a + 1) * 128], "KT2",
                                      lambda h, r=r, vt=vt: vt[:, r, h * 64:(h + 1) * 64], ("V2p" if a == 0 else "V2c"), 2,
                                      slice(r, 512, 16), r * 32, 32))
                    batches.append(((0 if a == 0 else 1), (3 + i if a == 0 else 7 + i), 0, 512, items))
                batches = [bt for bt in batches if str(bt[4][0][4]) in KA]
                def emit_qk(bt):
                    (bsel, midx, c_lo, c_hi, items) = bt
                    sp_ = cnt["sp"] % 2
                    cnt["sp"] += 1
                    Sh = [ps[2 * sp_], ps[2 * sp_ + 1]]
                    skeys = [("ps", 2 * sp_), ("ps", 2 * sp_ + 1)]
                    fs = [True, True]
                    for (ktf, ktk, vf, vk, b, qs, scol, N) in items:
                        for h in range(2):
                            op("pe", lambda e, h=h, ktf=ktf, b=b, qs=qs, scol=scol, N=N, st=fs[h], Sh=Sh, hp=hp: e.matmul(
                                Sh[h][:, scol:scol + N], lhsT=ktf(h), rhs=big[h * 64:(h + 1) * 64, b * 8 + hp, qs],
                                start=st, stop=False, skip_group_check=True),
                               reads=[ktk, ("big", b * 8 + hp)], writes=[skeys[h]])
                            fs[h] = False
                    for h in range(2):
                        op("pe", lambda e, h=h, Sh=Sh, midx=midx, c_lo=c_lo, c_hi=c_hi: e.matmul(
                            Sh[h][:, c_lo:c_hi], lhsT=identb[:, :], rhs=mkb[:, midx, c_lo:c_hi], start=False, stop=True,
                            skip_group_check=True),
                           reads=["identb", "mkb"], writes=[skeys[h]])
                    pb_ = cnt["pt"] % 2
                    cnt["pt"] += 1
                    op("act", lambda e, sp_=sp_, pb_=pb_, bsel=bsel, c_lo=c_lo, c_hi=c_hi: e.activation(
                        out=ptb[:, pb_, :, c_lo:c_hi], in_=psS[sp_][:, :, c_lo:c_hi], func=AF.Exp, scale=0.125,
                        bias=hbt[:, bsel:bsel + 1]),
                       reads=skeys + ["hbt"], writes=[("ptb", pb_)])
                    return pb_

                def emit_pv(bt, pb_):
                    (bsel, midx, c_lo, c_hi, items) = bt
                    for (ktf, ktk, vf, vk, b, qs, scol, N) in items:
                        for h in range(2):
                            f1 = firstUZ[0]
                            op("pe", lambda e, h=h, vf=vf, qs=qs, scol=scol, N=N, f1=f1, pb_=pb_, U=U: e.matmul(
                                U[h * 64:(h + 1) * 64, qs], lhsT=vf(h), rhs=ptb[:, pb_, h, scol:scol + N],
                                start=f1, stop=False, skip_group_check=True, tile_position=(0, h * 64)),
                               reads=[vk, ("ptb", pb_)], writes=[("ps", Ub)])
                            op("pe", lambda e, h=h, qs=qs, scol=scol, N=N, f1=f1, pb_=pb_, Z=Z: e.matmul(
                                Z[h * 64:(h + 1) * 64, qs], lhsT=onesb[:, 0:64], rhs=ptb[:, pb_, h, scol:scol + N],
                                start=f1, stop=False, skip_group_check=True, tile_position=(0, h * 64)),
                               reads=["onesb", ("ptb", pb_)], writes=[("ps", Zb)])
                        firstUZ[0] = False

                pbs = {}
                if batches:
                    pbs[0] = emit_qk(batches[0])
                for n_ in range(len(batches)):
                    if n_ + 1 < len(batches):
                        pbs[n_ + 1] = emit_qk(batches[n_ + 1])
                    emit_pv(batches[n_], pbs[n_])
                if "n" not in KA:
                    continue
                op("dve", lambda e, Z=Z: e.reciprocal(out=rstd[:, :], in_=Z[:, :]), reads=[("ps", Zb)], writes=["rstd"])
                op("dve", lambda e, U=U, hp=hp: e.tensor_tensor(out=hT[:, hp, :], in0=U[:, :], in1=rstd[:, :], op=ALU.mult),
                   reads=[("ps", Ub), "rstd"], writes=["hT"])
                if stage == -1:
                    op("dve", lambda e, U=U: e.tensor_copy(out=kf[:, 0, :], in_=U[:, :]), reads=[("ps", Ub)], writes=[("kf", 0)])
                    op("dve", lambda e, Z=Z: e.tensor_copy(out=kf[:, 1, :], in_=Z[:, :]), reads=[("ps", Zb)], writes=[("kf", 1)])
                    op("sp", lambda e, hp=hp: e.dma_start(out=t_u[:, hp, :], in_=kf[:, 0, :]), reads=[("kf", 0)], dkey="tu")
                    op("sp", lambda e, hp=hp: e.dma_start(out=t_z[:, hp, :], in_=kf[:, 1, :]), reads=[("kf", 1)], dkey="tu")


        mks = sb("mks_t", [128, 80], BF16)
        mkn = sb("mkn_t", [32, 12, 64], BF16)
        op("pool", lambda e: e.dma_start(out=mks[:], in_=mks_in), writes=["mks"], dkey="mks")
        op("pool", lambda e: e.dma_start(out=mkn[:], in_=mkn_in), writes=["mkn"], dkey="mks")

        def sample_attention():
            ksrs = [V2p[:].rearrange("p r c -> p (r c)")[:, k_ * 1024:(k_ + 1) * 1024] for k_ in range(2)]
            vsrs = [V2c[:].rearrange("p r c -> p (r c)")[:, k_ * 1024:(k_ + 1) * 1024] for k_ in range(2)]
            KTss = [KT2[:].rearrange("p r m -> p (r m)")[:, k_ * 1024:(k_ + 1) * 1024].rearrange("p (q k) -> p q k", q=8)
                    for k_ in range(2)]
            bcnt = [0]
            KTn = V0[:].rearrange("p k c -> p (k c)")[:, 0:256].rearrange("p (q t) -> p q t", q=8)
            Vn = V1[:].rearrange("p a r c -> p (a r c)")
            Us, Zs = ps[4], ps[5]
            firstUZ = [True]
            SK = [("kts", "S", 0), ("vs", "S", 0)]

            def block(M, qk_items, mask_ap, ncols, vtile, vkey):
                sp_ = cnt["sp"] % 2
                cnt["sp"] += 1
                Sh = [ps[2 * sp_], ps[2 * sp_ + 1]]
                skeys = [("ps", 2 * sp_), ("ps", 2 * sp_ + 1)]
                fs = [True, True]
                for (lf, lk, rf, scol, N, oc) in qk_items:
                    for h in range(2):
                        op("pe", lambda e, h=h, lf=lf, rf=rf, scol=scol, N=N, st=fs[h], Sh=Sh: e.matmul(
                            Sh[h][:M, scol:scol + N], lhsT=lf(h), rhs=rf(h), start=st, stop=False, skip_group_check=True),
                           reads=[lk] + [("big", j_) for j_ in range(24)], writes=[skeys[h]])
                        fs[h] = False
                if mask_ap is not None:
                    for h in range(2):
                        op("pe", lambda e, h=h, Sh=Sh: e.matmul(
                            Sh[h][:M, 0:ncols], lhsT=identb[:M, :M], rhs=mask_ap, start=False, stop=True, skip_group_check=True),
                           reads=["identb", "mks", "mkn"], writes=[skeys[h]])
                pb_ = cnt["pt"] % 2
                cnt["pt"] += 1
                op("act", lambda e, sp_=sp_, pb_=pb_: e.activation(
                    out=ptb[:M, pb_, :, 0:ncols], in_=psS[sp_][:M, :, 0:ncols], func=AF.Exp, scale=0.125),
                   reads=skeys, writes=[("ptb", pb_)])
                for (lf, lk, rf, scol, N, oc) in qk_items:
                    hpq = oc[0]
                    for h in range(2):
                        f1 = firstUZ[0]
                        op("pe", lambda e, h=h, scol=scol, N=N, oc=oc, f1=f1, pb_=pb_, hpq=hpq: e.matmul(
                            Us[h * 64:(h + 1) * 64, oc[1]], lhsT=vtile[:M, hpq * 128 + h * 64:hpq * 128 + h * 64 + 64],
                            rhs=ptb[:M, pb_, h, scol:scol + N], start=f1, stop=False, skip_group_check=True,
                            tile_position=(0, h * 64)),
                           reads=[vkey, ("ptb", pb_)], writes=[("ps", 4)])
                        op("pe", lambda e, h=h, scol=scol, N=N, oc=oc, f1=f1, pb_=pb_: e.matmul(
                            Zs[h * 64:(h + 1) * 64, oc[1]], lhsT=onesb[:M, 0:64],
                            rhs=ptb[:M, pb_, h, scol:scol + N], start=f1, stop=False, skip_group_check=True,
                            tile_position=(0, h * 64)),
                           reads=["onesb", ("ptb", pb_)], writes=[("ps", 5)])
                    firstUZ[0] = False

            for b, (Wd, dil) in enumerate(((128, 1), (512, 4), (2048, 16))):
                op("sp", lambda e, b=b: e.dma_start(out=KTn, in_=ktn_s[b].rearrange("q p t -> p q t")),
                   reads=SK, writes=["V0"], dkey="V0")
                op("sp", lambda e, b=b: e.dma_start(out=Vn[:32, :], in_=vn_s[b]), reads=SK, writes=["V1"], dkey="V1")
                for n in range(4):
                    items = []
                    for hp in range(8):
                        items.append((lambda h, hp=hp: KTn[h * 64:(h + 1) * 64, hp, :], "V0",
                                      lambda h, hp=hp, b=b, n=n: big[h * 64:(h + 1) * 64, b * 8 + hp, n * 8:n * 8 + 8],
                                      hp * 8, 8, (hp, slice(hp * 32 + n * 8, hp * 32 + n * 8 + 8))))
                    block(32, items, mkn[:32, n * 3 + b, :], 64, Vn, "V1")
                    ncls = min(dil, 8)
                    nq = 8 // ncls if dil <= 8 else 1
                    for r in range(ncls):
                        kk_ = bcnt[0] % 2
                        bcnt[0] += 1
                        ksr, vsr, KTs = ksrs[kk_], vsrs[kk_], KTss[kk_]
                        kK, kV, kT = ("V2p", kk_), ("V2c", kk_), ("KT2", kk_)
                        op("pool", lambda e, b=b, n=n, r=r, dil=dil, Wd=Wd, ksr=ksr: e.dma_start(
                            out=ksr, in_=cch[b][n, r:Wd:dil, 0, :]), writes=["V2p", kK], dkey=kK)
                        op("pool", lambda e, b=b, n=n, r=r, dil=dil, Wd=Wd, vsr=vsr: e.dma_start(
                            out=vsr, in_=cch[b][n, r:Wd:dil, 1, :]), writes=["V2c", kV], dkey=kV)
                        for hp in range(8):
                            op("pe", lambda e, hp=hp, ksr=ksr: e.transpose(
                                out=psT[:, hp * 128:(hp + 1) * 128], in_=ksr[:, hp * 128:(hp + 1) * 128], identity=identb[:]),
                               reads=[kK, "identb"], writes=[("ps", 7)])
                        op("dve", lambda e, KTs=KTs: e.tensor_copy(out=KTs, in_=psT[:].rearrange("p (q k) -> p q k", q=8)),
                           reads=[("ps", 7)], writes=["KT2", kT])
                        items = []
                        for hp in range(8):
                            if dil == 1:
                                qsl = slice(n * 8, n * 8 + 8)
                            elif dil == 4:
                                qsl = slice(n * 8 + r, n * 8 + 8, 4)
                            else:
                                qsl = slice(n * 8 + r, n * 8 + r + 1)
                            osl = slice(hp * 32 + qsl.start, hp * 32 + qsl.stop, qsl.step)
                            items.append((lambda h, hp=hp, KTs=KTs: KTs[h * 64:(h + 1) * 64, hp, :], kT,
                                          lambda h, hp=hp, b=b, qsl=qsl: big[h * 64:(h + 1) * 64, b * 8 + hp, qsl],
                                          hp * nq, nq, (hp, osl)))
                        mask_ap = None
                        if dil == 1:
                            mask_ap = mks[:, 0:64]
                        elif dil == 4:
                            mask_ap = mks[:, 64:80]
                        block(128, items, mask_ap, 8 * nq, vsr, kV)
            op("dve", lambda e: e.reciprocal(out=rstd[:, 0:256], in_=Zs[:, 0:256]), reads=[("ps", 5)], writes=["rstd"])
            op("dve", lambda e: e.tensor_tensor(
                out=hT[:, :, 0:32], in0=Us[:, 0:256].rearrange("p (q t) -> p q t", q=8),
                in1=rstd[:, 0:256].rearrange("p (q t) -> p q t", q=8), op=ALU.mult),
               reads=[("ps", 4), "rstd"], writes=["hT"])

        def o_proj(jb, c0, ntok):
            for c in range(8):
                wb = cnt["wd"] % 2
                cnt["wd"] += 1
                op("sp", lambda e, c=c, wb=wb: e.dma_start(
                    out=wdt[:, wb, 0:8, :], in_=wo_b[jb][:, c * 128:(c + 1) * 128].rearrange("(k p) n -> p k n", p=128)),
                   reads=[("wo", jb)], writes=[("wdt", wb)], dkey=("wdt", wb))
                pd = cnt["psD"] % 2
                cnt["psD"] += 1
                pD = ps[pd]
                for k in range(8):
                    op("pe", lambda e, k=k, wb=wb, pD=pD: e.matmul(
                        pD[:, :ntok], lhsT=wdt[:, wb, k, :], rhs=hT[:, k, :ntok], start=(k == 0), stop=(k == 7)),
                       reads=[("wdt", wb), "hT"], writes=[("ps", pd)])
                op("dve", lambda e, c=c, pD=pD: e.tensor_tensor(
                    out=xT[:, c, c0:c0 + ntok], in0=pD[:, :ntok], in1=xT[:, c, c0:c0 + ntok], op=ALU.add),
                   reads=[("ps", pd), ("xT", c0)], writes=[("xT", c0)])

        if stage == -1:
            t_q = din("t_q", [128, 24, 512])
            t_kt0 = din("t_kt0", [8, 128, 4096])
            t_kt1 = din("t_kt1", [8, 128, 4, 1024])
            t_kt2 = din("t_kt2", [8, 128, 16, 256])
            t_v = din("t_v", [3, 4096, D])
            t_out = dout("t_out", [128, 8, 512])
            t_u = dout("t_u", [128, 8, 512])
            t_z = dout("t_z", [128, 8, 512])
            for hp_ in range(8):
                op("pool", lambda e, hp_=hp_: e.dma_start(out=kt0_s[hp_], in_=t_kt0[hp_]), writes=[("kts", "H", 0)], dkey=("kts", "H", 0))
                op("pool", lambda e, hp_=hp_: e.dma_start(out=kt1_s[hp_], in_=t_kt1[hp_]), writes=[("kts", "H", 1)], dkey=("kts", "H", 1))
                op("pool", lambda e, hp_=hp_: e.dma_start(out=kt2_s[hp_], in_=t_kt2[hp_]), writes=[("kts", "H", 2)], dkey=("kts", "H", 2))
            for b_ in range(3):
                op("pool", lambda e, b_=b_: e.dma_start(out=v_s[b_], in_=t_v[b_]), writes=[("vs", "H", b_)], dkey=("vs", "H", b_))
            op("pool", lambda e: e.dma_start(out=big[:], in_=t_q), writes=[("big", j_) for j_ in range(24)], dkey="tq")
            attention(int(os.environ.get("KT_I", "0")))
            for hp_ in range(8):
                op("act", lambda e, hp_=hp_: e.activation(out=yst[:, 0, 0:512], in_=hT[:, hp_, :], func=AF.Copy),
                   reads=["hT"], writes=[("yst", 0)])
                op("sp", lambda e, hp_=hp_: e.dma_start(out=t_out[:, hp_, :], in_=yst[:, 0, 0:512]), reads=[("yst", 0)], dkey="y")
        KB = os.environ.get("KB", "qaofs") if stage >= 3 else ""
        NL0 = 2 if stage >= 3 else 0
        NG = int(os.environ.get("KB_NG", "4"))
        NL = min(NL0, int(os.environ.get("KB_NL", "2")))
        for jb in range(NL):
            setup_gab(1 + jb, q_norm[jb])
            for i in range(NG):
                c0 = 512 * i
                if "q" in KB:
                    q_proj("O", jb, c0, 512, 2048 + 512 * i)
                if "a" in KB:
                    attention(i)
                if "o" in KB:
                    o_proj(jb, c0, 512)
                if "f" in KB:
                    ffn(2 + jb, c0, 512)
                if jb == 1:
                    store_group_T(xT, ("xT", c0), c0, 512, y_own, i * 512, "y")
            if "s" in os.environ.get("KB", "qaofs"):
                q_proj("S", jb, 2048, 32, 0)
                sample_attention()
                o_proj(jb, 2048, 32)
            ffn(2 + jb, 2048, 32)
        if stage >= 3:
            store_group_T(xT, ("xT", 2048), 2048, 32, y_s, 0, "y")

    P.resolve()
    global _LASTP
    _LASTP = P
    sems = {}
    for n_, k in enumerate(P.sem_keys):
        sems[k] = es.enter_context(nc.semaphore("s%d" % n_))
    print("n_sems", len(sems), "n_ops", {e: len(P.ops[e]) for e in ENGS})
    P.emit(sems)
    es.close()
    return nc


_CACHE = {}


def _rope_tab(pos):
    inv = np.power(np.float32(10000.0), -np.arange(0, 64, 2, dtype=np.float32) / np.float32(64)).astype(np.float32)
    ang = pos.astype(np.float32)[:, None] * inv[None, :]
    c = np.cos(ang).astype(np.float32)
    s = np.sin(ang).astype(np.float32)
    return np.concatenate([c, c, s, s], axis=1).astype(np.float32)


def kernel(**inp):
    import os
    stage = int(os.environ.get("KSTAGE", "3"))
    if stage not in _CACHE:
        _CACHE[stage] = build(stage)
    nc = _CACHE[stage]
    f = lambda a: np.ascontiguousarray(np.asarray(a, dtype=np.float32))
    x_prompt = f(inp["x_prompt"])
    x_sample = f(inp["x_sample"])
    state_pool = f(inp["state_pool"])
    caches = [f(inp["cache_kv_w128"]), f(inp["cache_kv_w512"]), f(inp["cache_kv_w2048"])]
    wnames = ["a_norm", "pool_w", "pool_scale", "kv_norm", "w_kv", "k_norm", "b_norm", "w_q", "q_norm", "w_o",
              "ffn_norm", "w_gate", "w_up", "w_down"]
    W = {n: f(inp[n]) for n in wnames}
    kk = np.arange(128)
    cur = np.where(kk[:, None] <= kk[None, :], 0.0, NEG).astype(np.float32)
    prev = np.where(kk[:, None] >= kk[None, :], 0.0, NEG).astype(np.float32)
    mkall = np.zeros((128, 11, 512), np.float32)
    mkall[:, 0] = np.concatenate([cur, prev, cur, prev], axis=1)
    mkall[:, 1] = np.tile(prev, (1, 4))
    mkall[:, 2] = np.tile(cur, (1, 4))
    for i_ in range(4):
        mkall[:, 3 + i_] = np.tile(prev[:, 32 * i_:32 * i_ + 32], (1, 16))
        mkall[:, 7 + i_] = np.tile(cur[:, 32 * i_:32 * i_ + 32], (1, 16))
    mks_h = np.concatenate([np.tile(prev[:, 0:8], (1, 8)), np.tile(prev[:, 0:2], (1, 8))], axis=1).astype(np.float32)
    mkn_h = np.full((32, 12, 64), NEG, np.float32)
    for n_ in range(4):
        for b_, dil_ in enumerate((1, 4, 16)):
            for t_ in range(8):
                for tp_ in range(8):
                    if tp_ <= t_ and (t_ - tp_) % dil_ == 0:
                        mkn_h[n_ * 8 + tp_, n_ * 3 + b_, t_::8] = 0.0
    in_maps = []
    for c in range(8):
        b, half = c // 2, c % 2
        m = dict(W)
        if half == 1:
            m["xe"] = x_prompt[b]
            pos = np.arange(4096)
        else:
            m["xe"] = np.concatenate([np.zeros((2048, D), np.float32), x_prompt[b, :2048]], axis=0)
            pos = np.arange(4096) - 2048
        m["cs_e"] = _rope_tab(np.maximum(pos, 0))
        m["cs_s"] = np.tile(_rope_tab(8192 + np.arange(8)), (4, 1))
        m["xs"] = x_sample[4 * c:4 * c + 4].reshape(32, D)
        m["spool"] = state_pool[4 * c:4 * c + 4]
        m["c128"] = caches[0][4 * c:4 * c + 4].reshape(4, 128, 2, D)
        m["c512"] = caches[1][4 * c:4 * c + 4].reshape(4, 512, 2, D)
        m["c2048"] = caches[2][4 * c:4 * c + 4].reshape(4, 2048, 2, D)
        icv = np.zeros((2, 4, 16), np.float32)
        for g, w in enumerate(WIN):
            real = 1.0 / np.minimum(np.arange(16) + 1, w).astype(np.float32)
            plain = np.full(16, 1.0 / w, np.float32)
            icv[0, g] = real if half == 1 else plain
            icv[1, g] = real if half == 0 else plain
        m["ic"] = icv
        m["hb"] = np.full((128, 1), 0.0 if half == 1 else NEG, np.float32)
        m["mkall"] = mkall
        m["mks"] = mks_h
        m["mkn"] = mkn_h
        in_maps.append(m)
    res = run_bass_kernel_spmd(nc, in_maps, core_ids=list(range(8)))
    R = res.results
    y_prompt = np.zeros((4, 4096, D), np.float32)
    for c in range(8):
        y_prompt[c // 2, (c % 2) * 2048:(c % 2) * 2048 + 2048] = R[c]["y_own"]
    y_sample = np.concatenate([R[c]["y_s"].reshape(4, 8, D) for c in range(8)], axis=0)
    pool_prompt = np.stack([R[2 * b + 1]["pool_p"] for b in range(4)], axis=0)
    pool_sample = np.concatenate([R[c]["pool_s"] for c in range(8)], axis=0)
    outs = [y_prompt, y_sample, pool_prompt, pool_sample]
    for g, w in enumerate((128, 512, 2048)):
        outs.append(np.stack([R[2 * b + 1]["kvp%d" % w].reshape(w, 2, 16, 64) for b in range(4)], axis=0))
        outs.append(np.concatenate([R[c]["kvs%d" % w].reshape(4, w, 2, 16, 64) for c in range(8)], axis=0))
    return tuple(outs)
```

```python
import numpy as np
from contextlib import ExitStack
import concourse.bass as bass
import concourse.mybir as mybir
from concourse.bass_utils import run_bass_kernel_spmd

F32 = mybir.dt.float32
BF16 = mybir.dt.bfloat16
AF = mybir.ActivationFunctionType
ALU = mybir.AluOpType
AX = mybir.AxisListType

ENGS = ("pe", "act", "dve", "pool", "sp")


class _Op:
    __slots__ = ("eng", "fn", "reads", "writes", "dkey", "waits", "tok", "need_inc", "idx")


class Prog:
    def __init__(self, nc):
        self.nc = nc
        self.ops = {e: [] for e in ENGS}
        self.all_ops = []
        self.last_w = {}
        self.readers = {}

    def op(self, eng, fn, reads=(), writes=(), dkey=None):
        o = _Op()
        o.eng = eng
        o.fn = fn
        o.dkey = dkey
        o.need_inc = dkey is not None
        o.tok = None
        deps = []
        for r in reads:
            w = self.last_w.get(r)
            if w is not None:
                deps.append((w, "raw"))
        for r in writes:
            w = self.last_w.get(r)
            if w is not None:
                deps.append((w, "waw"))
            rd = self.readers.get(r)
            if rd is not None:
                for x in rd[0].values():
                    deps.append((x, "war"))
                for x in rd[1]:
                    deps.append((x, "war"))
        o.waits = deps
        for r in reads:
            rd = self.readers.get(r)
            if rd is None:
                rd = self.readers[r] = ({}, [])
            if dkey is None:
                rd[0][eng] = o
            else:
                rd[1].append(o)
        for r in writes:
            self.last_w[r] = o
            self.readers[r] = ({}, [])
        self.ops[eng].append(o)
        self.all_ops.append(o)
        return o

    def resolve(self):
        for o in self.all_ops:
            real = []
            for (d, kind) in o.waits:
                if d is o:
                    continue
                if d.dkey is None and o.dkey is None and d.eng == o.eng:
                    if o.eng == "pe" or kind != "raw":
                        continue
                real.append(d)
            o.waits = real
            for d in real:
                d.need_inc = True
        cnt = {e: 0 for e in ENGS}
        gen = {e: 0 for e in ENGS}
        dcnt = {}
        for o in self.all_ops:
            if not o.need_inc:
                continue
            if o.dkey is not None:
                k = ("d", o.dkey)
                dcnt[k] = dcnt.get(k, 0) + 16
                o.tok = (k, dcnt[k])
            else:
                e = o.eng
                if cnt[e] >= 12000:
                    gen[e] += 1
                    cnt[e] = 0
                cnt[e] += 1
                o.tok = (("e", e, gen[e]), cnt[e])
        self.sem_keys = []
        self.final = {}
        for o in self.all_ops:
            if o.tok is not None:
                if o.tok[0] not in self.final:
                    self.sem_keys.append(o.tok[0])
                self.final[o.tok[0]] = max(self.final.get(o.tok[0], 0), o.tok[1])

    def emit(self, sems):
        nc = self.nc
        engmap = {"pe": "tensor", "act": "scalar", "dve": "vector", "pool": "gpsimd", "sp": "sync"}
        with nc.Block() as block:
            for e in ENGS:
                def body(eng, ops=self.ops[e], e=e):
                    seen = {}
                    for o in ops:
                        need = {}
                        for d in o.waits:
                            k, v = d.tok
                            if seen.get(k, 0) >= v:
                                continue
                            if need.get(k, 0) < v:
                                need[k] = v
                        for k, v in need.items():
                            eng.wait_ge(sems[k], v)
                            seen[k] = v
                        ins = o.fn(eng)
                        if o.tok is not None:
                            ins.then_inc(sems[o.tok[0]], 16 if o.dkey is not None else 1)
                    if e == "sp":
                        for k, v in self.final.items():
                            if k[0] == "d":
                                eng.wait_ge(sems[k], v)
                getattr(block, engmap[e])(body)


D = 1024
DFF = 2816
NJ = 22
EPS = 1e-6
WIN = (2, 4, 8, 16)
NEG = -30000.0


def build(stage=99):
    nc = bass.Bass("TRN2", target_bir_lowering=False)
    P = Prog(nc)
    es = ExitStack()

    BIGW = ("w_kv", "w_q", "w_o", "w_gate", "w_up", "w_down", "xe", "c128", "c512", "c2048")

    def din(name, shape, dt=F32):
        if stage == -1 and name in BIGW:
            shape = [1, 1]
        return nc.dram_tensor(name, list(shape), dt, kind="ExternalInput").ap()

    def dout(name, shape):
        return nc.dram_tensor(name, list(shape), F32, kind="ExternalOutput").ap()

    def dint(name, shape, dt):
        return nc.dram_tensor(name, list(shape), dt, kind="Internal").ap()

    def sb(name, shape, dt):
        return es.enter_context(nc.sbuf_tensor(name, list(shape), dt))

    esA = ExitStack()

    def sbA(name, shape, dt):
        return esA.enter_context(nc.sbuf_tensor(name, list(shape), dt))

    def psum(name, shape, dt):
        return es.enter_context(nc.psum_tensor(name, list(shape), dt))

    xe = din("xe", [4096, D])
    xs = din("xs", [32, D])
    spool = din("spool", [4, 2, 15, D])
    cch = [din("c128", [4, 128, 2, D]), din("c512", [4, 512, 2, D]), din("c2048", [4, 2048, 2, D])]
    a_norm = din("a_norm", [2, D])
    pool_w = din("pool_w", [2, 4, 256, 256])
    pool_scale = din("pool_scale", [2, D])
    kv_norm = din("kv_norm", [D])
    w_kv = din("w_kv", [D, 6144])
    k_norm = din("k_norm", [3, 64])
    b_norm = din("b_norm", [2, D])
    w_q = din("w_q", [2, D, 3072])
    q_norm = din("q_norm", [2, 3, 64])
    w_o = din("w_o", [2, D, D])
    ffn_norm = din("ffn_norm", [4, D])
    w_gate = din("w_gate", [4, D, DFF])
    w_up = din("w_up", [4, D, DFF])
    w_down = din("w_down", [4, DFF, D])
    cs_e = din("cs_e", [4096, 128])
    cs_s = din("cs_s", [32, 128])
    ic = din("ic", [2, 4, 16])
    hb = din("hb", [128, 1])
    ident_in = din("ident", [128, 128])
    gcols_in = din("gcols_in", [128, 9, 8])
    mkall = din("mkall", [128, 11, 512])

    y_own = dout("y_own", [2048, D])
    y_s = dout("y_s", [32, D])
    pool_p = dout("pool_p", [2, 15, D])
    pool_s = dout("pool_s", [4, 2, 15, D])
    kvp = [dout("kvp128", [128, 2, D]), dout("kvp512", [512, 2, D]), dout("kvp2048", [2048, 2, D])]
    kvs = [dout("kvs128", [4, 128, 2, D]), dout("kvs512", [4, 512, 2, D]), dout("kvs2048", [4, 2048, 2, D])]

    wg_b = dint("wg_b", [4, D, DFF], BF16)
    wu_b = dint("wu_b", [4, D, DFF], BF16)
    wd_b = dint("wd_b", [4, DFF, D], BF16)
    wkv_b = dint("wkv_b", [D, 6144], BF16)
    wq_b = dint("wq_b", [2, D, 3072], BF16)
    wo_b = dint("wo_b", [2, D, D], BF16)
    kt0_s = dint("kt0_s", [8, 128, 4096], BF16)
    kt1_s = dint("kt1_s", [8, 128, 4, 1024], BF16)
    kt2_s = dint("kt2_s", [8, 128, 16, 256], BF16)
    v_s = dint("v_s", [3, 4096, D], BF16)
    ktn_s = dint("ktn_s", [3, 8, 128, 32], BF16)
    vn_s = dint("vn_s", [3, 32, D], BF16)
    mks_in = din("mks", [128, 80])
    mkn_in = din("mkn", [32, 12, 64])

    xT = sb("xT", [128, 8, 2080], F32)
    identf = sb("identf", [128, 128], F32)
    identb = sb("identb", [128, 128], BF16)
    onesb = sb("onesb", [128, 128], BF16)
    epst = sb("epst", [128, 1], F32)
    gcols = sb("gcols", [128, 9, 8], F32)
    sqb = sb("sqb", [128, 2, 512], BF16)
    rstd = sb("rstd", [128, 512], F32)
    hT = sb("hT", [128, 8, 512], BF16)
    big = sb("big", [128, 24, 512], BF16)
    sg = sb("sg", [128, 2, 512], F32)
    wgt = sb("wgt", [128, 2, 8, 256], BF16)
    wut = sb("wut", [128, 2, 8, 256], BF16)
    wdt = sb("wdt", [128, 2, 22, 128], BF16)
    gAB = sb("gAB", [128, 3, 3, 128], F32)
    gtmp = sb("gtmp", [128, 3, 64], F32)
    cst = sb("cst", [128, 128], F32)
    cs4 = sb("cs4", [128, 4, 128], F32)
    tabt = sb("tabt", [128, 128], F32)
    ss8 = sb("ss8", [128, 2, 8], F32)
    w1 = sb("w1", [128, 2, 512], F32)
    kf = sb("kf", [128, 2, 512], F32)
    kb = sb("kb", [128, 2, 512], BF16)
    ktst = sb("ktst", [128, 4, 512], BF16)
    yst = sb("yst", [128, 2, D], F32)

    uT = sbA("uT", [128, 2, 8, 528], BF16)
    wsum = sbA("wsum", [128, 2, 528], F32)
    icb = sbA("icb", [128, 2, 4, 16], F32)
    wpf = sbA("wpf", [128, 512], F32)
    wp = sbA("wp", [128, 2, 4, 2, 256], BF16)
    usT = sbA("usT", [128, 8, 4, 24], BF16)
    wsS = sbA("wsS", [128, 2, 4, 24], F32)
    uf = sbA("uf", [128, 8, 32], F32)
    psS = [psum("psS%d" % i, [128, 2, 512], F32) for i in range(2)]
    ps = [psS[0][:, 0, :], psS[0][:, 1, :], psS[1][:, 0, :], psS[1][:, 1, :]] + \
         [psum("ps%d" % i, [128, 512], F32) for i in range(4, 8)]
    psT = ps[7][:].bitcast(BF16)

    def op(eng, fn, reads=(), writes=(), dkey=None):
        return P.op(eng, fn, reads, writes, dkey)

    op("sp", lambda e: e.dma_start(out=identf[:], in_=ident_in), writes=["identf"], dkey="identf")
    op("dve", lambda e: e.tensor_copy(out=identb[:], in_=identf[:]), reads=["identf"], writes=["identb"])
    op("dve", lambda e: e.memset(onesb[:], 1.0), writes=["onesb"])
    op("dve", lambda e: e.memset(epst[:], EPS), writes=["epst"])
    op("dve", lambda e: e.memset(uT[:], 0.0), writes=[("uT", 0), ("uT", 1)])
    op("sp", lambda e: e.dma_start(out=gcols[:], in_=gcols_in), writes=["gcols"], dkey="gcols")
    op("sp", lambda e: e.dma_start(out=icb[:].rearrange("p a g t -> p (a g t)"),
                                   in_=ic.rearrange("a g t -> (a g t)").partition_broadcast(128)),
       writes=["icb"], dkey="icb")
    op("sp", lambda e: e.dma_start(out=yst[:].rearrange("p a c -> p (a c)"),
                                   in_=pool_scale.rearrange("a c -> (a c)").partition_broadcast(128)),
       writes=[("yst", 0), ("yst", 1)], dkey="scb")
    for l in range(2):
        for g in range(4):
            op("sp", lambda e, l=l, g=g: e.dma_start(
                out=wpf[:, 0:512].rearrange("p (k n) -> p k n", k=2),
                in_=pool_w[l, g].rearrange("(k p) n -> p k n", p=128)),
               writes=["wpf"], dkey="wpf")
            for k in range(2):
                op("dve", lambda e, l=l, g=g, k=k: e.tensor_tensor(
                    out=wp[:, l, g, k, :], in0=wpf[:, k * 256:(k + 1) * 256],
                    in1=yst[:, l, g * 256:(g + 1) * 256], op=ALU.mult),
                   reads=["wpf", ("yst", 0), ("yst", 1)], writes=["wp"])

    def cast(dst, src, key, nsplit=4):
        n = src.shape[0]
        st = n // nsplit
        for s in range(nsplit):
            op("pool", lambda e, s=s: e.dma_start(out=dst[s * st:(s + 1) * st], in_=src[s * st:(s + 1) * st]),
               writes=[key], dkey=key)

    def cast_ffn(l):
        cast(wg_b[l], w_gate[l], ("wg", l))
        cast(wu_b[l], w_up[l], ("wu", l))
        cast(wd_b[l], w_down[l], ("wd", l))

    if stage >= 0:
        cast_ffn(0)
        cast_ffn(1)
        cast(wkv_b, w_kv, "wkv", 8)
    if stage >= 3:
        cast(wq_b[0], w_q[0], ("wq", 0))
        cast(wo_b[0], w_o[0], ("wo", 0))
        cast_ffn(2)
        cast(wq_b[1], w_q[1], ("wq", 1))
        cast(wo_b[1], w_o[1], ("wo", 1))
        cast_ffn(3)

    cnt = {"psK": 0, "sp": 0, "pt": 0, "kf": 0, "x": 0, "sq": 0, "psA": 0, "psB": 0, "psD": 0, "sg": 0, "w": 0, "wd": 0, "ys": 0, "wk": 0}

    def load_group_x(src, r0, ntok, c0):
        for t0 in range(0, ntok, 128):
            nr = min(128, ntok - t0)
            b = cnt["x"] % 2
            cnt["x"] += 1
            op("sp", lambda e, b=b, t0=t0, nr=nr: e.dma_start(out=yst[:nr, b, :], in_=src[r0 + t0:r0 + t0 + nr, :]),
               writes=[("yst", b)], dkey=("yst", b))
            for hf in range(2):
                pb = ps[5 + hf]
                for k4 in range(4):
                    kc = hf * 4 + k4
                    op("pe", lambda e, b=b, kc=kc, k4=k4, nr=nr, pb=pb: e.transpose(
                        out=pb[:, k4 * 128:k4 * 128 + nr], in_=yst[:nr, b, kc * 128:(kc + 1) * 128],
                        identity=identf[:nr, :nr]),
                       reads=[("yst", b), "identf"], writes=[("ps", 5 + hf)])
                op("act", lambda e, hf=hf, nr=nr, t0=t0, pb=pb: e.activation(
                    out=xT[:, hf * 4:hf * 4 + 4, c0 + t0:c0 + t0 + nr],
                    in_=pb[:].rearrange("p (k t) -> p k t", k=4)[:, :, :nr], func=AF.Copy),
                   reads=[("ps", 5 + hf)], writes=[("xT", c0)])

    def store_group_T(srcT, srckey, c0, ntok, dst, r0, dkey):
        for t0 in range(0, ntok, 128):
            nr = min(128, ntok - t0)
            b = cnt["ys"] % 2
            cnt["ys"] += 1
            for hf in range(2):
                pb = ps[5 + hf]
                for k4 in range(4):
                    kc = hf * 4 + k4
                    op("pe", lambda e, kc=kc, k4=k4, nr=nr, t0=t0, pb=pb: e.transpose(
                        out=pb[:nr, k4 * 128:(k4 + 1) * 128], in_=srcT[:, kc, c0 + t0:c0 + t0 + nr],
                        identity=identf[:]),
                       reads=[srckey, "identf"], writes=[("ps", 5 + hf)])
                op("act", lambda e, hf=hf, nr=nr, b=b, pb=pb: e.activation(
                    out=yst[:nr, b, hf * 512:(hf + 1) * 512], in_=pb[:nr, :], func=AF.Copy),
                   reads=[("ps", 5 + hf)], writes=[("yst", b)])
            op("sp", lambda e, b=b, nr=nr, t0=t0: e.dma_start(out=dst[r0 + t0:r0 + t0 + nr, :], in_=yst[:nr, b, :]),
               reads=[("yst", b)], dkey=dkey)

    def norm_feat(c0, ntok, gi, dst, dkeyw, dcol0, samp=False):
        pr = ps[4]
        for kc in range(8):
            b = cnt["sq"] % 2
            cnt["sq"] += 1
            op("act", lambda e, kc=kc, b=b: e.activation(out=sqb[:, b, :ntok], in_=xT[:, kc, c0:c0 + ntok], func=AF.Square),
               reads=[("xT", c0)], writes=[("sqb", b)])
            op("pe", lambda e, kc=kc, b=b: e.matmul(pr[:, :ntok], lhsT=onesb[:], rhs=sqb[:, b, :ntok],
                                                    start=(kc == 0), stop=(kc == 7)),
               reads=[("sqb", b), "onesb"], writes=[("ps", 4)])
        op("act", lambda e: e.activation(out=rstd[:, :ntok], in_=pr[:, :ntok], func=AF.Sqrt, scale=1.0 / D, bias=epst[:]),
           reads=[("ps", 4), "epst"], writes=["rstd"])
        op("dve", lambda e: e.reciprocal(out=rstd[:, :ntok], in_=rstd[:, :ntok]), reads=["rstd"], writes=["rstd"])
        for kc in range(8):
            if samp:
                op("dve", lambda e, kc=kc: e.scalar_tensor_tensor(
                    out=dst[:, kc, :, 16:24], in0=xT[:, kc, c0:c0 + 32].rearrange("p (n t) -> p n t", n=4),
                    scalar=gcols[:, gi, kc:kc + 1], in1=rstd[:, :32].rearrange("p (n t) -> p n t", n=4),
                    op0=ALU.mult, op1=ALU.mult),
                   reads=[("xT", c0), "gcols", "rstd"], writes=[dkeyw])
            else:
                op("dve", lambda e, kc=kc: e.scalar_tensor_tensor(
                    out=dst[:, kc, dcol0:dcol0 + ntok], in0=xT[:, kc, c0:c0 + ntok], scalar=gcols[:, gi, kc:kc + 1],
                    in1=rstd[:, :ntok], op0=ALU.mult, op1=ALU.mult),
                   reads=[("xT", c0), "gcols", "rstd"], writes=[dkeyw])

    def u_rows(gi, c0, rcol0):
        for kc in range(8):
            op("dve", lambda e, kc=kc: e.scalar_tensor_tensor(
                out=uf[:, kc, :], in0=xT[:, kc, c0:c0 + 32], scalar=gcols[:, gi, kc:kc + 1],
                in1=rstd[:, rcol0:rcol0 + 32], op0=ALU.mult, op1=ALU.mult),
               reads=[("xT", c0), "gcols", "rstd"], writes=["uf"])
        b = cnt["ys"] % 2
        cnt["ys"] += 1
        for hf in range(2):
            pb = ps[5 + hf]
            for k4 in range(4):
                kc = hf * 4 + k4
                op("pe", lambda e, kc=kc, k4=k4, pb=pb: e.transpose(
                    out=pb[:32, k4 * 128:(k4 + 1) * 128], in_=uf[:, kc, :], identity=identf[:]),
                   reads=["uf", "identf"], writes=[("ps", 5 + hf)])
            op("act", lambda e, hf=hf, b=b, pb=pb: e.activation(
                out=yst[:32, b, hf * 512:(hf + 1) * 512], in_=pb[:32, :], func=AF.Copy),
               reads=[("ps", 5 + hf)], writes=[("yst", b)])
        return b

    def ffn(l, c0, ntok):
        gi = 2 + l
        norm_feat(c0, ntok, gi, hT[:], "hT", 0)
        for jp in range(11):
            wb = cnt["w"] % 2
            cnt["w"] += 1
            op("sp", lambda e, jp=jp, wb=wb: e.dma_start(
                out=wgt[:, wb], in_=wg_b[l, :, jp * 256:(jp + 1) * 256].rearrange("(k p) n -> p k n", p=128)),
               reads=[("wg", l)], writes=[("wgt", wb)], dkey=("wgt", wb))
            op("sp", lambda e, jp=jp, wb=wb: e.dma_start(
                out=wut[:, wb], in_=wu_b[l, :, jp * 256:(jp + 1) * 256].rearrange("(k p) n -> p k n", p=128)),
               reads=[("wu", l)], writes=[("wut", wb)], dkey=("wut", wb))
            for j2 in range(2):
                j = jp * 2 + j2
                pa = cnt["psA"] % 2
                cnt["psA"] += 1
                pG, pU = ps[pa], ps[2 + pa]
                for kc in range(8):
                    op("pe", lambda e, kc=kc, wb=wb, j2=j2, pG=pG: e.matmul(
                        pG[:, :ntok], lhsT=wgt[:, wb, kc, j2 * 128:(j2 + 1) * 128], rhs=hT[:, kc, :ntok],
                        start=(kc == 0), stop=(kc == 7)),
                       reads=[("wgt", wb), "hT"], writes=[("ps", pa)])
                for kc in range(8):
                    op("pe", lambda e, kc=kc, wb=wb, j2=j2, pU=pU: e.matmul(
                        pU[:, :ntok], lhsT=wut[:, wb, kc, j2 * 128:(j2 + 1) * 128], rhs=hT[:, kc, :ntok],
                        start=(kc == 0), stop=(kc == 7)),
                       reads=[("wut", wb), "hT"], writes=[("ps", 2 + pa)])
                sgb = cnt["sg"] % 2
                cnt["sg"] += 1
                op("act", lambda e, sgb=sgb, pG=pG: e.activation(out=sg[:, sgb, :ntok], in_=pG[:, :ntok], func=AF.Silu),
                   reads=[("ps", pa)], writes=[("sg", sgb)])
                op("dve", lambda e, sgb=sgb, pU=pU, j=j: e.tensor_tensor(
                    out=big[:, j, :ntok], in0=pU[:, :ntok], in1=sg[:, sgb, :ntok], op=ALU.mult),
                   reads=[("ps", 2 + pa), ("sg", sgb)], writes=[("big", j)])
        for c in range(8):
            wb = cnt["wd"] % 2
            cnt["wd"] += 1
            op("sp", lambda e, c=c, wb=wb: e.dma_start(
                out=wdt[:, wb], in_=wd_b[l, :, c * 128:(c + 1) * 128].rearrange("(j p) n -> p j n", p=128)),
               reads=[("wd", l)], writes=[("wdt", wb)], dkey=("wdt", wb))
            pd = cnt["psD"] % 2
            cnt["psD"] += 1
            pD = ps[pd]
            for j in range(NJ):
                op("pe", lambda e, j=j, wb=wb, pD=pD: e.matmul(
                    pD[:, :ntok], lhsT=wdt[:, wb, j, :], rhs=big[:, j, :ntok],
                    start=(j == 0), stop=(j == NJ - 1)),
                   reads=[("wdt", wb), ("big", j)], writes=[("ps", pd)])
            op("dve", lambda e, c=c, pD=pD: e.tensor_tensor(
                out=xT[:, c, c0:c0 + ntok], in0=pD[:, :ntok], in1=xT[:, c, c0:c0 + ntok], op=ALU.add),
               reads=[("ps", pd), ("xT", c0)], writes=[("xT", c0)])

    def pool_layer(i, c0, ntok, start_tab, last=False):
        u = uT[:, i]
        norm_feat(c0, ntok, i, u, ("uT", i), 16)
        if last:
            yb = u_rows(i, c0 + ntok - 32, ntok - 32)
            op("sp", lambda e, yb=yb: e.dma_start(out=pool_p[i], in_=yst[17:32, yb, :]), reads=[("yst", yb)], dkey="po")
        for kc in range(8):
            g = kc // 2
            src = None
            nst = g + 1
            for s in range(nst):
                sh = 1 << s
                lo = 2 * sh
                wbuf = s % 2
                if s == 0:
                    op("dve", lambda e, kc=kc: e.tensor_tensor(
                        out=wsum[:, 0, 2:16 + ntok], in0=u[:, kc, 2:16 + ntok], in1=u[:, kc, 1:15 + ntok], op=ALU.add),
                       reads=[("uT", i)], writes=[("wsum", 0)])
                else:
                    op("dve", lambda e, sh=sh, lo=lo, wbuf=wbuf: e.tensor_tensor(
                        out=wsum[:, wbuf, lo:16 + ntok], in0=wsum[:, 1 - wbuf, lo:16 + ntok],
                        in1=wsum[:, 1 - wbuf, lo - sh:16 + ntok - sh], op=ALU.add),
                       reads=[("wsum", 1 - wbuf)], writes=[("wsum", wbuf)])
            wl = (nst - 1) % 2
            op("dve", lambda e, kc=kc, wl=wl, g=g: e.scalar_tensor_tensor(
                out=hT[:, kc, :ntok], in0=wsum[:, wl, 16:16 + ntok], scalar=1.0 / WIN[g], in1=u[:, kc, 16:16 + ntok],
                op0=ALU.mult, op1=ALU.subtract),
               reads=[("wsum", wl), ("uT", i)], writes=["hT"])
            if start_tab is not None:
                op("dve", lambda e, kc=kc, wl=wl, g=g: e.tensor_tensor(
                    out=wsum[:, wl, 0:16], in0=wsum[:, wl, 16:32], in1=icb[:, start_tab, g, :], op=ALU.mult),
                   reads=[("wsum", wl), "icb"], writes=[("wsum", wl)])
                op("dve", lambda e, kc=kc, wl=wl: e.tensor_tensor(
                    out=hT[:, kc, 0:16], in0=wsum[:, wl, 0:16], in1=u[:, kc, 16:32], op=ALU.subtract),
                   reads=[("wsum", wl), ("uT", i)], writes=["hT"])
        pool_mm(i, c0, ntok)
        op("act", lambda e: e.activation(out=u[:, :, 1:16], in_=u[:, :, 1 + ntok:16 + ntok], func=AF.Copy),
           reads=[("uT", i)], writes=[("uT", i)])

    def pool_mm(i, c0, ntok):
        for c in range(8):
            g = c // 2
            pd = cnt["psD"] % 2
            cnt["psD"] += 1
            pD = ps[pd]
            for k in range(2):
                op("pe", lambda e, k=k, g=g, c=c, pD=pD: e.matmul(
                    pD[:, :ntok], lhsT=wp[:, i, g, k, (c % 2) * 128:(c % 2) * 128 + 128], rhs=hT[:, 2 * g + k, :ntok],
                    start=(k == 0), stop=(k == 1)),
                   reads=["wp", "hT"], writes=[("ps", pd)])
            op("dve", lambda e, c=c, pD=pD: e.tensor_tensor(
                out=xT[:, c, c0:c0 + ntok], in0=pD[:, :ntok], in1=xT[:, c, c0:c0 + ntok], op=ALU.add),
               reads=[("ps", pd), ("xT", c0)], writes=[("xT", c0)])


    def pool_layer_sample(i):
        c0 = 2048
        b = cnt["x"] % 2
        cnt["x"] += 1
        for n in range(4):
            op("sp", lambda e, b=b, n=n: e.dma_start(out=yst[n * 15:(n + 1) * 15, b, :], in_=spool[n, i]),
               writes=[("yst", b)], dkey=("yst", b))
        for hf in range(2):
            pb = ps[5 + hf]
            for k4 in range(4):
                kc = hf * 4 + k4
                op("pe", lambda e, b=b, kc=kc, k4=k4, pb=pb: e.transpose(
                    out=pb[:, k4 * 128:k4 * 128 + 60], in_=yst[:60, b, kc * 128:(kc + 1) * 128],
                    identity=identf[:60, :60]),
                   reads=[("yst", b), "identf"], writes=[("ps", 5 + hf)])
            for k4 in range(4):
                kc = hf * 4 + k4
                op("act", lambda e, kc=kc, k4=k4, pb=pb: e.activation(
                    out=usT[:, kc, :, 1:16], in_=pb[:, k4 * 128:k4 * 128 + 60].rearrange("p (n r) -> p n r", n=4),
                    func=AF.Copy),
                   reads=[("ps", 5 + hf)], writes=["usT"])
        norm_feat(c0, 32, i, usT, "usT", 0, samp=True)
        yb = u_rows(i, c0, 0)
        for n in range(4):
            op("sp", lambda e, n=n, yb=yb: e.dma_start(out=pool_s[n, i, 7:15, :], in_=yst[n * 8:(n + 1) * 8, yb, :]),
               reads=[("yst", yb)], dkey="po")
            op("sp", lambda e, n=n: e.dma_start(out=pool_s[n, i, 0:7, :], in_=spool[n, i, 8:15, :]), dkey="po")
        for kc in range(8):
            g = kc // 2
            nst = g + 1
            for s_ in range(nst):
                sh = 1 << s_
                lo = 2 * sh
                wbuf = s_ % 2
                if s_ == 0:
                    op("dve", lambda e, kc=kc: e.tensor_tensor(
                        out=wsS[:, 0, :, 2:24], in0=usT[:, kc, :, 2:24], in1=usT[:, kc, :, 1:23], op=ALU.add),
                       reads=["usT"], writes=[("wsS", 0)])
                else:
                    op("dve", lambda e, sh=sh, lo=lo, wbuf=wbuf: e.tensor_tensor(
                        out=wsS[:, wbuf, :, lo:24], in0=wsS[:, 1 - wbuf, :, lo:24],
                        in1=wsS[:, 1 - wbuf, :, lo - sh:24 - sh], op=ALU.add),
                       reads=[("wsS", 1 - wbuf)], writes=[("wsS", wbuf)])
            wl = (nst - 1) % 2
            op("dve", lambda e, kc=kc, wl=wl, g=g: e.scalar_tensor_tensor(
                out=hT[:, kc, 0:32].rearrange("p (n t) -> p n t", n=4), in0=wsS[:, wl, :, 16:24],
                scalar=1.0 / WIN[g], in1=usT[:, kc, :, 16:24], op0=ALU.mult, op1=ALU.subtract),
               reads=[("wsS", wl), "usT"], writes=["hT"])
        pool_mm(i, c0, 32)

    def setup_gab(si, gsrc_ap):
        op("sp", lambda e: e.dma_start(out=gtmp[:].rearrange("p b d -> p (b d)"),
                                       in_=gsrc_ap.rearrange("b d -> (b d)").partition_broadcast(128)),
           writes=["gtmp"], dkey="gtmp")
        op("dve", lambda e: e.tensor_copy(out=gAB[:, si, :, 0:64], in_=gtmp[:]), reads=["gtmp"], writes=["gAB"])
        op("dve", lambda e: e.tensor_scalar(out=gAB[:, si, :, 64:96], in0=gtmp[:, :, 32:64], scalar1=-1.0, scalar2=None,
                                            op0=ALU.mult), reads=["gtmp"], writes=["gAB"])
        op("dve", lambda e: e.tensor_copy(out=gAB[:, si, :, 96:128], in_=gtmp[:, :, 0:32]), reads=["gtmp"], writes=["gAB"])

    def make_tabs(si, cs_src, r0, ntok):
        cnt["si"] = si
        for t0 in range(0, ntok, 128):
            nr = min(128, ntok - t0)
            tt = t0 // 128
            op("sp", lambda e, t0=t0, nr=nr, tt=tt: e.dma_start(out=cs4[:nr, tt, :], in_=cs_src[r0 + t0:r0 + t0 + nr, :]),
               writes=["cs4"], dkey="cs4")

    def normrope(pk, nr, tt, b, bf_out=False):
        x3 = pk[:nr, :].rearrange("p (h d) -> p h d", h=8)
        kb_ = cnt["kf"] % 2
        cnt["kf"] += 1
        si = cnt["si"]
        op("act", lambda e: e.activation(out=w1[:nr, kb_, :], in_=pk[:nr, :], func=AF.Square),
           reads=[pkkey(pk)], writes=[("w1", kb_)])
        op("dve", lambda e: e.tensor_reduce(out=ss8[:nr, kb_, :], in_=w1[:nr, kb_, :].rearrange("p (h d) -> p h d", h=8),
                                            axis=AX.X, op=ALU.add), reads=[("w1", kb_)], writes=[("ss8", kb_)])
        op("act", lambda e: e.activation(out=ss8[:nr, kb_, :], in_=ss8[:nr, kb_, :], func=AF.Sqrt, scale=1.0 / 64, bias=epst[:nr, :]),
           reads=[("ss8", kb_), "epst"], writes=[("ss8", kb_)])
        op("dve", lambda e: e.reciprocal(out=ss8[:nr, kb_, :], in_=ss8[:nr, kb_, :]), reads=[("ss8", kb_)], writes=[("ss8", kb_)])
        op("dve", lambda e: e.tensor_tensor(out=tabt[:nr, :], in0=cs4[:nr, tt, :], in1=gAB[:nr, si, b, :], op=ALU.mult),
           reads=["cs4", "gAB"], writes=["tabK"])
        tv = kf[:nr, kb_, :].rearrange("p (h d) -> p h d", h=8)
        wv = w1[:nr, kb_, :].rearrange("p (h d) -> p h d", h=8)
        op("dve", lambda e: e.tensor_tensor(out=tv, in0=x3, in1=tabt[:nr, 0:64].unsqueeze(1).to_broadcast([nr, 8, 64]),
                                            op=ALU.mult), reads=[pkkey(pk), "tabK"], writes=[("kf", kb_)])
        op("dve", lambda e: e.tensor_tensor(out=wv[:, :, 0:32], in0=x3[:, :, 32:64],
                                            in1=tabt[:nr, 64:96].unsqueeze(1).to_broadcast([nr, 8, 32]), op=ALU.mult),
           reads=[pkkey(pk), "tabK", ("ss8", kb_)], writes=[("w1", kb_)])
        op("dve", lambda e: e.tensor_tensor(out=wv[:, :, 32:64], in0=x3[:, :, 0:32],
                                            in1=tabt[:nr, 96:128].unsqueeze(1).to_broadcast([nr, 8, 32]), op=ALU.mult),
           reads=[pkkey(pk), "tabK"], writes=[("w1", kb_)])
        op("dve", lambda e: e.tensor_tensor(out=kf[:nr, kb_, :], in0=kf[:nr, kb_, :], in1=w1[:nr, kb_, :], op=ALU.add),
           reads=[("kf", kb_), ("w1", kb_)], writes=[("kf", kb_)])
        if bf_out:
            op("pool", lambda e: e.tensor_tensor(out=kb[:nr, kb_, :].rearrange("p (h d) -> p h d", h=8), in0=tv,
                                                 in1=ss8[:nr, kb_, :].unsqueeze(2).to_broadcast([nr, 8, 64]), op=ALU.mult),
               reads=[("kf", kb_), ("ss8", kb_)], writes=[("kb", kb_)])
        else:
            op("pool", lambda e: e.tensor_tensor(out=tv, in0=tv,
                                                 in1=ss8[:nr, kb_, :].unsqueeze(2).to_broadcast([nr, 8, 64]), op=ALU.mult),
               reads=[("kf", kb_), ("ss8", kb_)], writes=[("kf", kb_)])
        return kb_

    pskeys = {}

    def pkkey(pk):
        return pskeys[id(pk)]

    for i_, p_ in enumerate(ps):
        pskeys[id(p_)] = ("ps", i_)

    def load_wchunk(src2d, col0, key):
        wb = cnt["wk"] % 2
        cnt["wk"] += 1
        t = wgt if wb == 0 else wut
        nm = "wgt" if wb == 0 else "wut"
        view = t[:].rearrange("p b k n -> p (b k n)").rearrange("p (k n) -> p k n", k=8)
        op("sp", lambda e: e.dma_start(out=view, in_=src2d[:, col0:col0 + 512].rearrange("(k p) n -> p k n", p=128)),
           reads=[key], writes=[(nm, 0), (nm, 1)], dkey=(nm, 0))
        return view, [(nm, 0), (nm, 1)]

    def kv_proj(kind, gi_, c0, ntok, r0):
        norm_feat(c0, ntok, 6, hT[:], "hT", 0)
        samp = (kind == "S")
        make_tabs(0, cs_s if samp else cs_e, 0 if samp else r0, ntok)
        if kind == "H" and gi_ < 3:
            chunks = [4, 5, 10, 11]
        else:
            chunks = list(range(12))
        for ck in chunks:
            isk = ck < 6
            b = (ck % 6) // 2
            hh = ck % 2
            wv, wkeys = load_wchunk(wkv_b, ck * 512, "wkv")
            for t0 in range(0, ntok, 128):
                nr = min(128, ntok - t0)
                tt = t0 // 128
                pa = cnt["psK"] % 4
                cnt["psK"] += 1
                pk = ps[pa]
                for kc in range(8):
                    op("pe", lambda e, kc=kc, pk=pk, t0=t0, nr=nr, wv=wv: e.matmul(
                        pk[:nr, :], lhsT=hT[:, kc, t0:t0 + nr], rhs=wv[:, kc, :], start=(kc == 0), stop=(kc == 7)),
                       reads=["hT"] + wkeys, writes=[("ps", pa)])
                if isk:
                    fb = normrope(pk, nr, tt, b)
                else:
                    fb = cnt["kf"] % 2
                    cnt["kf"] += 1
                    op("act", lambda e, fb=fb, pk=pk, nr=nr: e.activation(out=kf[:nr, fb, :], in_=pk[:nr, :], func=AF.Copy),
                       reads=[("ps", pa)], writes=[("kf", fb)])
                sel = 0 if isk else 1
                if kind == "O":
                    pos = gi_ * 512 + t0
                    keep = (128, 512, 2048)[b]
                    if pos >= 2048 - keep:
                        rr = pos - (2048 - keep)
                        op("sp", lambda e, fb=fb, rr=rr, nr=nr, b=b, sel=sel, hh=hh: e.dma_start(
                            out=kvp[b][rr:rr + nr, sel, hh * 512:(hh + 1) * 512], in_=kf[:nr, fb, :]),
                           reads=[("kf", fb)], dkey="kvo")
                if samp:
                    Wd = (128, 512, 2048)[b]
                    for n in range(4):
                        op("sp", lambda e, fb=fb, n=n, b=b, sel=sel, hh=hh, Wd=Wd: e.dma_start(
                            out=kvs[b][n, Wd - 8:Wd, sel, hh * 512:(hh + 1) * 512], in_=kf[n * 8:(n + 1) * 8, fb, :]),
                           reads=[("kf", fb)], dkey="kvo")
                if stage >= 3 and samp:
                    op("act", lambda e, fb=fb, nr=nr: e.activation(out=kb[:nr, fb, :], in_=kf[:nr, fb, :], func=AF.Copy),
                       reads=[("kf", fb)], writes=[("kb", fb)])
                    if isk:
                        for q in range(4):
                            op("pe", lambda e, fb=fb, q=q, nr=nr: e.transpose(
                                out=psT[:, q * 128:q * 128 + nr], in_=kb[:nr, fb, q * 128:(q + 1) * 128],
                                identity=identb[:nr, :nr]),
                               reads=[("kb", fb), "identb"], writes=[("ps", 7)])
                        op("dve", lambda e: e.tensor_copy(
                            out=ktst[:, :, 0:32], in_=psT[:, 0:512].rearrange("p (q t) -> p q t", q=4)[:, :, 0:32]),
                           reads=[("ps", 7)], writes=["ktst"])
                        op("sp", lambda e, b=b, hh=hh: e.dma_start(
                            out=ktn_s[b, hh * 4:hh * 4 + 4].rearrange("q p t -> p q t"), in_=ktst[:, :, 0:32]),
                           reads=["ktst"], writes=[("kts", "S", 0)], dkey=("kts", "S", 0))
                    else:
                        op("sp", lambda e, fb=fb, b=b, hh=hh: e.dma_start(
                            out=vn_s[b, :, hh * 512:(hh + 1) * 512], in_=kb[:32, fb, :]),
                           reads=[("kb", fb)], writes=[("vs", "S", 0)], dkey=("vs", "S", 0))
                if stage >= 3 and not samp:
                    op("act", lambda e, fb=fb, nr=nr: e.activation(out=kb[:nr, fb, :], in_=kf[:nr, fb, :], func=AF.Copy),
                       reads=[("kf", fb)], writes=[("kb", fb)])
                    if isk:
                        for q in range(4):
                            op("pe", lambda e, fb=fb, q=q, nr=nr: e.transpose(
                                out=psT[:, q * 128:q * 128 + nr], in_=kb[:nr, fb, q * 128:(q + 1) * 128],
                                identity=identb[:nr, :nr]),
                               reads=[("kb", fb), "identb"], writes=[("ps", 7)])
                        dil = (1, 4, 16)[b]
                        mt = 128 // dil
                        op("dve", lambda e, dil=dil, mt=mt, tt=tt: e.tensor_copy(
                            out=ktst[:].rearrange("p q (r m) -> p q r m", r=dil)[:, :, :, tt * mt:(tt + 1) * mt],
                            in_=psT[:, 0:512].rearrange("p (q m r) -> p q r m", q=4, r=dil)),
                           reads=[("ps", 7)], writes=["ktst"])
                        if t0 + 128 >= ntok:
                            for q in range(4):
                                hpq = hh * 4 + q
                                if b == 0:
                                    dst = kt0_s[hpq, :, r0:r0 + 512]
                                    src_ = ktst[:, q, :]
                                elif b == 1:
                                    dst = kt1_s[hpq, :, :, r0 // 4:r0 // 4 + 128]
                                    src_ = ktst[:, q, :].rearrange("p (r m) -> p r m", r=4)
                                else:
                                    dst = kt2_s[hpq, :, :, r0 // 16:r0 // 16 + 32]
                                    src_ = ktst[:, q, :].rearrange("p (r m) -> p r m", r=16)
                                op("sp", lambda e, dst=dst, src_=src_: e.dma_start(out=dst, in_=src_),
                                   reads=["ktst"], writes=[("kts", kind, gi_)], dkey=("kts", kind, gi_))
                    else:
                        op("sp", lambda e, fb=fb, nr=nr, b=b, hh=hh, t0=t0: e.dma_start(
                            out=v_s[b, r0 + t0:r0 + t0 + nr, hh * 512:(hh + 1) * 512], in_=kb[:nr, fb, :]),
                           reads=[("kb", fb)], writes=[("vs", kind, gi_)], dkey=("vs", kind, gi_))

    if stage >= 0:
        for b, Wd in enumerate((128, 512, 2048)):
            for n in range(4):
                op("act", lambda e, b=b, n=n, Wd=Wd: e.dma_start(out=kvs[b][n, 0:Wd - 8], in_=cch[b][n, 8:Wd]), dkey="cc")
        setup_gab(0, k_norm)
        def ginfo(kind, gi_):
            if kind == "S":
                return 2048, 0, 32
            return gi_ * 512, gi_ * 512 + (2048 if kind == "O" else 0), 512

        pairs = [[("H", 0), ("H", 1), ("H", 2), ("H", 3)], [("O", 0), ("O", 1), ("O", 2), ("O", 3)], [("S", 0)]]
        for pair in pairs:
            for (kind, gi_) in pair:
                c0, r0, ntok = ginfo(kind, gi_)
                load_group_x(xs if kind == "S" else xe, r0, ntok, c0)
            for i in range(2):
                for (kind, gi_) in pair:
                    c0, r0, ntok = ginfo(kind, gi_)
                    st = None
                    if kind == "H" and gi_ == 0:
                        st = 0
                    if kind == "O" and gi_ == 0:
                        st = 1
                    if kind == "S":
                        pool_layer_sample(i)
                    else:
                        pool_layer(i, c0, 512, st, last=(kind == "O" and gi_ == 3))
                for (kind, gi_) in pair:
                    c0, r0, ntok = ginfo(kind, gi_)
                    ffn(i, c0, ntok)
            for (kind, gi_) in pair:
                c0, r0, ntok = ginfo(kind, gi_)
                if stage >= 2:
                    kv_proj(kind, gi_, c0, ntok, r0)
                if stage < 3:
                    if kind == "O":
                        store_group_T(xT, ("xT", c0), c0, 512, y_own, gi_ * 512, "y")
                    if kind == "S":
                        store_group_T(xT, ("xT", c0), c0, 32, y_s, 0, "y")

    import os
    if stage >= 3 or stage == -1:
        OLD = [("uT", 0), ("uT", 1), ("wsum", 0), ("wsum", 1), "icb", "wpf", "wp", "usT", ("wsS", 0), ("wsS", 1), "uf"]
        esA.close()
        KT0 = sb("KT0", [128, 640], BF16)
        KT1 = sb("KT1", [128, 4, 256], BF16)
        KT2 = sb("KT2", [128, 16, 256], BF16)
        V0 = sb("V0", [128, 5, 128], BF16)
        V1 = sb("V1", [128, 2, 4, 128], BF16)
        V2p = sb("V2p", [128, 16, 128], BF16)
        V2c = sb("V2c", [128, 16, 128], BF16)
        mkb = sb("mkb", [128, 11, 512], BF16)
        ptb = sb("ptb", [128, 2, 2, 512], BF16)
        hbt = sb("hbt", [128, 2], F32)
        pending = {"KT0", "KT1", "KT2", "V0", "V1", "V2p", "V2c", "mkb", ("ptb", 0), ("ptb", 1), "hbt", "mks", "mkn"}
        _op0 = op

        def op(eng, fn, reads=(), writes=(), dkey=None):
            writes = list(writes)
            hit = [w for w in writes if w in pending]
            if hit:
                for w in hit:
                    pending.discard(w)
                writes = writes + OLD
            return _op0(eng, fn, reads, writes, dkey)

        op("pool", lambda e: e.dma_start(out=mkb[:], in_=mkall), writes=["mkb"], dkey="mkb")
        op("sp", lambda e: e.dma_start(out=hbt[:, 0:1], in_=hb), writes=["hbt"], dkey="hbt")
        op("dve", lambda e: e.memset(hbt[:, 1:2], 0.0), writes=["hbt"])
        SCR = [("kts", k_, g_) for k_ in ("H", "O") for g_ in range(4)] + [("vs", k_, g_) for k_ in ("H", "O") for g_ in range(4)]

        def q_proj(kind, jb, c0, ntok, r0):
            norm_feat(c0, ntok, 7 + jb, hT[:], "hT", 0)
            samp = (kind == "S")
            make_tabs(1 + jb, cs_s if samp else cs_e, 0 if samp else r0, ntok)
            for ck in range(6):
                b = ck // 2
                hh = ck % 2
                wv, wkeys = load_wchunk(wq_b[jb], ck * 512, ("wq", jb))
                for t0 in range(0, ntok, 128):
                    nr = min(128, ntok - t0)
                    tt = t0 // 128
                    pa = cnt["psK"] % 4
                    cnt["psK"] += 1
                    pk = ps[pa]
                    for kc in range(8):
                        op("pe", lambda e, kc=kc, pk=pk, t0=t0, nr=nr, wv=wv: e.matmul(
                            pk[:nr, :], lhsT=hT[:, kc, t0:t0 + nr], rhs=wv[:, kc, :], start=(kc == 0), stop=(kc == 7)),
                           reads=["hT"] + wkeys, writes=[("ps", pa)])
                    fb = normrope(pk, nr, tt, b, bf_out=True)
                    for q in range(4):
                        op("pe", lambda e, fb=fb, q=q, nr=nr: e.transpose(
                            out=psT[:, q * 128:q * 128 + nr], in_=kb[:nr, fb, q * 128:(q + 1) * 128],
                            identity=identb[:nr, :nr]),
                           reads=[("kb", fb), "identb"], writes=[("ps", 7)])
                    i0 = b * 8 + hh * 4
                    op("dve", lambda e, i0=i0, nr=nr, t0=t0: e.tensor_copy(
                        out=big[:, i0:i0 + 4, t0:t0 + nr], in_=psT[:, 0:512].rearrange("p (q t) -> p q t", q=4)[:, :, :nr]),
                       reads=[("ps", 7)], writes=[("big", i0 + q_) for q_ in range(4)])

        def attention(i):
            E0 = 2048 + 512 * i
            KA = os.environ.get("KA", "012nqmep")
            for hp in range(int(os.environ.get("KB_NHP", "8"))):
                hc = slice(hp * 128, (hp + 1) * 128)
                ld = lambda out, in_, key: op("sp", lambda e: e.dma_start(out=out, in_=in_), reads=SCR, writes=[key], dkey=key)
                ld(KT0[:], kt0_s[hp, :, E0 - 128:E0 + 512], "KT0")
                ld(V0[:], v_s[0, E0 - 128:E0 + 512, hc].rearrange("(k p) c -> p k c", p=128), "V0")
                ld(KT1[:], kt1_s[hp, :, :, (E0 - 512) // 4:(E0 + 512) // 4], "KT1")
                for a_ in range(2):
                    ld(V1[:, a_], v_s[1, E0 - 512 + 512 * a_:E0 + 512 * a_, hc].rearrange("(m r) c -> m r c", r=4), "V1")
                ld2 = lambda out, in_, key: op("sp", lambda e: e.dma_start(out=out, in_=in_), reads=SCR,
                                               writes=[key, (key, 0), (key, 1)], dkey=key)
                ld2(KT2[:], kt2_s[hp], "KT2")
                ld2(V2p[:], v_s[2, 0:2048, hc].rearrange("(m r) c -> m r c", r=16), "V2p")
                ld2(V2c[:], v_s[2, 2048:4096, hc].rearrange("(m r) c -> m r c", r=16), "V2c")
                Ub, Zb = 4 + 2 * (hp % 2), 5 + 2 * (hp % 2)
                U, Z = ps[Ub], ps[Zb]
                firstUZ = [True]
                batches = []
                hbias0 = 0 if i == 0 else 1
                batches.append((hbias0, 0, 128, 256, [(lambda h: KT0[h * 64:(h + 1) * 64, 0:128], "KT0",
                                                        lambda h: V0[:, 0, h * 64:(h + 1) * 64], "V0", 0, slice(0, 128), 128, 128)]))
                for kbj in range(3):
                    batches.append((1, 0, 0, 256, [(lambda h, kbj=kbj: KT0[h * 64:(h + 1) * 64, 128 * (kbj + 1):128 * (kbj + 2)], "KT0",
                                                    lambda h, kbj=kbj: V0[:, kbj + 1, h * 64:(h + 1) * 64], "V0", 0,
                                                    slice(128 * kbj, 128 * kbj + 256), 0, 256)]))
                batches.append((1, 0, 0, 128, [(lambda h: KT0[h * 64:(h + 1) * 64, 512:640], "KT0",
                                                lambda h: V0[:, 4, h * 64:(h + 1) * 64], "V0", 0, slice(384, 512), 0, 128)]))
                for a in range(2):
                    items = []
                    for r in range(4):
                        items.append((lambda h, a=a, r=r: KT1[h * 64:(h + 1) * 64, r, a * 128:(a + 1) * 128], "KT1",
                                      lambda h, a=a, r=r: V1[:, a, r, h * 64:(h + 1) * 64], "V1", 1,
                                      slice(r, 512, 4), r * 128, 128))
                    batches.append(((hbias0 if a == 0 else 1), 1 + a, 0, 512, items))
                for a in range(2):
                    items = []
                    for r in range(16):
                        vt = V2p if a == 0 else V2c
                        items.append((lambda h, r=r, a=a: KT2[h * 64:(h + 1) * 64, r, a * 128:(a + 1) * 128], "KT2",
                                      lambda h, r=r, vt=vt: vt[:, r, h * 64:(h + 1) * 64], ("V2p" if a == 0 else "V2c"), 2,
                                      slice(r, 512, 16), r * 32, 32))
                    batches.append(((0 if a == 0 else 1), (3 + i if a == 0 else 7 + i), 0, 512, items))
                batches = [bt for bt in batches if str(bt[4][0][4]) in KA]
                def emit_qk(bt):
                    (bsel, midx, c_lo, c_hi, items) = bt
                    sp_ = cnt["sp"] % 2
                    cnt["sp"] += 1
                    Sh = [ps[2 * sp_], ps[2 * sp_ + 1]]
                    skeys = [("ps", 2 * sp_), ("ps", 2 * sp_ + 1)]
                    fs = [True, True]
                    for (ktf, ktk, vf, vk, b, qs, scol, N) in items:
                        for h in range(2):
                            op("pe", lambda e, h=h, ktf=ktf, b=b, qs=qs, scol=scol, N=N, st=fs[h], Sh=Sh, hp=hp: e.matmul(
                                Sh[h][:, scol:scol + N], lhsT=ktf(h), rhs=big[h * 64:(h + 1) * 64, b * 8 + hp, qs],
                                start=st, stop=False, skip_group_check=True),
                               reads=[ktk, ("big", b * 8 + hp)], writes=[skeys[h]])
                            fs[h] = False
                    for h in range(2):
                        op("pe", lambda e, h=h, Sh=Sh, midx=midx, c_lo=c_lo, c_hi=c_hi: e.matmul(
                            Sh[h][:, c_lo:c_hi], lhsT=identb[:, :], rhs=mkb[:, midx, c_lo:c_hi], start=False, stop=True,
                            skip_group_check=True),
                           reads=["identb", "mkb"], writes=[skeys[h]])
                    pb_ = cnt["pt"] % 2
                    cnt["pt"] += 1
                    op("act", lambda e, sp_=sp_, pb_=pb_, bsel=bsel, c_lo=c_lo, c_hi=c_hi: e.activation(
                        out=ptb[:, pb_, :, c_lo:c_hi], in_=psS[sp_][:, :, c_lo:c_hi], func=AF.Exp, scale=0.125,
                        bias=hbt[:, bsel:bsel + 1]),
                       reads=skeys + ["hbt"], writes=[("ptb", pb_)])
                    return pb_

                def emit_pv(bt, pb_):
                    (bsel, midx, c_lo, c_hi, items) = bt
                    for (ktf, ktk, vf, vk, b, qs, scol, N) in items:
                        for h in range(2):
                            f1 = firstUZ[0]
                            op("pe", lambda e, h=h, vf=vf, qs=qs, scol=scol, N=N, f1=f1, pb_=pb_, U=U: e.matmul(
                                U[h * 64:(h + 1) * 64, qs], lhsT=vf(h), rhs=ptb[:, pb_, h, scol:scol + N],
                                start=f1, stop=False, skip_group_check=True, tile_position=(0, h * 64)),
                               reads=[vk, ("ptb", pb_)], writes=[("ps", Ub)])
                            op("pe", lambda e, h=h, qs=qs, scol=scol, N=N, f1=f1, pb_=pb_, Z=Z: e.matmul(
                                Z[h * 64:(h + 1) * 64, qs], lhsT=onesb[:, 0:64], rhs=ptb[:, pb_, h, scol:scol + N],
                                start=f1, stop=False, skip_group_check=True, tile_position=(0, h * 64)),
                               reads=["onesb", ("ptb", pb_)], writes=[("ps", Zb)])
                        firstUZ[0] = False

                pbs = {}
                if batches:
                    pbs[0] = emit_qk(batches[0])
                for n_ in range(len(batches)):
                    if n_ + 1 < len(batches):
                        pbs[n_ + 1] = emit_qk(batches[n_ + 1])
                    emit_pv(batches[n_], pbs[n_])
                if "n" not in KA:
                    continue
                op("dve", lambda e, Z=Z: e.reciprocal(out=rstd[:, :], in_=Z[:, :]), reads=[("ps", Zb)], writes=["rstd"])
                op("dve", lambda e, U=U, hp=hp: e.tensor_tensor(out=hT[:, hp, :], in0=U[:, :], in1=rstd[:, :], op=ALU.mult),
                   reads=[("ps", Ub), "rstd"], writes=["hT"])
                if stage == -1:
                    op("dve", lambda e, U=U: e.tensor_copy(out=kf[:, 0, :], in_=U[:, :]), reads=[("ps", Ub)], writes=[("kf", 0)])
                    op("dve", lambda e, Z=Z: e.tensor_copy(out=kf[:, 1, :], in_=Z[:, :]), reads=[("ps", Zb)], writes=[("kf", 1)])
                    op("sp", lambda e, hp=hp: e.dma_start(out=t_u[:, hp, :], in_=kf[:, 0, :]), reads=[("kf", 0)], dkey="tu")
                    op("sp", lambda e, hp=hp: e.dma_start(out=t_z[:, hp, :], in_=kf[:, 1, :]), reads=[("kf", 1)], dkey="tu")


        mks = sb("mks_t", [128, 80], BF16)
        mkn = sb("mkn_t", [32, 12, 64], BF16)
        op("pool", lambda e: e.dma_start(out=mks[:], in_=mks_in), writes=["mks"], dkey="mks")
        op("pool", lambda e: e.dma_start(out=mkn[:], in_=mkn_in), writes=["mkn"], dkey="mks")

        def sample_attention():
            ksrs = [V2p[:].rearrange("p r c -> p (r c)")[:, k_ * 1024:(k_ + 1) * 1024] for k_ in range(2)]
            vsrs = [V2c[:].rearrange("p r c -> p (r c)")[:, k_ * 1024:(k_ + 1) * 1024] for k_ in range(2)]
            KTss = [KT2[:].rearrange("p r m -> p (r m)")[:, k_ * 1024:(k_ + 1) * 1024].rearrange("p (q k) -> p q k", q=8)
                    for k_ in range(2)]
            bcnt = [0]
            KTn = V0[:].rearrange("p k c -> p (k c)")[:, 0:256].rearrange("p (q t) -> p q t", q=8)
            Vn = V1[:].rearrange("p a r c -> p (a r c)")
            Us, Zs = ps[4], ps[5]
            firstUZ = [True]
            SK = [("kts", "S", 0), ("vs", "S", 0)]

            def block(M, qk_items, mask_ap, ncols, vtile, vkey):
                sp_ = cnt["sp"] % 2
                cnt["sp"] += 1
                Sh = [ps[2 * sp_], ps[2 * sp_ + 1]]
                skeys = [("ps", 2 * sp_), ("ps", 2 * sp_ + 1)]
                fs = [True, True]
                for (lf, lk, rf, scol, N, oc) in qk_items:
                    for h in range(2):
                        op("pe", lambda e, h=h, lf=lf, rf=rf, scol=scol, N=N, st=fs[h], Sh=Sh: e.matmul(
                            Sh[h][:M, scol:scol + N], lhsT=lf(h), rhs=rf(h), start=st, stop=False, skip_group_check=True),
                           reads=[lk] + [("big", j_) for j_ in range(24)], writes=[skeys[h]])
                        fs[h] = False
                if mask_ap is not None:
                    for h in range(2):
                        op("pe", lambda e, h=h, Sh=Sh: e.matmul(
                            Sh[h][:M, 0:ncols], lhsT=identb[:M, :M], rhs=mask_ap, start=False, stop=True, skip_group_check=True),
                           reads=["identb", "mks", "mkn"], writes=[skeys[h]])
                pb_ = cnt["pt"] % 2
                cnt["pt"] += 1
                op("act", lambda e, sp_=sp_, pb_=pb_: e.activation(
                    out=ptb[:M, pb_, :, 0:ncols], in_=psS[sp_][:M, :, 0:ncols], func=AF.Exp, scale=0.125),
                   reads=skeys, writes=[("ptb", pb_)])
                for (lf, lk, rf, scol, N, oc) in qk_items:
                    hpq = oc[0]
                    for h in range(2):
                        f1 = firstUZ[0]
                        op("pe", lambda e, h=h, scol=scol, N=N, oc=oc, f1=f1, pb_=pb_, hpq=hpq: e.matmul(
                            Us[h * 64:(h + 1) * 64, oc[1]], lhsT=vtile[:M, hpq * 128 + h * 64:hpq * 128 + h * 64 + 64],
                            rhs=ptb[:M, pb_, h, scol:scol + N], start=f1, stop=False, skip_group_check=True,
                            tile_position=(0, h * 64)),
                           reads=[vkey, ("ptb", pb_)], writes=[("ps", 4)])
                        op("pe", lambda e, h=h, scol=scol, N=N, oc=oc, f1=f1, pb_=pb_: e.matmul(
                            Zs[h * 64:(h + 1) * 64, oc[1]], lhsT=onesb[:M, 0:64],
                            rhs=ptb[:M, pb_, h, scol:scol + N], start=f1, stop=False, skip_group_check=True,
                            tile_position=(0, h * 64)),
                           reads=["onesb", ("ptb", pb_)], writes=[("ps", 5)])
                    firstUZ[0] = False

            for b, (Wd, dil) in enumerate(((128, 1), (512, 4), (2048, 16))):
                op("sp", lambda e, b=b: e.dma_start(out=KTn, in_=ktn_s[b].rearrange("q p t -> p q t")),
                   reads=SK, writes=["V0"], dkey="V0")
                op("sp", lambda e, b=b: e.dma_start(out=Vn[:32, :], in_=vn_s[b]), reads=SK, writes=["V1"], dkey="V1")
                for n in range(4):
                    items = []
                    for hp in range(8):
                        items.append((lambda h, hp=hp: KTn[h * 64:(h + 1) * 64, hp, :], "V0",
                                      lambda h, hp=hp, b=b, n=n: big[h * 64:(h + 1) * 64, b * 8 + hp, n * 8:n * 8 + 8],
                                      hp * 8, 8, (hp, slice(hp * 32 + n * 8, hp * 32 + n * 8 + 8))))
                    block(32, items, mkn[:32, n * 3 + b, :], 64, Vn, "V1")
                    ncls = min(dil, 8)
                    nq = 8 // ncls if dil <= 8 else 1
                    for r in range(ncls):
                        kk_ = bcnt[0] % 2
                        bcnt[0] += 1
                        ksr, vsr, KTs = ksrs[kk_], vsrs[kk_], KTss[kk_]
                        kK, kV, kT = ("V2p", kk_), ("V2c", kk_), ("KT2", kk_)
                        op("pool", lambda e, b=b, n=n, r=r, dil=dil, Wd=Wd, ksr=ksr: e.dma_start(
                            out=ksr, in_=cch[b][n, r:Wd:dil, 0, :]), writes=["V2p", kK], dkey=kK)
                        op("pool", lambda e, b=b, n=n, r=r, dil=dil, Wd=Wd, vsr=vsr: e.dma_start(
                            out=vsr, in_=cch[b][n, r:Wd:dil, 1, :]), writes=["V2c", kV], dkey=kV)
                        for hp in range(8):
                            op("pe", lambda e, hp=hp, ksr=ksr: e.transpose(
                                out=psT[:, hp * 128:(hp + 1) * 128], in_=ksr[:, hp * 128:(hp + 1) * 128], identity=identb[:]),
                               reads=[kK, "identb"], writes=[("ps", 7)])
                        op("dve", lambda e, KTs=KTs: e.tensor_copy(out=KTs, in_=psT[:].rearrange("p (q k) -> p q k", q=8)),
                           reads=[("ps", 7)], writes=["KT2", kT])
                        items = []
                        for hp in range(8):
                            if dil == 1:
                                qsl = slice(n * 8, n * 8 + 8)
                            elif dil == 4:
                                qsl = slice(n * 8 + r, n * 8 + 8, 4)
                            else:
                                qsl = slice(n * 8 + r, n * 8 + r + 1)
                            osl = slice(hp * 32 + qsl.start, hp * 32 + qsl.stop, qsl.step)
                            items.append((lambda h, hp=hp, KTs=KTs: KTs[h * 64:(h + 1) * 64, hp, :], kT,
                                          lambda h, hp=hp, b=b, qsl=qsl: big[h * 64:(h + 1) * 64, b * 8 + hp, qsl],
                                          hp * nq, nq, (hp, osl)))
                        mask_ap = None
                        if dil == 1:
                            mask_ap = mks[:, 0:64]
                        elif dil == 4:
                            mask_ap = mks[:, 64:80]
                        block(128, items, mask_ap, 8 * nq, vsr, kV)
            op("dve", lambda e: e.reciprocal(out=rstd[:, 0:256], in_=Zs[:, 0:256]), reads=[("ps", 5)], writes=["rstd"])
            op("dve", lambda e: e.tensor_tensor(
                out=hT[:, :, 0:32], in0=Us[:, 0:256].rearrange("p (q t) -> p q t", q=8),
                in1=rstd[:, 0:256].rearrange("p (q t) -> p q t", q=8), op=ALU.mult),
               reads=[("ps", 4), "rstd"], writes=["hT"])

        def o_proj(jb, c0, ntok):
            for c in range(8):
                wb = cnt["wd"] % 2
                cnt["wd"] += 1
                op("sp", lambda e, c=c, wb=wb: e.dma_start(
                    out=wdt[:, wb, 0:8, :], in_=wo_b[jb][:, c * 128:(c + 1) * 128].rearrange("(k p) n -> p k n", p=128)),
                   reads=[("wo", jb)], writes=[("wdt", wb)], dkey=("wdt", wb))
                pd = cnt["psD"] % 2
                cnt["psD"] += 1
                pD = ps[pd]
                for k in range(8):
                    op("pe", lambda e, k=k, wb=wb, pD=pD: e.matmul(
                        pD[:, :ntok], lhsT=wdt[:, wb, k, :], rhs=hT[:, k, :ntok], start=(k == 0), stop=(k == 7)),
                       reads=[("wdt", wb), "hT"], writes=[("ps", pd)])
                op("dve", lambda e, c=c, pD=pD: e.tensor_tensor(
                    out=xT[:, c, c0:c0 + ntok], in0=pD[:, :ntok], in1=xT[:, c, c0:c0 + ntok], op=ALU.add),
                   reads=[("ps", pd), ("xT", c0)], writes=[("xT", c0)])

        if stage == -1:
            t_q = din("t_q", [128, 24, 512])
            t_kt0 = din("t_kt0", [8, 128, 4096])
            t_kt1 = din("t_kt1", [8, 128, 4, 1024])
            t_kt2 = din("t_kt2", [8, 128, 16, 256])
            t_v = din("t_v", [3, 4096, D])
            t_out = dout("t_out", [128, 8, 512])
            t_u = dout("t_u", [128, 8, 512])
            t_z = dout("t_z", [128, 8, 512])
            for hp_ in range(8):
                op("pool", lambda e, hp_=hp_: e.dma_start(out=kt0_s[hp_], in_=t_kt0[hp_]), writes=[("kts", "H", 0)], dkey=("kts", "H", 0))
                op("pool", lambda e, hp_=hp_: e.dma_start(out=kt1_s[hp_], in_=t_kt1[hp_]), writes=[("kts", "H", 1)], dkey=("kts", "H", 1))
                op("pool", lambda e, hp_=hp_: e.dma_start(out=kt2_s[hp_], in_=t_kt2[hp_]), writes=[("kts", "H", 2)], dkey=("kts", "H", 2))
            for b_ in range(3):
                op("pool", lambda e, b_=b_: e.dma_start(out=v_s[b_], in_=t_v[b_]), writes=[("vs", "H", b_)], dkey=("vs", "H", b_))
            op("pool", lambda e: e.dma_start(out=big[:], in_=t_q), writes=[("big", j_) for j_ in range(24)], dkey="tq")
            attention(int(os.environ.get("KT_I", "0")))
            for hp_ in range(8):
                op("act", lambda e, hp_=hp_: e.activation(out=yst[:, 0, 0:512], in_=hT[:, hp_, :], func=AF.Copy),
                   reads=["hT"], writes=[("yst", 0)])
                op("sp", lambda e, hp_=hp_: e.dma_start(out=t_out[:, hp_, :], in_=yst[:, 0, 0:512]), reads=[("yst", 0)], dkey="y")
        KB = os.environ.get("KB", "qaofs") if stage >= 3 else ""
        NL0 = 2 if stage >= 3 else 0
        NG = int(os.environ.get("KB_NG", "4"))
        NL = min(NL0, int(os.environ.get("KB_NL", "2")))
        for jb in range(NL):
            setup_gab(1 + jb, q_norm[jb])
            for i in range(NG):
                c0 = 512 * i
                if "q" in KB:
                    q_proj("O", jb, c0, 512, 2048 + 512 * i)
                if "a" in KB:
                    attention(i)
                if "o" in KB:
                    o_proj(jb, c0, 512)
                if "f" in KB:
                    ffn(2 + jb, c0, 512)
                if jb == 1:
                    store_group_T(xT, ("xT", c0), c0, 512, y_own, i * 512, "y")
            if "s" in os.environ.get("KB", "qaofs"):
                q_proj("S", jb, 2048, 32, 0)
                sample_attention()
                o_proj(jb, 2048, 32)
            ffn(2 + jb, 2048, 32)
        if stage >= 3:
            store_group_T(xT, ("xT", 2048), 2048, 32, y_s, 0, "y")

    P.resolve()
    global _LASTP
    _LASTP = P
    sems = {}
    for n_, k in enumerate(P.sem_keys):
        sems[k] = es.enter_context(nc.semaphore("s%d" % n_))
    print("n_sems", len(sems), "n_ops", {e: len(P.ops[e]) for e in ENGS})
    P.emit(sems)
    es.close()
    return nc


_CACHE = {}


def _rope_tab(pos):
    inv = np.power(np.float32(10000.0), -np.arange(0, 64, 2, dtype=np.float32) / np.float32(64)).astype(np.float32)
    ang = pos.astype(np.float32)[:, None] * inv[None, :]
    c = np.cos(ang).astype(np.float32)
    s = np.sin(ang).astype(np.float32)
    return np.concatenate([c, c, s, s], axis=1).astype(np.float32)


def kernel(**inp):
    import os
    stage = int(os.environ.get("KSTAGE", "3"))
    if stage not in _CACHE:
        _CACHE[stage] = build(stage)
    nc = _CACHE[stage]
    f = lambda a: np.ascontiguousarray(np.asarray(a, dtype=np.float32))
    x_prompt = f(inp["x_prompt"])
    x_sample = f(inp["x_sample"])
    state_pool = f(inp["state_pool"])
    caches = [f(inp["cache_kv_w128"]), f(inp["cache_kv_w512"]), f(inp["cache_kv_w2048"])]
    wnames = ["a_norm", "pool_w", "pool_scale", "kv_norm", "w_kv", "k_norm", "b_norm", "w_q", "q_norm", "w_o",
              "ffn_norm", "w_gate", "w_up", "w_down"]
    W = {n: f(inp[n]) for n in wnames}
    kk = np.arange(128)
    cur = np.where(kk[:, None] <= kk[None, :], 0.0, NEG).astype(np.float32)
    prev = np.where(kk[:, None] >= kk[None, :], 0.0, NEG).astype(np.float32)
    mkall = np.zeros((128, 11, 512), np.float32)
    mkall[:, 0] = np.concatenate([cur, prev, cur, prev], axis=1)
    mkall[:, 1] = np.tile(prev, (1, 4))
    mkall[:, 2] = np.tile(cur, (1, 4))
    for i_ in range(4):
        mkall[:, 3 + i_] = np.tile(prev[:, 32 * i_:32 * i_ + 32], (1, 16))
        mkall[:, 7 + i_] = np.tile(cur[:, 32 * i_:32 * i_ + 32], (1, 16))
    mks_h = np.concatenate([np.tile(prev[:, 0:8], (1, 8)), np.tile(prev[:, 0:2], (1, 8))], axis=1).astype(np.float32)
    mkn_h = np.full((32, 12, 64), NEG, np.float32)
    for n_ in range(4):
        for b_, dil_ in enumerate((1, 4, 16)):
            for t_ in range(8):
                for tp_ in range(8):
                    if tp_ <= t_ and (t_ - tp_) % dil_ == 0:
                        mkn_h[n_ * 8 + tp_, n_ * 3 + b_, t_::8] = 0.0
    gc_h = np.stack([W["a_norm"][0], W["a_norm"][1], W["ffn_norm"][0], W["ffn_norm"][1], W["ffn_norm"][2],
                     W["ffn_norm"][3], W["kv_norm"], W["b_norm"][0], W["b_norm"][1]], axis=0)
    gc_h = np.ascontiguousarray(gc_h.reshape(9, 8, 128).transpose(2, 0, 1))
    in_maps = []
    for c in range(8):
        b, half = c // 2, c % 2
        m = dict(W)
        if half == 1:
            m["xe"] = x_prompt[b]
            pos = np.arange(4096)
        else:
            m["xe"] = np.concatenate([np.zeros((2048, D), np.float32), x_prompt[b, :2048]], axis=0)
            pos = np.arange(4096) - 2048
        m["cs_e"] = _rope_tab(np.maximum(pos, 0))
        m["cs_s"] = np.tile(_rope_tab(8192 + np.arange(8)), (4, 1))
        m["xs"] = x_sample[4 * c:4 * c + 4].reshape(32, D)
        m["spool"] = state_pool[4 * c:4 * c + 4]
        m["c128"] = caches[0][4 * c:4 * c + 4].reshape(4, 128, 2, D)
        m["c512"] = caches[1][4 * c:4 * c + 4].reshape(4, 512, 2, D)
        m["c2048"] = caches[2][4 * c:4 * c + 4].reshape(4, 2048, 2, D)
        icv = np.zeros((2, 4, 16), np.float32)
        for g, w in enumerate(WIN):
            real = 1.0 / np.minimum(np.arange(16) + 1, w).astype(np.float32)
            plain = np.full(16, 1.0 / w, np.float32)
            icv[0, g] = real if half == 1 else plain
            icv[1, g] = real if half == 0 else plain
        m["ic"] = icv
        m["hb"] = np.full((128, 1), 0.0 if half == 1 else NEG, np.float32)
        m["mkall"] = mkall
        m["ident"] = np.eye(128, dtype=np.float32)
        m["gcols_in"] = gc_h
        m["mks"] = mks_h
        m["mkn"] = mkn_h
        in_maps.append(m)
    res = run_bass_kernel_spmd(nc, in_maps, core_ids=list(range(8)))
    R = res.results
    y_prompt = np.zeros((4, 4096, D), np.float32)
    for c in range(8):
        y_prompt[c // 2, (c % 2) * 2048:(c % 2) * 2048 + 2048] = R[c]["y_own"]
    y_sample = np.concatenate([R[c]["y_s"].reshape(4, 8, D) for c in range(8)], axis=0)
    pool_prompt = np.stack([R[2 * b + 1]["pool_p"] for b in range(4)], axis=0)
    pool_sample = np.concatenate([R[c]["pool_s"] for c in range(8)], axis=0)
    outs = [y_prompt, y_sample, pool_prompt, pool_sample]
    for g, w in enumerate((128, 512, 2048)):
        outs.append(np.stack([R[2 * b + 1]["kvp%d" % w].reshape(w, 2, 16, 64) for b in range(4)], axis=0))
        outs.append(np.concatenate([R[c]["kvs%d" % w].reshape(4, w, 2, 16, 64) for c in range(8)], axis=0))
    return tuple(outs)
```

```python
import numpy as np
from contextlib import ExitStack
import concourse.bass as bass
import concourse.mybir as mybir
from concourse.bass_utils import run_bass_kernel_spmd

F32 = mybir.dt.float32
BF16 = mybir.dt.bfloat16
AF = mybir.ActivationFunctionType
ALU = mybir.AluOpType
AX = mybir.AxisListType

ENGS = ("pe", "act", "dve", "pool", "sp")


class _Op:
    __slots__ = ("eng", "fn", "reads", "writes", "dkey", "waits", "tok", "need_inc", "idx")


class Prog:
    def __init__(self, nc):
        self.nc = nc
        self.ops = {e: [] for e in ENGS}
        self.all_ops = []
        self.last_w = {}
        self.readers = {}

    def op(self, eng, fn, reads=(), writes=(), dkey=None):
        o = _Op()
        o.eng = eng
        o.fn = fn
        o.dkey = dkey
        o.need_inc = dkey is not None
        o.tok = None
        deps = []
        for r in reads:
            w = self.last_w.get(r)
            if w is not None:
                deps.append((w, "raw"))
        for r in writes:
            w = self.last_w.get(r)
            if w is not None:
                deps.append((w, "waw"))
            rd = self.readers.get(r)
            if rd is not None:
                for x in rd[0].values():
                    deps.append((x, "war"))
                for x in rd[1]:
                    deps.append((x, "war"))
        o.waits = deps
        for r in reads:
            rd = self.readers.get(r)
            if rd is None:
                rd = self.readers[r] = ({}, [])
            if dkey is None:
                rd[0][eng] = o
            else:
                rd[1].append(o)
        for r in writes:
            self.last_w[r] = o
            self.readers[r] = ({}, [])
        self.ops[eng].append(o)
        self.all_ops.append(o)
        return o

    def resolve(self):
        for o in self.all_ops:
            real = []
            for (d, kind) in o.waits:
                if d is o:
                    continue
                if d.dkey is None and o.dkey is None and d.eng == o.eng:
                    if o.eng == "pe" or kind != "raw":
                        continue
                real.append(d)
            o.waits = real
            for d in real:
                d.need_inc = True
        cnt = {e: 0 for e in ENGS}
        gen = {e: 0 for e in ENGS}
        dcnt = {}
        for o in self.all_ops:
            if not o.need_inc:
                continue
            if o.dkey is not None:
                k = ("d", o.dkey)
                dcnt[k] = dcnt.get(k, 0) + 16
                o.tok = (k, dcnt[k])
            else:
                e = o.eng
                if cnt[e] >= 12000:
                    gen[e] += 1
                    cnt[e] = 0
                cnt[e] += 1
                o.tok = (("e", e, gen[e]), cnt[e])
        self.sem_keys = []
        self.final = {}
        for o in self.all_ops:
            if o.tok is not None:
                if o.tok[0] not in self.final:
                    self.sem_keys.append(o.tok[0])
                self.final[o.tok[0]] = max(self.final.get(o.tok[0], 0), o.tok[1])

    def emit(self, sems):
        nc = self.nc
        engmap = {"pe": "tensor", "act": "scalar", "dve": "vector", "pool": "gpsimd", "sp": "sync"}
        with nc.Block() as block:
            for e in ENGS:
                def body(eng, ops=self.ops[e], e=e):
                    seen = {}
                    for o in ops:
                        need = {}
                        for d in o.waits:
                            k, v = d.tok
                            if seen.get(k, 0) >= v:
                                continue
                            if need.get(k, 0) < v:
                                need[k] = v
                        for k, v in need.items():
                            eng.wait_ge(sems[k], v)
                            seen[k] = v
                        ins = o.fn(eng)
                        if o.tok is not None:
                            ins.then_inc(sems[o.tok[0]], 16 if o.dkey is not None else 1)
                    if e == "sp":
                        for k, v in self.final.items():
                            if k[0] == "d":
                                eng.wait_ge(sems[k], v)
                getattr(block, engmap[e])(body)


D = 1024
DFF = 2816
NJ = 22
EPS = 1e-6
WIN = (2, 4, 8, 16)
NEG = -30000.0


def build(stage=99):
    nc = bass.Bass("TRN2", target_bir_lowering=False)
    P = Prog(nc)
    es = ExitStack()

    BIGW = ("w_kv", "w_q", "w_o", "w_gate", "w_up", "w_down", "xe", "c128", "c512", "c2048")

    def din(name, shape, dt=F32):
        if stage == -1 and name in BIGW:
            shape = [1, 1]
        return nc.dram_tensor(name, list(shape), dt, kind="ExternalInput").ap()

    def dout(name, shape):
        return nc.dram_tensor(name, list(shape), F32, kind="ExternalOutput").ap()

    def dint(name, shape, dt):
        return nc.dram_tensor(name, list(shape), dt, kind="Internal").ap()

    def sb(name, shape, dt):
        return es.enter_context(nc.sbuf_tensor(name, list(shape), dt))

    esA = ExitStack()

    def sbA(name, shape, dt):
        return esA.enter_context(nc.sbuf_tensor(name, list(shape), dt))

    def psum(name, shape, dt):
        return es.enter_context(nc.psum_tensor(name, list(shape), dt))

    xe = din("xe", [4096, D])
    xs = din("xs", [32, D])
    spool = din("spool", [4, 2, 15, D])
    cch = [din("c128", [4, 128, 2, D]), din("c512", [4, 512, 2, D]), din("c2048", [4, 2048, 2, D])]
    a_norm = din("a_norm", [2, D])
    pool_w = din("pool_w", [2, 4, 256, 256])
    pool_scale = din("pool_scale", [2, D])
    kv_norm = din("kv_norm", [D])
    w_kv = din("w_kv", [D, 6144])
    k_norm = din("k_norm", [3, 64])
    b_norm = din("b_norm", [2, D])
    w_q = din("w_q", [2, D, 3072])
    q_norm = din("q_norm", [2, 3, 64])
    w_o = din("w_o", [2, D, D])
    ffn_norm = din("ffn_norm", [4, D])
    w_gate = din("w_gate", [4, D, DFF])
    w_up = din("w_up", [4, D, DFF])
    w_down = din("w_down", [4, DFF, D])
    cs_e = din("cs_e", [4096, 128])
    cs_s = din("cs_s", [32, 128])
    ic = din("ic", [2, 4, 16])
    hb = din("hb", [128, 1])
    ident_in = din("ident", [128, 128])
    gcols_in = din("gcols_in", [128, 9, 8])
    mkall = din("mkall", [128, 11, 512])

    y_own = dout("y_own", [2048, D])
    y_s = dout("y_s", [32, D])
    pool_p = dout("pool_p", [2, 15, D])
    pool_s = dout("pool_s", [4, 2, 15, D])
    kvp = [dout("kvp128", [128, 2, D]), dout("kvp512", [512, 2, D]), dout("kvp2048", [2048, 2, D])]
    kvs = [dout("kvs128", [4, 128, 2, D]), dout("kvs512", [4, 512, 2, D]), dout("kvs2048", [4, 2048, 2, D])]

    wg_b = dint("wg_b", [4, D, DFF], BF16)
    wu_b = dint("wu_b", [4, D, DFF], BF16)
    wd_b = dint("wd_b", [4, DFF, D], BF16)
    wkv_b = dint("wkv_b", [D, 6144], BF16)
    wq_b = dint("wq_b", [2, D, 3072], BF16)
    wo_b = dint("wo_b", [2, D, D], BF16)
    kt0_s = dint("kt0_s", [8, 128, 4096], BF16)
    kt1_s = dint("kt1_s", [8, 128, 4, 1024], BF16)
    kt2_s = dint("kt2_s", [8, 128, 16, 256], BF16)
    v_s = dint("v_s", [3, 4096, D], BF16)
    ktn_s = dint("ktn_s", [3, 8, 128, 32], BF16)
    vn_s = dint("vn_s", [3, 32, D], BF16)
    mks_in = din("mks", [128, 80])
    mkn_in = din("mkn", [32, 12, 64])

    xT = sb("xT", [128, 8, 2080], F32)
    identf = sb("identf", [128, 128], F32)
    identb = sb("identb", [128, 128], BF16)
    onesb = sb("onesb", [128, 128], BF16)
    epst = sb("epst", [128, 1], F32)
    gcols = sb("gcols", [128, 9, 8], F32)
    sqb = sb("sqb", [128, 2, 512], BF16)
    rstd = sb("rstd", [128, 512], F32)
    hT = sb("hT", [128, 8, 512], BF16)
    big = sb("big", [128, 24, 512], BF16)
    sg = sb("sg", [128, 2, 512], F32)
    wgt = sb("wgt", [128, 2, 8, 256], BF16)
    wut = sb("wut", [128, 2, 8, 256], BF16)
    wdt = sb("wdt", [128, 2, 22, 128], BF16)
    gAB = sb("gAB", [128, 3, 3, 128], F32)
    gtmp = sb("gtmp", [128, 3, 64], F32)
    cst = sb("cst", [128, 128], F32)
    cs4 = sb("cs4", [128, 4, 128], F32)
    tabt = sb("tabt", [128, 128], F32)
    ss8 = sb("ss8", [128, 2, 8], F32)
    w1 = sb("w1", [128, 2, 512], F32)
    kf = sb("kf", [128, 2, 512], F32)
    kb = sb("kb", [128, 2, 512], BF16)
    ktst = sb("ktst", [128, 4, 512], BF16)
    yst = sb("yst", [128, 2, D], F32)

    uT = sbA("uT", [128, 2, 8, 528], BF16)
    wsum = sbA("wsum", [128, 2, 528], F32)
    icb = sbA("icb", [128, 2, 4, 16], F32)
    wpf = sbA("wpf", [128, 512], F32)
    wp = sbA("wp", [128, 2, 4, 2, 256], BF16)
    usT = sbA("usT", [128, 8, 4, 24], BF16)
    wsS = sbA("wsS", [128, 2, 4, 24], F32)
    uf = sbA("uf", [128, 8, 32], F32)
    psS = [psum("psS%d" % i, [128, 2, 512], F32) for i in range(2)]
    ps = [psS[0][:, 0, :], psS[0][:, 1, :], psS[1][:, 0, :], psS[1][:, 1, :]] + \
         [psum("ps%d" % i, [128, 512], F32) for i in range(4, 8)]
    psT = ps[7][:].bitcast(BF16)

    def op(eng, fn, reads=(), writes=(), dkey=None):
        return P.op(eng, fn, reads, writes, dkey)

    op("sp", lambda e: e.dma_start(out=identf[:], in_=ident_in), writes=["identf"], dkey="identf")
    op("dve", lambda e: e.tensor_copy(out=identb[:], in_=identf[:]), reads=["identf"], writes=["identb"])
    op("dve", lambda e: e.memset(onesb[:], 1.0), writes=["onesb"])
    op("dve", lambda e: e.memset(epst[:], EPS), writes=["epst"])
    op("dve", lambda e: e.memset(uT[:], 0.0), writes=[("uT", 0), ("uT", 1)])
    op("sp", lambda e: e.dma_start(out=gcols[:], in_=gcols_in), writes=["gcols"], dkey="gcols")
    op("sp", lambda e: e.dma_start(out=icb[:].rearrange("p a g t -> p (a g t)"),
                                   in_=ic.rearrange("a g t -> (a g t)").partition_broadcast(128)),
       writes=["icb"], dkey="icb")
    op("sp", lambda e: e.dma_start(out=yst[:].rearrange("p a c -> p (a c)"),
                                   in_=pool_scale.rearrange("a c -> (a c)").partition_broadcast(128)),
       writes=[("yst", 0), ("yst", 1)], dkey="scb")
    for l in range(2):
        for g in range(4):
            op("sp", lambda e, l=l, g=g: e.dma_start(
                out=wpf[:, 0:512].rearrange("p (k n) -> p k n", k=2),
                in_=pool_w[l, g].rearrange("(k p) n -> p k n", p=128)),
               writes=["wpf"], dkey="wpf")
            for k in range(2):
                op("dve", lambda e, l=l, g=g, k=k: e.tensor_tensor(
                    out=wp[:, l, g, k, :], in0=wpf[:, k * 256:(k + 1) * 256],
                    in1=yst[:, l, g * 256:(g + 1) * 256], op=ALU.mult),
                   reads=["wpf", ("yst", 0), ("yst", 1)], writes=["wp"])

    def cast(dst, src, key, nsplit=4):
        n = src.shape[0]
        st = n // nsplit
        for s in range(nsplit):
            op("pool", lambda e, s=s: e.dma_start(out=dst[s * st:(s + 1) * st], in_=src[s * st:(s + 1) * st]),
               writes=[key], dkey=key)

    def cast_ffn(l):
        cast(wg_b[l], w_gate[l], ("wg", l))
        cast(wu_b[l], w_up[l], ("wu", l))
        cast(wd_b[l], w_down[l], ("wd", l))

    if stage >= 0:
        cast_ffn(0)
        cast_ffn(1)
        cast(wkv_b, w_kv, "wkv", 8)

    cnt = {"psK": 0, "sp": 0, "pt": 0, "kf": 0, "x": 0, "sq": 0, "psA": 0, "psB": 0, "psD": 0, "sg": 0, "w": 0, "wd": 0, "ys": 0, "wk": 0}

    def load_group_x(src, r0, ntok, c0):
        for t0 in range(0, ntok, 128):
            nr = min(128, ntok - t0)
            b = cnt["x"] % 2
            cnt["x"] += 1
            op("sp", lambda e, b=b, t0=t0, nr=nr: e.dma_start(out=yst[:nr, b, :], in_=src[r0 + t0:r0 + t0 + nr, :]),
               writes=[("yst", b)], dkey=("yst", b))
            for hf in range(2):
                pb = ps[5 + hf]
                for k4 in range(4):
                    kc = hf * 4 + k4
                    op("pe", lambda e, b=b, kc=kc, k4=k4, nr=nr, pb=pb: e.transpose(
                        out=pb[:, k4 * 128:k4 * 128 + nr], in_=yst[:nr, b, kc * 128:(kc + 1) * 128],
                        identity=identf[:nr, :nr]),
                       reads=[("yst", b), "identf"], writes=[("ps", 5 + hf)])
                op("act", lambda e, hf=hf, nr=nr, t0=t0, pb=pb: e.activation(
                    out=xT[:, hf * 4:hf * 4 + 4, c0 + t0:c0 + t0 + nr],
                    in_=pb[:].rearrange("p (k t) -> p k t", k=4)[:, :, :nr], func=AF.Copy),
                   reads=[("ps", 5 + hf)], writes=[("xT", c0)])

    def store_group_T(srcT, srckey, c0, ntok, dst, r0, dkey):
        for t0 in range(0, ntok, 128):
            nr = min(128, ntok - t0)
            b = cnt["ys"] % 2
            cnt["ys"] += 1
            for hf in range(2):
                pb = ps[5 + hf]
                for k4 in range(4):
                    kc = hf * 4 + k4
                    op("pe", lambda e, kc=kc, k4=k4, nr=nr, t0=t0, pb=pb: e.transpose(
                        out=pb[:nr, k4 * 128:(k4 + 1) * 128], in_=srcT[:, kc, c0 + t0:c0 + t0 + nr],
                        identity=identf[:]),
                       reads=[srckey, "identf"], writes=[("ps", 5 + hf)])
                op("act", lambda e, hf=hf, nr=nr, b=b, pb=pb: e.activation(
                    out=yst[:nr, b, hf * 512:(hf + 1) * 512], in_=pb[:nr, :], func=AF.Copy),
                   reads=[("ps", 5 + hf)], writes=[("yst", b)])
            op("sp", lambda e, b=b, nr=nr, t0=t0: e.dma_start(out=dst[r0 + t0:r0 + t0 + nr, :], in_=yst[:nr, b, :]),
               reads=[("yst", b)], dkey=dkey)

    def norm_feat(c0, ntok, gi, dst, dkeyw, dcol0, samp=False):
        pr = ps[4]
        for kc in range(8):
            b = cnt["sq"] % 2
            cnt["sq"] += 1
            op("act", lambda e, kc=kc, b=b: e.activation(out=sqb[:, b, :ntok], in_=xT[:, kc, c0:c0 + ntok], func=AF.Square),
               reads=[("xT", c0)], writes=[("sqb", b)])
            op("pe", lambda e, kc=kc, b=b: e.matmul(pr[:, :ntok], lhsT=onesb[:], rhs=sqb[:, b, :ntok],
                                                    start=(kc == 0), stop=(kc == 7)),
               reads=[("sqb", b), "onesb"], writes=[("ps", 4)])
        op("act", lambda e: e.activation(out=rstd[:, :ntok], in_=pr[:, :ntok], func=AF.Sqrt, scale=1.0 / D, bias=epst[:]),
           reads=[("ps", 4), "epst"], writes=["rstd"])
        op("dve", lambda e: e.reciprocal(out=rstd[:, :ntok], in_=rstd[:, :ntok]), reads=["rstd"], writes=["rstd"])
        for kc in range(8):
            if samp:
                op("dve", lambda e, kc=kc: e.scalar_tensor_tensor(
                    out=dst[:, kc, :, 16:24], in0=xT[:, kc, c0:c0 + 32].rearrange("p (n t) -> p n t", n=4),
                    scalar=gcols[:, gi, kc:kc + 1], in1=rstd[:, :32].rearrange("p (n t) -> p n t", n=4),
                    op0=ALU.mult, op1=ALU.mult),
                   reads=[("xT", c0), "gcols", "rstd"], writes=[dkeyw])
            else:
                op("dve", lambda e, kc=kc: e.scalar_tensor_tensor(
                    out=dst[:, kc, dcol0:dcol0 + ntok], in0=xT[:, kc, c0:c0 + ntok], scalar=gcols[:, gi, kc:kc + 1],
                    in1=rstd[:, :ntok], op0=ALU.mult, op1=ALU.mult),
                   reads=[("xT", c0), "gcols", "rstd"], writes=[dkeyw])

    def u_rows(gi, c0, rcol0):
        for kc in range(8):
            op("dve", lambda e, kc=kc: e.scalar_tensor_tensor(
                out=uf[:, kc, :], in0=xT[:, kc, c0:c0 + 32], scalar=gcols[:, gi, kc:kc + 1],
                in1=rstd[:, rcol0:rcol0 + 32], op0=ALU.mult, op1=ALU.mult),
               reads=[("xT", c0), "gcols", "rstd"], writes=["uf"])
        b = cnt["ys"] % 2
        cnt["ys"] += 1
        for hf in range(2):
            pb = ps[5 + hf]
            for k4 in range(4):
                kc = hf * 4 + k4
                op("pe", lambda e, kc=kc, k4=k4, pb=pb: e.transpose(
                    out=pb[:32, k4 * 128:(k4 + 1) * 128], in_=uf[:, kc, :], identity=identf[:]),
                   reads=["uf", "identf"], writes=[("ps", 5 + hf)])
            op("act", lambda e, hf=hf, b=b, pb=pb: e.activation(
                out=yst[:32, b, hf * 512:(hf + 1) * 512], in_=pb[:32, :], func=AF.Copy),
               reads=[("ps", 5 + hf)], writes=[("yst", b)])
        return b

    def ffn(l, c0, ntok):
        gi = 2 + l
        norm_feat(c0, ntok, gi, hT[:], "hT", 0)
        for jp in range(11):
            wb = cnt["w"] % 2
            cnt["w"] += 1
            op("sp", lambda e, jp=jp, wb=wb: e.dma_start(
                out=wgt[:, wb], in_=wg_b[l, :, jp * 256:(jp + 1) * 256].rearrange("(k p) n -> p k n", p=128)),
               reads=[("wg", l)], writes=[("wgt", wb)], dkey=("wgt", wb))
            op("sp", lambda e, jp=jp, wb=wb: e.dma_start(
                out=wut[:, wb], in_=wu_b[l, :, jp * 256:(jp + 1) * 256].rearrange("(k p) n -> p k n", p=128)),
               reads=[("wu", l)], writes=[("wut", wb)], dkey=("wut", wb))
            for j2 in range(2):
                j = jp * 2 + j2
                pa = cnt["psA"] % 2
                cnt["psA"] += 1
                pG, pU = ps[pa], ps[2 + pa]
                for kc in range(8):
                    op("pe", lambda e, kc=kc, wb=wb, j2=j2, pG=pG: e.matmul(
                        pG[:, :ntok], lhsT=wgt[:, wb, kc, j2 * 128:(j2 + 1) * 128], rhs=hT[:, kc, :ntok],
                        start=(kc == 0), stop=(kc == 7)),
                       reads=[("wgt", wb), "hT"], writes=[("ps", pa)])
                for kc in range(8):
                    op("pe", lambda e, kc=kc, wb=wb, j2=j2, pU=pU: e.matmul(
                        pU[:, :ntok], lhsT=wut[:, wb, kc, j2 * 128:(j2 + 1) * 128], rhs=hT[:, kc, :ntok],
                        start=(kc == 0), stop=(kc == 7)),
                       reads=[("wut", wb), "hT"], writes=[("ps", 2 + pa)])
                sgb = cnt["sg"] % 2
                cnt["sg"] += 1
                op("act", lambda e, sgb=sgb, pG=pG: e.activation(out=sg[:, sgb, :ntok], in_=pG[:, :ntok], func=AF.Silu),
                   reads=[("ps", pa)], writes=[("sg", sgb)])
                op("dve", lambda e, sgb=sgb, pU=pU, j=j: e.tensor_tensor(
                    out=big[:, j, :ntok], in0=pU[:, :ntok], in1=sg[:, sgb, :ntok], op=ALU.mult),
                   reads=[("ps", 2 + pa), ("sg", sgb)], writes=[("big", j)])
        for c in range(8):
            wb = cnt["wd"] % 2
            cnt["wd"] += 1
            op("sp", lambda e, c=c, wb=wb: e.dma_start(
                out=wdt[:, wb], in_=wd_b[l, :, c * 128:(c + 1) * 128].rearrange("(j p) n -> p j n", p=128)),
               reads=[("wd", l)], writes=[("wdt", wb)], dkey=("wdt", wb))
            pd = cnt["psD"] % 2
            cnt["psD"] += 1
            pD = ps[pd]
            for j in range(NJ):
                op("pe", lambda e, j=j, wb=wb, pD=pD: e.matmul(
                    pD[:, :ntok], lhsT=wdt[:, wb, j, :], rhs=big[:, j, :ntok],
                    start=(j == 0), stop=(j == NJ - 1)),
                   reads=[("wdt", wb), ("big", j)], writes=[("ps", pd)])
            op("dve", lambda e, c=c, pD=pD: e.tensor_tensor(
                out=xT[:, c, c0:c0 + ntok], in0=pD[:, :ntok], in1=xT[:, c, c0:c0 + ntok], op=ALU.add),
               reads=[("ps", pd), ("xT", c0)], writes=[("xT", c0)])

    def pool_layer(i, c0, ntok, start_tab, last=False):
        u = uT[:, i]
        norm_feat(c0, ntok, i, u, ("uT", i), 16)
        if last:
            yb = u_rows(i, c0 + ntok - 32, ntok - 32)
            op("sp", lambda e, yb=yb: e.dma_start(out=pool_p[i], in_=yst[17:32, yb, :]), reads=[("yst", yb)], dkey="po")
        for kc in range(8):
            g = kc // 2
            src = None
            nst = g + 1
            for s in range(nst):
                sh = 1 << s
                lo = 2 * sh
                wbuf = s % 2
                if s == 0:
                    op("dve", lambda e, kc=kc: e.tensor_tensor(
                        out=wsum[:, 0, 2:16 + ntok], in0=u[:, kc, 2:16 + ntok], in1=u[:, kc, 1:15 + ntok], op=ALU.add),
                       reads=[("uT", i)], writes=[("wsum", 0)])
                else:
                    op("dve", lambda e, sh=sh, lo=lo, wbuf=wbuf: e.tensor_tensor(
                        out=wsum[:, wbuf, lo:16 + ntok], in0=wsum[:, 1 - wbuf, lo:16 + ntok],
                        in1=wsum[:, 1 - wbuf, lo - sh:16 + ntok - sh], op=ALU.add),
                       reads=[("wsum", 1 - wbuf)], writes=[("wsum", wbuf)])
            wl = (nst - 1) % 2
            op("dve", lambda e, kc=kc, wl=wl, g=g: e.scalar_tensor_tensor(
                out=hT[:, kc, :ntok], in0=wsum[:, wl, 16:16 + ntok], scalar=1.0 / WIN[g], in1=u[:, kc, 16:16 + ntok],
                op0=ALU.mult, op1=ALU.subtract),
               reads=[("wsum", wl), ("uT", i)], writes=["hT"])
            if start_tab is not None:
                op("dve", lambda e, kc=kc, wl=wl, g=g: e.tensor_tensor(
                    out=wsum[:, wl, 0:16], in0=wsum[:, wl, 16:32], in1=icb[:, start_tab, g, :], op=ALU.mult),
                   reads=[("wsum", wl), "icb"], writes=[("wsum", wl)])
                op("dve", lambda e, kc=kc, wl=wl: e.tensor_tensor(
                    out=hT[:, kc, 0:16], in0=wsum[:, wl, 0:16], in1=u[:, kc, 16:32], op=ALU.subtract),
                   reads=[("wsum", wl), ("uT", i)], writes=["hT"])
        pool_mm(i, c0, ntok)
        op("act", lambda e: e.activation(out=u[:, :, 1:16], in_=u[:, :, 1 + ntok:16 + ntok], func=AF.Copy),
           reads=[("uT", i)], writes=[("uT", i)])

    def pool_mm(i, c0, ntok):
        for c in range(8):
            g = c // 2
            pd = cnt["psD"] % 2
            cnt["psD"] += 1
            pD = ps[pd]
            for k in range(2):
                op("pe", lambda e, k=k, g=g, c=c, pD=pD: e.matmul(
                    pD[:, :ntok], lhsT=wp[:, i, g, k, (c % 2) * 128:(c % 2) * 128 + 128], rhs=hT[:, 2 * g + k, :ntok],
                    start=(k == 0), stop=(k == 1)),
                   reads=["wp", "hT"], writes=[("ps", pd)])
            op("dve", lambda e, c=c, pD=pD: e.tensor_tensor(
                out=xT[:, c, c0:c0 + ntok], in0=pD[:, :ntok], in1=xT[:, c, c0:c0 + ntok], op=ALU.add),
               reads=[("ps", pd), ("xT", c0)], writes=[("xT", c0)])


    def pool_layer_sample(i):
        c0 = 2048
        b = cnt["x"] % 2
        cnt["x"] += 1
        for n in range(4):
            op("sp", lambda e, b=b, n=n: e.dma_start(out=yst[n * 15:(n + 1) * 15, b, :], in_=spool[n, i]),
               writes=[("yst", b)], dkey=("yst", b))
        for hf in range(2):
            pb = ps[5 + hf]
            for k4 in range(4):
                kc = hf * 4 + k4
                op("pe", lambda e, b=b, kc=kc, k4=k4, pb=pb: e.transpose(
                    out=pb[:, k4 * 128:k4 * 128 + 60], in_=yst[:60, b, kc * 128:(kc + 1) * 128],
                    identity=identf[:60, :60]),
                   reads=[("yst", b), "identf"], writes=[("ps", 5 + hf)])
            for k4 in range(4):
                kc = hf * 4 + k4
                op("act", lambda e, kc=kc, k4=k4, pb=pb: e.activation(
                    out=usT[:, kc, :, 1:16], in_=pb[:, k4 * 128:k4 * 128 + 60].rearrange("p (n r) -> p n r", n=4),
                    func=AF.Copy),
                   reads=[("ps", 5 + hf)], writes=["usT"])
        norm_feat(c0, 32, i, usT, "usT", 0, samp=True)
        yb = u_rows(i, c0, 0)
        for n in range(4):
            op("sp", lambda e, n=n, yb=yb: e.dma_start(out=pool_s[n, i, 7:15, :], in_=yst[n * 8:(n + 1) * 8, yb, :]),
               reads=[("yst", yb)], dkey="po")
            op("sp", lambda e, n=n: e.dma_start(out=pool_s[n, i, 0:7, :], in_=spool[n, i, 8:15, :]), dkey="po")
        for kc in range(8):
            g = kc // 2
            nst = g + 1
            for s_ in range(nst):
                sh = 1 << s_
                lo = 2 * sh
                wbuf = s_ % 2
                if s_ == 0:
                    op("dve", lambda e, kc=kc: e.tensor_tensor(
                        out=wsS[:, 0, :, 2:24], in0=usT[:, kc, :, 2:24], in1=usT[:, kc, :, 1:23], op=ALU.add),
                       reads=["usT"], writes=[("wsS", 0)])
                else:
                    op("dve", lambda e, sh=sh, lo=lo, wbuf=wbuf: e.tensor_tensor(
                        out=wsS[:, wbuf, :, lo:24], in0=wsS[:, 1 - wbuf, :, lo:24],
                        in1=wsS[:, 1 - wbuf, :, lo - sh:24 - sh], op=ALU.add),
                       reads=[("wsS", 1 - wbuf)], writes=[("wsS", wbuf)])
            wl = (nst - 1) % 2
            op("dve", lambda e, kc=kc, wl=wl, g=g: e.scalar_tensor_tensor(
                out=hT[:, kc, 0:32].rearrange("p (n t) -> p n t", n=4), in0=wsS[:, wl, :, 16:24],
                scalar=1.0 / WIN[g], in1=usT[:, kc, :, 16:24], op0=ALU.mult, op1=ALU.subtract),
               reads=[("wsS", wl), "usT"], writes=["hT"])
        pool_mm(i, c0, 32)

    def setup_gab(si, gsrc_ap):
        op("sp", lambda e: e.dma_start(out=gtmp[:].rearrange("p b d -> p (b d)"),
                                       in_=gsrc_ap.rearrange("b d -> (b d)").partition_broadcast(128)),
           writes=["gtmp"], dkey="gtmp")
        op("dve", lambda e: e.tensor_copy(out=gAB[:, si, :, 0:64], in_=gtmp[:]), reads=["gtmp"], writes=["gAB"])
        op("dve", lambda e: e.tensor_scalar(out=gAB[:, si, :, 64:96], in0=gtmp[:, :, 32:64], scalar1=-1.0, scalar2=None,
                                            op0=ALU.mult), reads=["gtmp"], writes=["gAB"])
        op("dve", lambda e: e.tensor_copy(out=gAB[:, si, :, 96:128], in_=gtmp[:, :, 0:32]), reads=["gtmp"], writes=["gAB"])

    def make_tabs(si, cs_src, r0, ntok):
        cnt["si"] = si
        for t0 in range(0, ntok, 128):
            nr = min(128, ntok - t0)
            tt = t0 // 128
            op("sp", lambda e, t0=t0, nr=nr, tt=tt: e.dma_start(out=cs4[:nr, tt, :], in_=cs_src[r0 + t0:r0 + t0 + nr, :]),
               writes=["cs4"], dkey="cs4")

    def normrope(pk, nr, tt, b, bf_out=False):
        x3 = pk[:nr, :].rearrange("p (h d) -> p h d", h=8)
        kb_ = cnt["kf"] % 2
        cnt["kf"] += 1
        si = cnt["si"]
        op("act", lambda e: e.activation(out=w1[:nr, kb_, :], in_=pk[:nr, :], func=AF.Square),
           reads=[pkkey(pk)], writes=[("w1", kb_)])
        op("dve", lambda e: e.tensor_reduce(out=ss8[:nr, kb_, :], in_=w1[:nr, kb_, :].rearrange("p (h d) -> p h d", h=8),
                                            axis=AX.X, op=ALU.add), reads=[("w1", kb_)], writes=[("ss8", kb_)])
        op("act", lambda e: e.activation(out=ss8[:nr, kb_, :], in_=ss8[:nr, kb_, :], func=AF.Sqrt, scale=1.0 / 64, bias=epst[:nr, :]),
           reads=[("ss8", kb_), "epst"], writes=[("ss8", kb_)])
        op("dve", lambda e: e.reciprocal(out=ss8[:nr, kb_, :], in_=ss8[:nr, kb_, :]), reads=[("ss8", kb_)], writes=[("ss8", kb_)])
        op("dve", lambda e: e.tensor_tensor(out=tabt[:nr, :], in0=cs4[:nr, tt, :], in1=gAB[:nr, si, b, :], op=ALU.mult),
           reads=["cs4", "gAB"], writes=["tabK"])
        tv = kf[:nr, kb_, :].rearrange("p (h d) -> p h d", h=8)
        wv = w1[:nr, kb_, :].rearrange("p (h d) -> p h d", h=8)
        op("dve", lambda e: e.tensor_tensor(out=tv, in0=x3, in1=tabt[:nr, 0:64].unsqueeze(1).to_broadcast([nr, 8, 64]),
                                            op=ALU.mult), reads=[pkkey(pk), "tabK"], writes=[("kf", kb_)])
        op("dve", lambda e: e.tensor_tensor(out=wv[:, :, 0:32], in0=x3[:, :, 32:64],
                                            in1=tabt[:nr, 64:96].unsqueeze(1).to_broadcast([nr, 8, 32]), op=ALU.mult),
           reads=[pkkey(pk), "tabK", ("ss8", kb_)], writes=[("w1", kb_)])
        op("dve", lambda e: e.tensor_tensor(out=wv[:, :, 32:64], in0=x3[:, :, 0:32],
                                            in1=tabt[:nr, 96:128].unsqueeze(1).to_broadcast([nr, 8, 32]), op=ALU.mult),
           reads=[pkkey(pk), "tabK"], writes=[("w1", kb_)])
        op("dve", lambda e: e.tensor_tensor(out=kf[:nr, kb_, :], in0=kf[:nr, kb_, :], in1=w1[:nr, kb_, :], op=ALU.add),
           reads=[("kf", kb_), ("w1", kb_)], writes=[("kf", kb_)])
        if bf_out:
            op("pool", lambda e: e.tensor_tensor(out=kb[:nr, kb_, :].rearrange("p (h d) -> p h d", h=8), in0=tv,
                                                 in1=ss8[:nr, kb_, :].unsqueeze(2).to_broadcast([nr, 8, 64]), op=ALU.mult),
               reads=[("kf", kb_), ("ss8", kb_)], writes=[("kb", kb_)])
        else:
            op("pool", lambda e: e.tensor_tensor(out=tv, in0=tv,
                                                 in1=ss8[:nr, kb_, :].unsqueeze(2).to_broadcast([nr, 8, 64]), op=ALU.mult),
               reads=[("kf", kb_), ("ss8", kb_)], writes=[("kf", kb_)])
        return kb_

    pskeys = {}

    def pkkey(pk):
        return pskeys[id(pk)]

    for i_, p_ in enumerate(ps):
        pskeys[id(p_)] = ("ps", i_)

    def load_wchunk(src2d, col0, key):
        wb = cnt["wk"] % 2
        cnt["wk"] += 1
        t = wgt if wb == 0 else wut
        nm = "wgt" if wb == 0 else "wut"
        view = t[:].rearrange("p b k n -> p (b k n)").rearrange("p (k n) -> p k n", k=8)
        op("sp", lambda e: e.dma_start(out=view, in_=src2d[:, col0:col0 + 512].rearrange("(k p) n -> p k n", p=128)),
           reads=[key], writes=[(nm, 0), (nm, 1)], dkey=(nm, 0))
        return view, [(nm, 0), (nm, 1)]

    def kv_proj(kind, gi_, c0, ntok, r0):
        norm_feat(c0, ntok, 6, hT[:], "hT", 0)
        samp = (kind == "S")
        make_tabs(0, cs_s if samp else cs_e, 0 if samp else r0, ntok)
        if kind == "H" and gi_ < 3:
            chunks = [4, 5, 10, 11]
        else:
            chunks = list(range(12))
        for ck in chunks:
            isk = ck < 6
            b = (ck % 6) // 2
            hh = ck % 2
            wv, wkeys = load_wchunk(wkv_b, ck * 512, "wkv")
            for t0 in range(0, ntok, 128):
                nr = min(128, ntok - t0)
                tt = t0 // 128
                pa = cnt["psK"] % 4
                cnt["psK"] += 1
                pk = ps[pa]
                for kc in range(8):
                    op("pe", lambda e, kc=kc, pk=pk, t0=t0, nr=nr, wv=wv: e.matmul(
                        pk[:nr, :], lhsT=hT[:, kc, t0:t0 + nr], rhs=wv[:, kc, :], start=(kc == 0), stop=(kc == 7)),
                       reads=["hT"] + wkeys, writes=[("ps", pa)])
                if isk:
                    fb = normrope(pk, nr, tt, b)
                else:
                    fb = cnt["kf"] % 2
                    cnt["kf"] += 1
                    op("act", lambda e, fb=fb, pk=pk, nr=nr: e.activation(out=kf[:nr, fb, :], in_=pk[:nr, :], func=AF.Copy),
                       reads=[("ps", pa)], writes=[("kf", fb)])
                sel = 0 if isk else 1
                if kind == "O":
                    pos = gi_ * 512 + t0
                    keep = (128, 512, 2048)[b]
                    if pos >= 2048 - keep:
                        rr = pos - (2048 - keep)
                        op("sp", lambda e, fb=fb, rr=rr, nr=nr, b=b, sel=sel, hh=hh: e.dma_start(
                            out=kvp[b][rr:rr + nr, sel, hh * 512:(hh + 1) * 512], in_=kf[:nr, fb, :]),
                           reads=[("kf", fb)], dkey="kvo")
                if samp:
                    Wd = (128, 512, 2048)[b]
                    for n in range(4):
                        op("sp", lambda e, fb=fb, n=n, b=b, sel=sel, hh=hh, Wd=Wd: e.dma_start(
                            out=kvs[b][n, Wd - 8:Wd, sel, hh * 512:(hh + 1) * 512], in_=kf[n * 8:(n + 1) * 8, fb, :]),
                           reads=[("kf", fb)], dkey="kvo")
                if stage >= 3 and samp:
                    op("act", lambda e, fb=fb, nr=nr: e.activation(out=kb[:nr, fb, :], in_=kf[:nr, fb, :], func=AF.Copy),
                       reads=[("kf", fb)], writes=[("kb", fb)])
                    if isk:
                        for q in range(4):
                            op("pe", lambda e, fb=fb, q=q, nr=nr: e.transpose(
                                out=psT[:, q * 128:q * 128 + nr], in_=kb[:nr, fb, q * 128:(q + 1) * 128],
                                identity=identb[:nr, :nr]),
                               reads=[("kb", fb), "identb"], writes=[("ps", 7)])
                        op("dve", lambda e: e.tensor_copy(
                            out=ktst[:, :, 0:32], in_=psT[:, 0:512].rearrange("p (q t) -> p q t", q=4)[:, :, 0:32]),
                           reads=[("ps", 7)], writes=["ktst"])
                        op("sp", lambda e, b=b, hh=hh: e.dma_start(
                            out=ktn_s[b, hh * 4:hh * 4 + 4].rearrange("q p t -> p q t"), in_=ktst[:, :, 0:32]),
                           reads=["ktst"], writes=[("kts", "S", 0)], dkey=("kts", "S", 0))
                    else:
                        op("sp", lambda e, fb=fb, b=b, hh=hh: e.dma_start(
                            out=vn_s[b, :, hh * 512:(hh + 1) * 512], in_=kb[:32, fb, :]),
                           reads=[("kb", fb)], writes=[("vs", "S", 0)], dkey=("vs", "S", 0))
                if stage >= 3 and not samp:
                    op("act", lambda e, fb=fb, nr=nr: e.activation(out=kb[:nr, fb, :], in_=kf[:nr, fb, :], func=AF.Copy),
                       reads=[("kf", fb)], writes=[("kb", fb)])
                    if isk:
                        for q in range(4):
                            op("pe", lambda e, fb=fb, q=q, nr=nr: e.transpose(
                                out=psT[:, q * 128:q * 128 + nr], in_=kb[:nr, fb, q * 128:(q + 1) * 128],
                                identity=identb[:nr, :nr]),
                               reads=[("kb", fb), "identb"], writes=[("ps", 7)])
                        dil = (1, 4, 16)[b]
                        mt = 128 // dil
                        op("dve", lambda e, dil=dil, mt=mt, tt=tt: e.tensor_copy(
                            out=ktst[:].rearrange("p q (r m) -> p q r m", r=dil)[:, :, :, tt * mt:(tt + 1) * mt],
                            in_=psT[:, 0:512].rearrange("p (q m r) -> p q r m", q=4, r=dil)),
                           reads=[("ps", 7)], writes=["ktst"])
                        if t0 + 128 >= ntok:
                            for q in range(4):
                                hpq = hh * 4 + q
                                if b == 0:
                                    dst = kt0_s[hpq, :, r0:r0 + 512]
                                    src_ = ktst[:, q, :]
                                elif b == 1:
                                    dst = kt1_s[hpq, :, :, r0 // 4:r0 // 4 + 128]
                                    src_ = ktst[:, q, :].rearrange("p (r m) -> p r m", r=4)
                                else:
                                    dst = kt2_s[hpq, :, :, r0 // 16:r0 // 16 + 32]
                                    src_ = ktst[:, q, :].rearrange("p (r m) -> p r m", r=16)
                                op("sp", lambda e, dst=dst, src_=src_: e.dma_start(out=dst, in_=src_),
                                   reads=["ktst"], writes=[("kts", kind, gi_)], dkey=("kts", kind, gi_))
                    else:
                        op("sp", lambda e, fb=fb, nr=nr, b=b, hh=hh, t0=t0: e.dma_start(
                            out=v_s[b, r0 + t0:r0 + t0 + nr, hh * 512:(hh + 1) * 512], in_=kb[:nr, fb, :]),
                           reads=[("kb", fb)], writes=[("vs", kind, gi_)], dkey=("vs", kind, gi_))

    if stage >= 0:
        setup_gab(0, k_norm)
        def ginfo(kind, gi_):
            if kind == "S":
                return 2048, 0, 32
            return gi_ * 512, gi_ * 512 + (2048 if kind == "O" else 0), 512

        pairs = [[("H", 0), ("H", 1), ("H", 2), ("H", 3)], [("O", 0), ("O", 1), ("O", 2), ("O", 3)], [("S", 0)]]
        for pair in pairs:
            for (kind, gi_) in pair:
                c0, r0, ntok = ginfo(kind, gi_)
                load_group_x(xs if kind == "S" else xe, r0, ntok, c0)
            for i in range(2):
                for (kind, gi_) in pair:
                    c0, r0, ntok = ginfo(kind, gi_)
                    st = None
                    if kind == "H" and gi_ == 0:
                        st = 0
                    if kind == "O" and gi_ == 0:
                        st = 1
                    if kind == "S":
                        pool_layer_sample(i)
                    else:
                        pool_layer(i, c0, 512, st, last=(kind == "O" and gi_ == 3))
                for (kind, gi_) in pair:
                    c0, r0, ntok = ginfo(kind, gi_)
                    ffn(i, c0, ntok)
            for (kind, gi_) in pair:
                c0, r0, ntok = ginfo(kind, gi_)
                if stage >= 2:
                    kv_proj(kind, gi_, c0, ntok, r0)
                if stage < 3:
                    if kind == "O":
                        store_group_T(xT, ("xT", c0), c0, 512, y_own, gi_ * 512, "y")
                    if kind == "S":
                        store_group_T(xT, ("xT", c0), c0, 32, y_s, 0, "y")
            if pair[0] == ("H", 0):
                for b, Wd in enumerate((128, 512, 2048)):
                    for n in range(4):
                        op("act", lambda e, b=b, n=n, Wd=Wd: e.dma_start(out=kvs[b][n, 0:Wd - 8], in_=cch[b][n, 8:Wd]), dkey="cc")
                if stage >= 3:
                    cast(wq_b[0], w_q[0], ("wq", 0))
                    cast(wo_b[0], w_o[0], ("wo", 0))
                    cast_ffn(2)
                    cast(wq_b[1], w_q[1], ("wq", 1))
                    cast(wo_b[1], w_o[1], ("wo", 1))
                    cast_ffn(3)

    import os
    if stage >= 3 or stage == -1:
        OLD = [("uT", 0), ("uT", 1), ("wsum", 0), ("wsum", 1), "icb", "wpf", "wp", "usT", ("wsS", 0), ("wsS", 1), "uf"]
        esA.close()
        KT0 = sb("KT0", [128, 640], BF16)
        KT1 = sb("KT1", [128, 4, 256], BF16)
        KT2 = sb("KT2", [128, 16, 256], BF16)
        V0 = sb("V0", [128, 5, 128], BF16)
        V1 = sb("V1", [128, 2, 4, 128], BF16)
        V2p = sb("V2p", [128, 16, 128], BF16)
        V2c = sb("V2c", [128, 16, 128], BF16)
        mkb = sb("mkb", [128, 11, 512], BF16)
        ptb = sb("ptb", [128, 2, 2, 512], BF16)
        hbt = sb("hbt", [128, 2], F32)
        pending = {"KT0", "KT1", "KT2", "V0", "V1", "V2p", "V2c", "mkb", ("ptb", 0), ("ptb", 1), "hbt", "mks", "mkn"}
        _op0 = op

        def op(eng, fn, reads=(), writes=(), dkey=None):
            writes = list(writes)
            hit = [w for w in writes if w in pending]
            if hit:
                for w in hit:
                    pending.discard(w)
                writes = writes + OLD
            return _op0(eng, fn, reads, writes, dkey)

        op("pool", lambda e: e.dma_start(out=mkb[:], in_=mkall), writes=["mkb"], dkey="mkb")
        op("sp", lambda e: e.dma_start(out=hbt[:, 0:1], in_=hb), writes=["hbt"], dkey="hbt")
        op("dve", lambda e: e.memset(hbt[:, 1:2], 0.0), writes=["hbt"])
        SCR = [("kts", k_, g_) for k_ in ("H", "O") for g_ in range(4)] + [("vs", k_, g_) for k_ in ("H", "O") for g_ in range(4)]

        def q_proj(kind, jb, c0, ntok, r0):
            norm_feat(c0, ntok, 7 + jb, hT[:], "hT", 0)
            samp = (kind == "S")
            make_tabs(1 + jb, cs_s if samp else cs_e, 0 if samp else r0, ntok)
            for ck in range(6):
                b = ck // 2
                hh = ck % 2
                wv, wkeys = load_wchunk(wq_b[jb], ck * 512, ("wq", jb))
                for t0 in range(0, ntok, 128):
                    nr = min(128, ntok - t0)
                    tt = t0 // 128
                    pa = cnt["psK"] % 4
                    cnt["psK"] += 1
                    pk = ps[pa]
                    for kc in range(8):
                        op("pe", lambda e, kc=kc, pk=pk, t0=t0, nr=nr, wv=wv: e.matmul(
                            pk[:nr, :], lhsT=hT[:, kc, t0:t0 + nr], rhs=wv[:, kc, :], start=(kc == 0), stop=(kc == 7)),
                           reads=["hT"] + wkeys, writes=[("ps", pa)])
                    fb = normrope(pk, nr, tt, b, bf_out=True)
                    for q in range(4):
                        op("pe", lambda e, fb=fb, q=q, nr=nr: e.transpose(
                            out=psT[:, q * 128:q * 128 + nr], in_=kb[:nr, fb, q * 128:(q + 1) * 128],
                            identity=identb[:nr, :nr]),
                           reads=[("kb", fb), "identb"], writes=[("ps", 7)])
                    i0 = b * 8 + hh * 4
                    op("dve", lambda e, i0=i0, nr=nr, t0=t0: e.tensor_copy(
                        out=big[:, i0:i0 + 4, t0:t0 + nr], in_=psT[:, 0:512].rearrange("p (q t) -> p q t", q=4)[:, :, :nr]),
                       reads=[("ps", 7)], writes=[("big", i0 + q_) for q_ in range(4)])

        def attention(i):
            E0 = 2048 + 512 * i
            KA = os.environ.get("KA", "012nqmep")
            for hp in range(int(os.environ.get("KB_NHP", "8"))):
                hc = slice(hp * 128, (hp + 1) * 128)
                ld = lambda out, in_, key: op("sp", lambda e: e.dma_start(out=out, in_=in_), reads=SCR, writes=[key], dkey=key)
                ld(KT0[:], kt0_s[hp, :, E0 - 128:E0 + 512], "KT0")
                ld(V0[:], v_s[0, E0 - 128:E0 + 512, hc].rearrange("(k p) c -> p k c", p=128), "V0")
                ld(KT1[:], kt1_s[hp, :, :, (E0 - 512) // 4:(E0 + 512) // 4], "KT1")
                for a_ in range(2):
                    ld(V1[:, a_], v_s[1, E0 - 512 + 512 * a_:E0 + 512 * a_, hc].rearrange("(m r) c -> m r c", r=4), "V1")
                ld2 = lambda out, in_, key: op("sp", lambda e: e.dma_start(out=out, in_=in_), reads=SCR,
                                               writes=[key, (key, 0), (key, 1)], dkey=key)
                ld2(KT2[:], kt2_s[hp], "KT2")
                ld2(V2p[:], v_s[2, 0:2048, hc].rearrange("(m r) c -> m r c", r=16), "V2p")
                ld2(V2c[:], v_s[2, 2048:4096, hc].rearrange("(m r) c -> m r c", r=16), "V2c")
                Ub, Zb = 4 + 2 * (hp % 2), 5 + 2 * (hp % 2)
                U, Z = ps[Ub], ps[Zb]
                firstUZ = [True]
                batches = []
                hbias0 = 0 if i == 0 else 1
                batches.append((hbias0, 0, 128, 256, [(lambda h: KT0[h * 64:(h + 1) * 64, 0:128], "KT0",
                                                        lambda h: V0[:, 0, h * 64:(h + 1) * 64], "V0", 0, slice(0, 128), 128, 128)]))
                for kbj in range(3):
                    batches.append((1, 0, 0, 256, [(lambda h, kbj=kbj: KT0[h * 64:(h + 1) * 64, 128 * (kbj + 1):128 * (kbj + 2)], "KT0",
                                                    lambda h, kbj=kbj: V0[:, kbj + 1, h * 64:(h + 1) * 64], "V0", 0,
                                                    slice(128 * kbj, 128 * kbj + 256), 0, 256)]))
                batches.append((1, 0, 0, 128, [(lambda h: KT0[h * 64:(h + 1) * 64, 512:640], "KT0",
                                                lambda h: V0[:, 4, h * 64:(h + 1) * 64], "V0", 0, slice(384, 512), 0, 128)]))
                for a in range(2):
                    items = []
                    for r in range(4):
                        items.append((lambda h, a=a, r=r: KT1[h * 64:(h + 1) * 64, r, a * 128:(a + 1) * 128], "KT1",
                                      lambda h, a=a, r=r: V1[:, a, r, h * 64:(h + 1) * 64], "V1", 1,
                                      slice(r, 512, 4), r * 128, 128))
                    batches.append(((hbias0 if a == 0 else 1), 1 + a, 0, 512, items))
                for a in range(2):
                    items = []
                    for r in range(16):
                        vt = V2p if a == 0 else V2c
                        items.append((lambda h, r=r, a=a: KT2[h * 64:(h + 1) * 64, r, a * 128:(a + 1) * 128], "KT2",
                                      lambda h, r=r, vt=vt: vt[:, r, h * 64:(h + 1) * 64], ("V2p" if a == 0 else "V2c"), 2,
                                      slice(r, 512, 16), r * 32, 32))
                    batches.append(((0 if a == 0 else 1), (3 + i if a == 0 else 7 + i), 0, 512, items))
                batches = [bt for bt in batches if str(bt[4][0][4]) in KA]
                def emit_qk(bt):
                    (bsel, midx, c_lo, c_hi, items) = bt
                    sp_ = cnt["sp"] % 2
                    cnt["sp"] += 1
                    Sh = [ps[2 * sp_], ps[2 * sp_ + 1]]
                    skeys = [("ps", 2 * sp_), ("ps", 2 * sp_ + 1)]
                    fs = [True, True]
                    for (ktf, ktk, vf, vk, b, qs, scol, N) in items:
                        for h in range(2):
                            op("pe", lambda e, h=h, ktf=ktf, b=b, qs=qs, scol=scol, N=N, st=fs[h], Sh=Sh, hp=hp: e.matmul(
                                Sh[h][:, scol:scol + N], lhsT=ktf(h), rhs=big[h * 64:(h + 1) * 64, b * 8 + hp, qs],
                                start=st, stop=False, skip_group_check=True),
                               reads=[ktk, ("big", b * 8 + hp)], writes=[skeys[h]])
                            fs[h] = False
                    for h in range(2):
                        op("pe", lambda e, h=h, Sh=Sh, midx=midx, c_lo=c_lo, c_hi=c_hi: e.matmul(
                            Sh[h][:, c_lo:c_hi], lhsT=identb[:, :], rhs=mkb[:, midx, c_lo:c_hi], start=False, stop=True,
                            skip_group_check=True),
                           reads=["identb", "mkb"], writes=[skeys[h]])
                    pb_ = cnt["pt"] % 2
                    cnt["pt"] += 1
                    op("act", lambda e, sp_=sp_, pb_=pb_, bsel=bsel, c_lo=c_lo, c_hi=c_hi: e.activation(
                        out=ptb[:, pb_, :, c_lo:c_hi], in_=psS[sp_][:, :, c_lo:c_hi], func=AF.Exp, scale=0.125,
                        bias=hbt[:, bsel:bsel + 1]),
                       reads=skeys + ["hbt"], writes=[("ptb", pb_)])
                    return pb_

                def emit_pv(bt, pb_):
                    (bsel, midx, c_lo, c_hi, items) = bt
                    for (ktf, ktk, vf, vk, b, qs, scol, N) in items:
                        for h in range(2):
                            f1 = firstUZ[0]
                            op("pe", lambda e, h=h, vf=vf, qs=qs, scol=scol, N=N, f1=f1, pb_=pb_, U=U: e.matmul(
                                U[h * 64:(h + 1) * 64, qs], lhsT=vf(h), rhs=ptb[:, pb_, h, scol:scol + N],
                                start=f1, stop=False, skip_group_check=True, tile_position=(0, h * 64)),
                               reads=[vk, ("ptb", pb_)], writes=[("ps", Ub)])
                            op("pe", lambda e, h=h, qs=qs, scol=scol, N=N, f1=f1, pb_=pb_, Z=Z: e.matmul(
                                Z[h * 64:(h + 1) * 64, qs], lhsT=onesb[:, 0:64], rhs=ptb[:, pb_, h, scol:scol + N],
                                start=f1, stop=False, skip_group_check=True, tile_position=(0, h * 64)),
                               reads=["onesb", ("ptb", pb_)], writes=[("ps", Zb)])
                        firstUZ[0] = False

                pbs = {}
                if batches:
                    pbs[0] = emit_qk(batches[0])
                for n_ in range(len(batches)):
                    if n_ + 1 < len(batches):
                        pbs[n_ + 1] = emit_qk(batches[n_ + 1])
                    emit_pv(batches[n_], pbs[n_])
                if "n" not in KA:
                    continue
                op("dve", lambda e, Z=Z: e.reciprocal(out=rstd[:, :], in_=Z[:, :]), reads=[("ps", Zb)], writes=["rstd"])
                op("dve", lambda e, U=U, hp=hp: e.tensor_tensor(out=hT[:, hp, :], in0=U[:, :], in1=rstd[:, :], op=ALU.mult),
                   reads=[("ps", Ub), "rstd"], writes=["hT"])
                if stage == -1:
                    op("dve", lambda e, U=U: e.tensor_copy(out=kf[:, 0, :], in_=U[:, :]), reads=[("ps", Ub)], writes=[("kf", 0)])
                    op("dve", lambda e, Z=Z: e.tensor_copy(out=kf[:, 1, :], in_=Z[:, :]), reads=[("ps", Zb)], writes=[("kf", 1)])
                    op("sp", lambda e, hp=hp: e.dma_start(out=t_u[:, hp, :], in_=kf[:, 0, :]), reads=[("kf", 0)], dkey="tu")
                    op("sp", lambda e, hp=hp: e.dma_start(out=t_z[:, hp, :], in_=kf[:, 1, :]), reads=[("kf", 1)], dkey="tu")


        mks = sb("mks_t", [128, 80], BF16)
        mkn = sb("mkn_t", [32, 12, 64], BF16)
        op("pool", lambda e: e.dma_start(out=mks[:], in_=mks_in), writes=["mks"], dkey="mks")
        op("pool", lambda e: e.dma_start(out=mkn[:], in_=mkn_in), writes=["mkn"], dkey="mks")

        def sample_attention():
            ksrs = [V2p[:].rearrange("p r c -> p (r c)")[:, k_ * 1024:(k_ + 1) * 1024] for k_ in range(2)]
            vsrs = [V2c[:].rearrange("p r c -> p (r c)")[:, k_ * 1024:(k_ + 1) * 1024] for k_ in range(2)]
            KTss = [KT2[:].rearrange("p r m -> p (r m)")[:, k_ * 1024:(k_ + 1) * 1024].rearrange("p (q k) -> p q k", q=8)
                    for k_ in range(2)]
            bcnt = [0]
            KTn = V0[:].rearrange("p k c -> p (k c)")[:, 0:256].rearrange("p (q t) -> p q t", q=8)
            Vn = V1[:].rearrange("p a r c -> p (a r c)")
            Us, Zs = ps[4], ps[5]
            firstUZ = [True]
            SK = [("kts", "S", 0), ("vs", "S", 0)]

            def block(M, qk_items, mask_ap, ncols, vtile, vkey):
                sp_ = cnt["sp"] % 2
                cnt["sp"] += 1
                Sh = [ps[2 * sp_], ps[2 * sp_ + 1]]
                skeys = [("ps", 2 * sp_), ("ps", 2 * sp_ + 1)]
                fs = [True, True]
                for (lf, lk, rf, scol, N, oc) in qk_items:
                    for h in range(2):
                        op("pe", lambda e, h=h, lf=lf, rf=rf, scol=scol, N=N, st=fs[h], Sh=Sh: e.matmul(
                            Sh[h][:M, scol:scol + N], lhsT=lf(h), rhs=rf(h), start=st, stop=False, skip_group_check=True),
                           reads=[lk] + [("big", j_) for j_ in range(24)], writes=[skeys[h]])
                        fs[h] = False
                if mask_ap is not None:
                    for h in range(2):
                        op("pe", lambda e, h=h, Sh=Sh: e.matmul(
                            Sh[h][:M, 0:ncols], lhsT=identb[:M, :M], rhs=mask_ap, start=False, stop=True, skip_group_check=True),
                           reads=["identb", "mks", "mkn"], writes=[skeys[h]])
                pb_ = cnt["pt"] % 2
                cnt["pt"] += 1
                op("act", lambda e, sp_=sp_, pb_=pb_: e.activation(
                    out=ptb[:M, pb_, :, 0:ncols], in_=psS[sp_][:M, :, 0:ncols], func=AF.Exp, scale=0.125),
                   reads=skeys, writes=[("ptb", pb_)])
                for (lf, lk, rf, scol, N, oc) in qk_items:
                    hpq = oc[0]
                    for h in range(2):
                        f1 = firstUZ[0]
                        op("pe", lambda e, h=h, scol=scol, N=N, oc=oc, f1=f1, pb_=pb_, hpq=hpq: e.matmul(
                            Us[h * 64:(h + 1) * 64, oc[1]], lhsT=vtile[:M, hpq * 128 + h * 64:hpq * 128 + h * 64 + 64],
                            rhs=ptb[:M, pb_, h, scol:scol + N], start=f1, stop=False, skip_group_check=True,
                            tile_position=(0, h * 64)),
                           reads=[vkey, ("ptb", pb_)], writes=[("ps", 4)])
                        op("pe", lambda e, h=h, scol=scol, N=N, oc=oc, f1=f1, pb_=pb_: e.matmul(
                            Zs[h * 64:(h + 1) * 64, oc[1]], lhsT=onesb[:M, 0:64],
                            rhs=ptb[:M, pb_, h, scol:scol + N], start=f1, stop=False, skip_group_check=True,
                            tile_position=(0, h * 64)),
                           reads=["onesb", ("ptb", pb_)], writes=[("ps", 5)])
                    firstUZ[0] = False

            for b, (Wd, dil) in enumerate(((128, 1), (512, 4), (2048, 16))):
                op("sp", lambda e, b=b: e.dma_start(out=KTn, in_=ktn_s[b].rearrange("q p t -> p q t")),
                   reads=SK, writes=["V0"], dkey="V0")
                op("sp", lambda e, b=b: e.dma_start(out=Vn[:32, :], in_=vn_s[b]), reads=SK, writes=["V1"], dkey="V1")
                for n in range(4):
                    items = []
                    for hp in range(8):
                        items.append((lambda h, hp=hp: KTn[h * 64:(h + 1) * 64, hp, :], "V0",
                                      lambda h, hp=hp, b=b, n=n: big[h * 64:(h + 1) * 64, b * 8 + hp, n * 8:n * 8 + 8],
                                      hp * 8, 8, (hp, slice(hp * 32 + n * 8, hp * 32 + n * 8 + 8))))
                    block(32, items, mkn[:32, n * 3 + b, :], 64, Vn, "V1")
                    ncls = min(dil, 8)
                    nq = 8 // ncls if dil <= 8 else 1
                    for r in range(ncls):
                        kk_ = bcnt[0] % 2
                        bcnt[0] += 1
                        ksr, vsr, KTs = ksrs[kk_], vsrs[kk_], KTss[kk_]
                        kK, kV, kT = ("V2p", kk_), ("V2c", kk_), ("KT2", kk_)
                        op("pool", lambda e, b=b, n=n, r=r, dil=dil, Wd=Wd, ksr=ksr: e.dma_start(
                            out=ksr, in_=cch[b][n, r:Wd:dil, 0, :]), writes=["V2p", kK], dkey=kK)
                        op("pool", lambda e, b=b, n=n, r=r, dil=dil, Wd=Wd, vsr=vsr: e.dma_start(
                            out=vsr, in_=cch[b][n, r:Wd:dil, 1, :]), writes=["V2c", kV], dkey=kV)
                        for hp in range(8):
                            op("pe", lambda e, hp=hp, ksr=ksr: e.transpose(
                                out=psT[:, hp * 128:(hp + 1) * 128], in_=ksr[:, hp * 128:(hp + 1) * 128], identity=identb[:]),
                               reads=[kK, "identb"], writes=[("ps", 7)])
                        op("dve", lambda e, KTs=KTs: e.tensor_copy(out=KTs, in_=psT[:].rearrange("p (q k) -> p q k", q=8)),
                           reads=[("ps", 7)], writes=["KT2", kT])
                        items = []
                        for hp in range(8):
                            if dil == 1:
                                qsl = slice(n * 8, n * 8 + 8)
                            elif dil == 4:
                                qsl = slice(n * 8 + r, n * 8 + 8, 4)
                            else:
                                qsl = slice(n * 8 + r, n * 8 + r + 1)
                            osl = slice(hp * 32 + qsl.start, hp * 32 + qsl.stop, qsl.step)
                            items.append((lambda h, hp=hp, KTs=KTs: KTs[h * 64:(h + 1) * 64, hp, :], kT,
                                          lambda h, hp=hp, b=b, qsl=qsl: big[h * 64:(h + 1) * 64, b * 8 + hp, qsl],
                                          hp * nq, nq, (hp, osl)))
                        mask_ap = None
                        if dil == 1:
                            mask_ap = mks[:, 0:64]
                        elif dil == 4:
                            mask_ap = mks[:, 64:80]
                        block(128, items, mask_ap, 8 * nq, vsr, kV)
            op("dve", lambda e: e.reciprocal(out=rstd[:, 0:256], in_=Zs[:, 0:256]), reads=[("ps", 5)], writes=["rstd"])
            op("dve", lambda e: e.tensor_tensor(
                out=hT[:, :, 0:32], in0=Us[:, 0:256].rearrange("p (q t) -> p q t", q=8),
                in1=rstd[:, 0:256].rearrange("p (q t) -> p q t", q=8), op=ALU.mult),
               reads=[("ps", 4), "rstd"], writes=["hT"])

        def o_proj(jb, c0, ntok):
            for c in range(8):
                wb = cnt["wd"] % 2
                cnt["wd"] += 1
                op("sp", lambda e, c=c, wb=wb: e.dma_start(
                    out=wdt[:, wb, 0:8, :], in_=wo_b[jb][:, c * 128:(c + 1) * 128].rearrange("(k p) n -> p k n", p=128)),
                   reads=[("wo", jb)], writes=[("wdt", wb)], dkey=("wdt", wb))
                pd = cnt["psD"] % 2
                cnt["psD"] += 1
                pD = ps[pd]
                for k in range(8):
                    op("pe", lambda e, k=k, wb=wb, pD=pD: e.matmul(
                        pD[:, :ntok], lhsT=wdt[:, wb, k, :], rhs=hT[:, k, :ntok], start=(k == 0), stop=(k == 7)),
                       reads=[("wdt", wb), "hT"], writes=[("ps", pd)])
                op("dve", lambda e, c=c, pD=pD: e.tensor_tensor(
                    out=xT[:, c, c0:c0 + ntok], in0=pD[:, :ntok], in1=xT[:, c, c0:c0 + ntok], op=ALU.add),
                   reads=[("ps", pd), ("xT", c0)], writes=[("xT", c0)])

        if stage == -1:
            t_q = din("t_q", [128, 24, 512])
            t_kt0 = din("t_kt0", [8, 128, 4096])
            t_kt1 = din("t_kt1", [8, 128, 4, 1024])
            t_kt2 = din("t_kt2", [8, 128, 16, 256])
            t_v = din("t_v", [3, 4096, D])
            t_out = dout("t_out", [128, 8, 512])
            t_u = dout("t_u", [128, 8, 512])
            t_z = dout("t_z", [128, 8, 512])
            for hp_ in range(8):
                op("pool", lambda e, hp_=hp_: e.dma_start(out=kt0_s[hp_], in_=t_kt0[hp_]), writes=[("kts", "H", 0)], dkey=("kts", "H", 0))
                op("pool", lambda e, hp_=hp_: e.dma_start(out=kt1_s[hp_], in_=t_kt1[hp_]), writes=[("kts", "H", 1)], dkey=("kts", "H", 1))
                op("pool", lambda e, hp_=hp_: e.dma_start(out=kt2_s[hp_], in_=t_kt2[hp_]), writes=[("kts", "H", 2)], dkey=("kts", "H", 2))
            for b_ in range(3):
                op("pool", lambda e, b_=b_: e.dma_start(out=v_s[b_], in_=t_v[b_]), writes=[("vs", "H", b_)], dkey=("vs", "H", b_))
            op("pool", lambda e: e.dma_start(out=big[:], in_=t_q), writes=[("big", j_) for j_ in range(24)], dkey="tq")
            attention(int(os.environ.get("KT_I", "0")))
            for hp_ in range(8):
                op("act", lambda e, hp_=hp_: e.activation(out=yst[:, 0, 0:512], in_=hT[:, hp_, :], func=AF.Copy),
                   reads=["hT"], writes=[("yst", 0)])
                op("sp", lambda e, hp_=hp_: e.dma_start(out=t_out[:, hp_, :], in_=yst[:, 0, 0:512]), reads=[("yst", 0)], dkey="y")
        KB = os.environ.get("KB", "qaofs") if stage >= 3 else ""
        NL0 = 2 if stage >= 3 else 0
        NG = int(os.environ.get("KB_NG", "4"))
        NL = min(NL0, int(os.environ.get("KB_NL", "2")))
        for jb in range(NL):
            setup_gab(1 + jb, q_norm[jb])
            for i in range(NG):
                c0 = 512 * i
                if "q" in KB:
                    q_proj("O", jb, c0, 512, 2048 + 512 * i)
                if "a" in KB:
                    attention(i)
                if "o" in KB:
                    o_proj(jb, c0, 512)
                if "f" in KB:
                    ffn(2 + jb, c0, 512)
                if jb == 1:
                    store_group_T(xT, ("xT", c0), c0, 512, y_own, i * 512, "y")
            if "s" in os.environ.get("KB", "qaofs"):
                q_proj("S", jb, 2048, 32, 0)
                sample_attention()
                o_proj(jb, 2048, 32)
            ffn(2 + jb, 2048, 32)
        if stage >= 3:
            store_group_T(xT, ("xT", 2048), 2048, 32, y_s, 0, "y")

    P.resolve()
    global _LASTP
    _LASTP = P
    sems = {}
    for n_, k in enumerate(P.sem_keys):
        sems[k] = es.enter_context(nc.semaphore("s%d" % n_))
    print("n_sems", len(sems), "n_ops", {e: len(P.ops[e]) for e in ENGS})
    P.emit(sems)
    es.close()
    return nc


_CACHE = {}


def _rope_tab(pos):
    inv = np.power(np.float32(10000.0), -np.arange(0, 64, 2, dtype=np.float32) / np.float32(64)).astype(np.float32)
    ang = pos.astype(np.float32)[:, None] * inv[None, :]
    c = np.cos(ang).astype(np.float32)
    s = np.sin(ang).astype(np.float32)
    return np.concatenate([c, c, s, s], axis=1).astype(np.float32)


def kernel(**inp):
    import os
    stage = int(os.environ.get("KSTAGE", "3"))
    if stage not in _CACHE:
        _CACHE[stage] = build(stage)
    nc = _CACHE[stage]
    f = lambda a: np.ascontiguousarray(np.asarray(a, dtype=np.float32))
    x_prompt = f(inp["x_prompt"])
    x_sample = f(inp["x_sample"])
    state_pool = f(inp["state_pool"])
    caches = [f(inp["cache_kv_w128"]), f(inp["cache_kv_w512"]), f(inp["cache_kv_w2048"])]
    wnames = ["a_norm", "pool_w", "pool_scale", "kv_norm", "w_kv", "k_norm", "b_norm", "w_q", "q_norm", "w_o",
              "ffn_norm", "w_gate", "w_up", "w_down"]
    W = {n: f(inp[n]) for n in wnames}
    kk = np.arange(128)
    cur = np.where(kk[:, None] <= kk[None, :], 0.0, NEG).astype(np.float32)
    prev = np.where(kk[:, None] >= kk[None, :], 0.0, NEG).astype(np.float32)
    mkall = np.zeros((128, 11, 512), np.float32)
    mkall[:, 0] = np.concatenate([cur, prev, cur, prev], axis=1)
    mkall[:, 1] = np.tile(prev, (1, 4))
    mkall[:, 2] = np.tile(cur, (1, 4))
    for i_ in range(4):
        mkall[:, 3 + i_] = np.tile(prev[:, 32 * i_:32 * i_ + 32], (1, 16))
        mkall[:, 7 + i_] = np.tile(cur[:, 32 * i_:32 * i_ + 32], (1, 16))
    mks_h = np.concatenate([np.tile(prev[:, 0:8], (1, 8)), np.tile(prev[:, 0:2], (1, 8))], axis=1).astype(np.float32)
    mkn_h = np.full((32, 12, 64), NEG, np.float32)
    for n_ in range(4):
        for b_, dil_ in enumerate((1, 4, 16)):
            for t_ in range(8):
                for tp_ in range(8):
                    if tp_ <= t_ and (t_ - tp_) % dil_ == 0:
                        mkn_h[n_ * 8 + tp_, n_ * 3 + b_, t_::8] = 0.0
    gc_h = np.stack([W["a_norm"][0], W["a_norm"][1], W["ffn_norm"][0], W["ffn_norm"][1], W["ffn_norm"][2],
                     W["ffn_norm"][3], W["kv_norm"], W["b_norm"][0], W["b_norm"][1]], axis=0)
    gc_h = np.ascontiguousarray(gc_h.reshape(9, 8, 128).transpose(2, 0, 1))
    in_maps = []
    for c in range(8):
        b, half = c // 2, c % 2
        m = dict(W)
        if half == 1:
            m["xe"] = x_prompt[b]
            pos = np.arange(4096)
        else:
            m["xe"] = np.concatenate([np.zeros((2048, D), np.float32), x_prompt[b, :2048]], axis=0)
            pos = np.arange(4096) - 2048
        m["cs_e"] = _rope_tab(np.maximum(pos, 0))
        m["cs_s"] = np.tile(_rope_tab(8192 + np.arange(8)), (4, 1))
        m["xs"] = x_sample[4 * c:4 * c + 4].reshape(32, D)
        m["spool"] = state_pool[4 * c:4 * c + 4]
        m["c128"] = caches[0][4 * c:4 * c + 4].reshape(4, 128, 2, D)
        m["c512"] = caches[1][4 * c:4 * c + 4].reshape(4, 512, 2, D)
        m["c2048"] = caches[2][4 * c:4 * c + 4].reshape(4, 2048, 2, D)
        icv = np.zeros((2, 4, 16), np.float32)
        for g, w in enumerate(WIN):
            real = 1.0 / np.minimum(np.arange(16) + 1, w).astype(np.float32)
            plain = np.full(16, 1.0 / w, np.float32)
            icv[0, g] = real if half == 1 else plain
            icv[1, g] = real if half == 0 else plain
        m["ic"] = icv
        m["hb"] = np.full((128, 1), 0.0 if half == 1 else NEG, np.float32)
        m["mkall"] = mkall
        m["ident"] = np.eye(128, dtype=np.float32)
        m["gcols_in"] = gc_h
        m["mks"] = mks_h
        m["mkn"] = mkn_h
        in_maps.append(m)
    res = run_bass_kernel_spmd(nc, in_maps, core_ids=list(range(8)))
    R = res.results
    y_prompt = np.zeros((4, 4096, D), np.float32)
    for c in range(8):
        y_prompt[c // 2, (c % 2) * 2048:(c % 2) * 2048 + 2048] = R[c]["y_own"]
    y_sample = np.concatenate([R[c]["y_s"].reshape(4, 8, D) for c in range(8)], axis=0)
    pool_prompt = np.stack([R[2 * b + 1]["pool_p"] for b in range(4)], axis=0)
    pool_sample = np.concatenate([R[c]["pool_s"] for c in range(8)], axis=0)
    outs = [y_prompt, y_sample, pool_prompt, pool_sample]
    for g, w in enumerate((128, 512, 2048)):
        outs.append(np.stack([R[2 * b + 1]["kvp%d" % w].reshape(w, 2, 16, 64) for b in range(4)], axis=0))
        outs.append(np.concatenate([R[c]["kvs%d" % w].reshape(4, w, 2, 16, 64) for c in range(8)], axis=0))
    return tuple(outs)
```
